# Optimizing a Trainium2 kernel written in Bass

```python
import math
import jax, jax.numpy as jnp
from jax import lax
import numpy as np

D_MODEL = 2048
BATCH = 1
SEQ = 8192
DEPTH = 4

N_A_LAYERS = DEPTH // 2
N_B_LAYERS = DEPTH - N_A_LAYERS
HEAD_DIM = 128
ROPE_THETA = 10000.0
NORM_EPS = 1e-6
BLOCK = 128
DILATED_GROUPS = ((128, 1), (512, 4), (2048, 16))
N_DIL_GROUPS = len(DILATED_GROUPS)
DIL_HEADS = 8
A_QKV_WIDTH = N_DIL_GROUPS * 3 * DIL_HEADS * HEAD_DIM
MEM_TOKENS = 256
MEM_HEADS = 4
MEM_WIDTH = MEM_HEADS * HEAD_DIM
A_IN_WIDTH = A_QKV_WIDTH + MEM_WIDTH
A_OUT_WIDTH = DIL_HEADS * HEAD_DIM + MEM_WIDTH
MLA_HEADS = 12
QK_NOPE_DIM = 128
QK_ROPE_DIM = 64
V_HEAD_DIM = 128
Q_LORA_RANK = 512
KV_LORA_RANK = 512
B_IN_WIDTH = Q_LORA_RANK + MEM_WIDTH
B_OUT_WIDTH = MLA_HEADS * V_HEAD_DIM + MEM_WIDTH
D_FF = 5632
CONV_WIDTH = 3

kernel_name = 'dilated_mla_yoco_hybrid'


def rmsnorm(x, g):
    xf = x.astype(jnp.float32)
    y = xf * lax.rsqrt(jnp.mean(xf * xf, axis=-1, keepdims=True) + NORM_EPS)
    return y.astype(x.dtype) * g


def rope(x, pos):
    half = x.shape[-1] // 2
    inv = ROPE_THETA ** (-jnp.arange(half, dtype=jnp.float32) / half)
    ang = pos.astype(jnp.float32)[..., None] * inv
    ang = ang.reshape(ang.shape[:2] + (1,) * (x.ndim - 3) + (half,))
    cos, sin = jnp.cos(ang), jnp.sin(ang)
    xf = x.astype(jnp.float32)
    x1, x2 = xf[..., :half], xf[..., half:]
    return jnp.concatenate([x1 * cos - x2 * sin, x2 * cos + x1 * sin], axis=-1).astype(x.dtype)


def dilated_group_attention(q, k, v, window, dilation):
    n_back = window // dilation
    B, S, H, Dh = q.shape
    span = dilation * BLOCK
    Sp = -(-S // span) * span
    L = Sp // dilation
    nb = L // BLOCK

    def split(t):
        t = jnp.pad(t, ((0, 0), (0, Sp - S), (0, 0), (0, 0))).reshape(B, L, dilation, H, Dh)
        return t.transpose(0, 2, 3, 1, 4).reshape(B, dilation, H, nb, BLOCK, Dh)

    def with_prev(t):
        prev = jnp.pad(t, ((0, 0), (0, 0), (0, 0), (1, 0), (0, 0), (0, 0)))[:, :, :, :-1]
        return jnp.concatenate([prev, t], axis=4)

    qb = split(q)
    kc = with_prev(split(k))
    vc = with_prev(split(v))
    s = jnp.einsum('bdhnqe,bdhnke->bdhnqk', qb, kc).astype(jnp.float32) * (Dh ** -0.5)
    qi = jnp.arange(BLOCK)[:, None] + BLOCK
    kj = jnp.arange(2 * BLOCK)[None, :]
    rel = qi - kj
    band = (rel >= 0) & (rel <= n_back)
    valid = (jnp.arange(nb)[:, None, None] * BLOCK + kj[None] - BLOCK) >= 0
    s = jnp.where(band[None] & valid, s, -jnp.inf)
    m = jnp.max(s, axis=-1)
    p = jnp.exp(s - m[..., None])
    l = jnp.sum(p, axis=-1)
    acc = jnp.einsum('bdhnqk,bdhnke->bdhnqe', p, vc.astype(jnp.float32))

    def merge(t):
        t = t.reshape((B, dilation, H, L) + t.shape[5:])
        t = jnp.moveaxis(t, 3, 1)
        return t.reshape((B, Sp, H) + t.shape[4:])[:, :S]

    return merge(m), merge(l), merge(acc)


def dilated_mixture(qkv, pos):
    q = rope(qkv[:, :, :, 0], pos)
    k = rope(qkv[:, :, :, 1], pos)
    v = qkv[:, :, :, 2]
    stats = [dilated_group_attention(q[:, :, g], k[:, :, g], v[:, :, g], w, d)
             for g, (w, d) in enumerate(DILATED_GROUPS)]
    ms = jnp.stack([st[0] for st in stats])
    ls = jnp.stack([st[1] for st in stats])
    accs = jnp.stack([st[2] for st in stats])
    wgt = jnp.exp(ms - jnp.max(ms, axis=0))
    num = jnp.sum(wgt[..., None] * accs, axis=0)
    den = jnp.sum(wgt * ls, axis=0)
    return num / den[..., None]


def memory_attention(q, mem_kv):
    mk, mv = mem_kv[:, :, 0], mem_kv[:, :, 1]
    s = jnp.einsum('bshe,bmhe->bhsm', q, mk).astype(jnp.float32) * (HEAD_DIM ** -0.5)
    p = jax.nn.softmax(s, axis=-1)
    return jnp.einsum('bhsm,bmhe->bshe', p, mv.astype(jnp.float32))


def mla_attention(q_nope, q_rope, k_nope, k_rope, v):
    B, S, H, _ = q_nope.shape
    nb = S // BLOCK
    scale = (QK_NOPE_DIM + QK_ROPE_DIM) ** -0.5
    kpos = jnp.arange(S)
    vf = v.astype(jnp.float32)

    def blocks(t):
        return jnp.moveaxis(t.reshape((B, nb, BLOCK) + t.shape[2:]), 1, 0)

    def one_block(args):
        qn, qr, b = args
        s = (jnp.einsum('bqhe,bkhe->bhqk', qn, k_nope).astype(jnp.float32)
             + jnp.einsum('bqhe,bke->bhqk', qr, k_rope).astype(jnp.float32)) * scale
        qpos = b * BLOCK + jnp.arange(BLOCK)
        s = jnp.where(kpos[None, :] <= qpos[:, None], s, -jnp.inf)
        p = jax.nn.softmax(s, axis=-1)
        return jnp.einsum('bhqk,bkhe->bqhe', p, vf)

    out = lax.map(one_block, (blocks(q_nope), blocks(q_rope), jnp.arange(nb)))
    return jnp.moveaxis(out, 0, 1).reshape(B, S, H, V_HEAD_DIM)


def conv_ffn(h, w_gate, w_val, conv_w, conv_b, w_down):
    S = h.shape[1]
    g = h @ w_gate
    gp = jnp.pad(g, ((0, 0), (CONV_WIDTH - 1, 0), (0, 0)))
    gc = conv_b + gp[:, 0:S] * conv_w[0]
    for j in range(1, CONV_WIDTH):
        gc = gc + gp[:, j:j + S] * conv_w[j]
    return (jax.nn.silu(gc) * (h @ w_val)) @ w_down


def setup_inputs(seed: int = 0) -> dict:
    key = jax.random.key(seed)
    ks = jax.random.split(key, 24)
    f32 = jnp.float32
    out_scale = (2 * DEPTH) ** -0.5

    def dense(k, shape, fan_in, scale=1.0):
        return jax.random.normal(k, shape, f32) * (scale * fan_in ** -0.5)

    def gain(k, shape):
        return 1.0 + 0.05 * jax.random.normal(k, shape, f32)

    return {
        'x': jax.random.normal(ks[0], (BATCH, SEQ, D_MODEL), f32),
        'mem': jax.random.normal(ks[1], (BATCH, MEM_TOKENS, D_MODEL), f32),
        'positions': jnp.broadcast_to(jnp.arange(SEQ, dtype=jnp.int32), (BATCH, SEQ)),
        'a_w_in': dense(ks[2], (N_A_LAYERS, D_MODEL, A_IN_WIDTH), D_MODEL),
        'a_w_mem_kv': dense(ks[3], (N_A_LAYERS, D_MODEL, 2 * MEM_WIDTH), D_MODEL),
        'a_w_out': dense(ks[4], (N_A_LAYERS, A_OUT_WIDTH, D_MODEL), A_OUT_WIDTH, out_scale),
        'b_w_in': dense(ks[5], (N_B_LAYERS, D_MODEL, B_IN_WIDTH), D_MODEL),
        'b_g_qnorm': gain(ks[6], (N_B_LAYERS, Q_LORA_RANK)),
        'b_w_uq': dense(ks[7], (N_B_LAYERS, Q_LORA_RANK, MLA_HEADS * (QK_NOPE_DIM + QK_ROPE_DIM)), Q_LORA_RANK),
        'b_w_mem_kv': dense(ks[8], (N_B_LAYERS, D_MODEL, 2 * MEM_WIDTH), D_MODEL),
        'b_w_out': dense(ks[9], (N_B_LAYERS, B_OUT_WIDTH, D_MODEL), B_OUT_WIDTH, out_scale),
        'kv_g_norm': gain(ks[10], (D_MODEL,)),
        'kv_w_dkv': dense(ks[11], (D_MODEL, KV_LORA_RANK + QK_ROPE_DIM), D_MODEL),
        'kv_g_latent': gain(ks[12], (KV_LORA_RANK,)),
        'kv_w_uk': dense(ks[13], (KV_LORA_RANK, MLA_HEADS * QK_NOPE_DIM), KV_LORA_RANK),
        'kv_w_uv': dense(ks[14], (KV_LORA_RANK, MLA_HEADS * V_HEAD_DIM), KV_LORA_RANK),
        'g_attn': gain(ks[15], (DEPTH, D_MODEL)),
        'g_ffn': gain(ks[16], (DEPTH, D_MODEL)),
        'ffn_w_gate': dense(ks[17], (DEPTH, D_MODEL, D_FF), D_MODEL),
        'ffn_w_val': dense(ks[18], (DEPTH, D_MODEL, D_FF), D_MODEL),
        'ffn_conv_w': dense(ks[19], (DEPTH, CONV_WIDTH, D_FF), CONV_WIDTH),
        'ffn_conv_b': 0.02 * jax.random.normal(ks[20], (DEPTH, D_FF), f32),
        'ffn_w_down': dense(ks[21], (DEPTH, D_FF, D_MODEL), D_FF, out_scale),
        'g_final': gain(ks[22], (D_MODEL,)),
    }


def reference(x, mem, positions, a_w_in, a_w_mem_kv, a_w_out, b_w_in, b_g_qnorm, b_w_uq,
              b_w_mem_kv, b_w_out, kv_g_norm, kv_w_dkv, kv_g_latent, kv_w_uk, kv_w_uv,
              g_attn, g_ffn, ffn_w_gate, ffn_w_val, ffn_conv_w, ffn_conv_b, ffn_w_down, g_final):
    B, S, _ = x.shape
    M = mem.shape[1]
    k_nope = k_rope = v_shared = None
    for layer in range(DEPTH):
        if layer == N_A_LAYERS:
            c = rmsnorm(x, kv_g_norm) @ kv_w_dkv
            c_kv = rmsnorm(c[..., :KV_LORA_RANK], kv_g_latent)
            k_rope = rope(c[..., KV_LORA_RANK:], positions)
            k_nope = (c_kv @ kv_w_uk).reshape(B, S, MLA_HEADS, QK_NOPE_DIM)
            v_shared = (c_kv @ kv_w_uv).reshape(B, S, MLA_HEADS, V_HEAD_DIM)
        h = rmsnorm(x, g_attn[layer])
        if layer < N_A_LAYERS:
            i = layer
            proj = h @ a_w_in[i]
            qkv = proj[..., :A_QKV_WIDTH].reshape(B, S, N_DIL_GROUPS, 3, DIL_HEADS, HEAD_DIM)
            q_mem = proj[..., A_QKV_WIDTH:].reshape(B, S, MEM_HEADS, HEAD_DIM)
            mix = dilated_mixture(qkv, positions)
            mem_kv = (mem @ a_w_mem_kv[i]).reshape(B, M, 2, MEM_HEADS, HEAD_DIM)
            w_out = a_w_out[i]
        else:
            i = layer - N_A_LAYERS
            proj = h @ b_w_in[i]
            cq = rmsnorm(proj[..., :Q_LORA_RANK], b_g_qnorm[i])
            q = (cq @ b_w_uq[i]).reshape(B, S, MLA_HEADS, QK_NOPE_DIM + QK_ROPE_DIM)
            q_rope = rope(q[..., QK_NOPE_DIM:], positions)
            q_mem = proj[..., Q_LORA_RANK:].reshape(B, S, MEM_HEADS, HEAD_DIM)
            mix = mla_attention(q[..., :QK_NOPE_DIM], q_rope, k_nope, k_rope, v_shared)
            mem_kv = (mem @ b_w_mem_kv[i]).reshape(B, M, 2, MEM_HEADS, HEAD_DIM)
            w_out = b_w_out[i]
        mem_out = memory_attention(q_mem, mem_kv)
        heads = jnp.concatenate([mix.reshape(B, S, -1), mem_out.reshape(B, S, -1)], axis=-1).astype(x.dtype)
        x = x + heads @ w_out
        x = x + conv_ffn(rmsnorm(x, g_ffn[layer]), ffn_w_gate[layer], ffn_w_val[layer],
                         ffn_conv_w[layer], ffn_conv_b[layer], ffn_w_down[layer])
    return rmsnorm(x, g_final)
```

```python
import numpy as np
from contextlib import ExitStack
import concourse.bass as bass
import concourse.mybir as mybir
from concourse.bass_utils import run_bass_kernel_spmd

F32 = mybir.dt.float32
BF16 = mybir.dt.bfloat16
I32 = mybir.dt.int32
AF = mybir.ActivationFunctionType
ALU = mybir.AluOpType

PE, ACT, DVE, POOL, SP = "pe", "act", "dve", "pool", "sp"
NCORES = 8
SAME_ENGINE_SYNC = True
N_DMA_SEMS = 8


class Op:
    __slots__ = ("eng", "fn", "deps", "signal", "ticket", "is_dma", "dsem", "dcount", "presem")

    def __init__(self, eng, fn, is_dma):
        self.eng = eng
        self.fn = fn
        self.deps = ()
        self.signal = False
        self.ticket = 0
        self.is_dma = is_dma
        self.dsem = None
        self.dcount = 0
        self.presem = None


class Prog:
    def __init__(self):
        self.nc = bass.Bass("TRN2", target_bir_lowering=False)
        self.ops = []
        self.last_w = {}
        self.rd_eng = {}
        self.rd_dma = {}
        self.stack = ExitStack()
        self.uid = 0

    def din(self, name, shape, dt=F32):
        return self.nc.dram_tensor(name, list(shape), dt, kind="ExternalInput").ap()

    def dout(self, name, shape, dt=F32):
        return self.nc.dram_tensor(name, list(shape), dt, kind="ExternalOutput").ap()

    def sb(self, name, shape, dt):
        return self.stack.enter_context(self.nc.sbuf_tensor(name, list(shape), dt))

    def ps(self, name, shape, dt=F32):
        return self.stack.enter_context(self.nc.psum_tensor(name, list(shape), dt))

    def op(self, eng, fn, reads=(), writes=(), is_dma=False):
        o = Op(eng, fn, is_dma)
        deps = set()
        for k in reads:
            w = self.last_w.get(k)
            if w is not None:
                deps.add(w)
        for k in writes:
            w = self.last_w.get(k)
            if w is not None:
                deps.add(w)
            for r in self.rd_eng.get(k, {}).values():
                deps.add(r)
            for r in self.rd_dma.get(k, ()):
                deps.add(r)
        for k in reads:
            if is_dma:
                self.rd_dma.setdefault(k, []).append(o)
            else:
                self.rd_eng.setdefault(k, {})[eng] = o
        for k in writes:
            self.last_w[k] = o
            self.rd_eng[k] = {}
            self.rd_dma[k] = []
        deps.discard(o)
        o.deps = tuple(deps)
        self.ops.append(o)
        return o

    def dma(self, queue, out, in_, reads=(), writes=()):
        return self.op(queue, lambda e: e.dma_start(out=out, in_=in_), reads, writes, is_dma=True)

    def build(self):
        nc = self.nc
        for o in self.ops:
            for d in o.deps:
                if d.is_dma:
                    continue
                if d.eng == o.eng and (d.eng == PE or not SAME_ENGINE_SYNC):
                    continue
                d.signal = True
        cnt = {}
        dma_i = {}
        for o in self.ops:
            if o.is_dma:
                i = dma_i.get(o.eng, 0)
                dma_i[o.eng] = i + 1
                o.dsem = (o.eng, i % N_DMA_SEMS)
                o.dcount = 16 * (i // N_DMA_SEMS + 1)
                if i >= N_DMA_SEMS:
                    o.presem = (o.dsem, o.dcount - 16)
            elif o.signal:
                cnt[o.eng] = cnt.get(o.eng, 0) + 1
                o.ticket = cnt[o.eng]
        sems = {}
        for eng in (PE, ACT, DVE, POOL):
            sems[eng] = self.stack.enter_context(nc.semaphore("s_" + eng))
        for q in (SP, POOL, ACT):
            if q in dma_i:
                for i in range(min(N_DMA_SEMS, dma_i[q])):
                    sems[(q, i)] = self.stack.enter_context(nc.semaphore("d_%s%d" % (q, i)))
        final_dma = {}
        for o in self.ops:
            if o.is_dma:
                final_dma[o.dsem] = o.dcount
        block = self.stack.enter_context(nc.Block())

        def make_body(eng_name):
            ops_e = [o for o in self.ops if o.eng == eng_name]

            def body(e):
                waited = {}
                for o in ops_e:
                    need = {}
                    for d in o.deps:
                        if d.is_dma:
                            k, v = d.dsem, d.dcount
                        else:
                            if d.eng == eng_name and (eng_name == PE or not SAME_ENGINE_SYNC):
                                continue
                            k, v = d.eng, d.ticket
                        if waited.get(k, 0) >= v:
                            continue
                        if need.get(k, 0) < v:
                            need[k] = v
                    if o.presem is not None:
                        k, v = o.presem
                        if waited.get(k, 0) < v and need.get(k, 0) < v:
                            need[k] = v
                    for k, v in need.items():
                        e.wait_ge(sems[k], v)
                        waited[k] = v
                    ins = o.fn(e)
                    if o.is_dma:
                        ins.then_inc(sems[o.dsem], 16)
                    elif o.signal:
                        ins.then_inc(sems[eng_name], 1)
                for k, v in final_dma.items():
                    if k[0] == eng_name and waited.get(k, 0) < v:
                        e.wait_ge(sems[k], v)
            return body

        for eng_name, deco in ((PE, block.tensor), (ACT, block.scalar), (DVE, block.vector),
                               (POOL, block.gpsimd), (SP, block.sync)):
            if any(o.eng == eng_name for o in self.ops):
                deco(make_body(eng_name))
        self.stack.close()
        return nc


def emit_consts(P):
    ones = P.sb("ones_bf", [128, 128], BF16)
    P.op(DVE, lambda e: e.memset(ones[:], 1.0), writes=["ones"])
    return ones


def emit_rmsnorm(P, gkey, xin, xkeys, KC, T, D, g_sb, out, okeys, ones, ps_ss, pskey, sq, rs1, rs2, eps=1e-6,
                 post=None):
    for kc in range(KC):
        P.op(ACT, lambda e, kc=kc: e.activation(out=sq[:, kc, 0:T], in_=xin(kc), func=AF.Square),
             reads=[xkeys[kc]], writes=[("nsq", kc)])

    def mm(e):
        ins = None
        for kc in range(KC):
            ins = e.matmul(ps_ss[:, 0:T], lhsT=ones[:], rhs=sq[:, kc, 0:T], start=(kc == 0), stop=(kc == KC - 1))
        return ins
    P.op(PE, mm, reads=[("nsq", kc) for kc in range(KC)] + ["ones"], writes=[pskey])
    P.op(ACT, lambda e: e.activation(out=rs1[:, 0:T], in_=ps_ss[:, 0:T], func=AF.Sqrt, scale=1.0 / D, bias=eps),
         reads=[pskey], writes=["rs1"])
    P.op(DVE, lambda e: e.reciprocal(out=rs2[:, 0:T], in_=rs1[:, 0:T]), reads=["rs1"], writes=["rs2"])
    for kc in range(KC):
        P.op(DVE, lambda e, kc=kc: e.scalar_tensor_tensor(out=out(kc), in0=xin(kc), scalar=g_sb[:, kc:kc + 1],
                                                          in1=rs2[:, 0:T], op0=ALU.mult, op1=ALU.mult),
             reads=[xkeys[kc], "rs2", gkey], writes=[okeys[kc]])
        if post is not None:
            post(kc)


TOK = 1024
DM = 2048
KC = 16
DFF = 5632
FG = 256
NG = DFF // FG
NFT = FG // 128


def build_ffn(final_norm=False):
    P = Prog()
    xT_d = P.din("xT", [DM, TOK])
    xh_d = P.din("xh", [DM, 2])
    wg_d = P.din("wg", [NG, 128, KC, FG])
    wv_d = P.din("wv", [NG, 128, KC, FG])
    wd_d = P.din("wd", [DFF, DM])
    cw_d = P.din("cw", [128, DFF // 128, 4])
    gf_d = P.din("gf", [128, KC])
    xo_d = P.dout("xo", [DM, TOK])
    if final_norm:
        gfin_d = P.din("gfin", [128, KC])
        fo_d = P.dout("fin", [DM, TOK])

    ones = emit_consts(P)
    xT = P.sb("xT_sb", [128, KC, TOK], F32)
    xh = P.sb("xh_sb", [128, KC, 2], F32)
    hT = P.sb("hT_sb", [128, KC, TOK], BF16)
    hh = P.sb("hh_sb", [128, KC, 2], BF16)
    gf = P.sb("gf_sb", [128, KC], F32)
    cw = P.sb("cw_sb", [128, DFF // 128, 4], F32)
    sq = P.sb("sq_sb", [128, KC, 512], BF16)
    rs1 = P.sb("rs1_sb", [128, 512], F32)
    rs2 = P.sb("rs2_sb", [128, 512], F32)
    wg = [P.sb("wg_sb%d" % i, [128, KC, FG], BF16) for i in range(2)]
    wv = [P.sb("wv_sb%d" % i, [128, KC, FG], BF16) for i in range(2)]
    wd = [P.sb("wd_sb%d" % i, [128, NFT, DM], BF16) for i in range(2)]
    gsb = [P.sb("g_sb%d" % i, [128, TOK + 2], F32) for i in range(2)]
    t1 = [P.sb("t1_sb%d" % i, [128, TOK], F32) for i in range(2)]
    t2 = [P.sb("t2_sb%d" % i, [128, TOK], F32) for i in range(2)]
    uT = [P.sb("uT_sb%d" % i, [128, NFT, TOK], BF16) for i in range(2)]
    pg = [P.ps("pg%d" % i, [128, 512]) for i in range(2)]
    pv = [P.ps("pv%d" % i, [128, 512]) for i in range(2)]
    pd = [P.ps("pd%d" % i, [128, 512]) for i in range(2)]
    ph = P.ps("ph", [128, 512])

    xT_v = xT_d.rearrange("(k p) t -> p k t", p=128)
    for kc in range(KC):
        P.dma(SP, xT[:, kc, :], xT_v[:, kc, :], writes=[("x", kc, 0), ("x", kc, 1)])
    P.dma(SP, xh[:, :, :], xh_d.rearrange("(k p) t -> p k t", p=128), writes=["xh"])
    P.dma(SP, gf[:, :], gf_d[:, :], writes=["gf"])
    P.dma(SP, cw[:, :, :], cw_d[:, :, :], writes=["cw"])

    def load_w(G):
        b = G % 2
        P.dma(POOL, wg[b][:, :, :], wg_d[G], writes=[("wg", b)])
        P.dma(POOL, wv[b][:, :, :], wv_d[G], writes=[("wv", b)])

    def load_wd(G):
        b = G % 2
        P.dma(POOL, wd[b][:, :, :], wd_d[G * FG:(G + 1) * FG, :].rearrange("(f p) n -> p f n", p=128),
              writes=[("wd", b)])

    load_w(0)
    load_wd(0)
    for tt in range(2):
        emit_rmsnorm(P, "gf", lambda kc, tt=tt: xT[:, kc, tt * 512:(tt + 1) * 512], [("x", kc, tt) for kc in range(KC)],
                     KC, 512, DM, gf, lambda kc, tt=tt: hT[:, kc, tt * 512:(tt + 1) * 512],
                     [("h", kc, tt) for kc in range(KC)], ones, ph, "ph", sq, rs1, rs2)
    emit_rmsnorm(P, "gf", lambda kc: xh[:, kc, :], ["xh"] * KC, KC, 2, DM, gf, lambda kc: hh[:, kc, :],
                 [("hh", kc) for kc in range(KC)], ones, ph, "ph", sq, rs1, rs2)

    hkeys = lambda tt: [("h", kc, tt) for kc in range(KC)]
    hhkeys = [("hh", kc) for kc in range(KC)]

    def down(G):
        b = G % 2
        for fo in range(KC):
            for tt in range(2):
                i = (fo * 2 + tt) % 2

                def mm(e, fo=fo, tt=tt, i=i, b=b):
                    ins = None
                    for ft in range(NFT):
                        ins = e.matmul(pd[i][:, :], lhsT=wd[b][:, ft, fo * 128:(fo + 1) * 128],
                                       rhs=uT[b][:, ft, tt * 512:(tt + 1) * 512], start=(ft == 0), stop=(ft == NFT - 1))
                    return ins
                P.op(PE, mm, reads=[("wd", b)] + [("u", b, ft, tt) for ft in range(NFT)], writes=[("pd", i)])
                P.op(DVE, lambda e, fo=fo, tt=tt, i=i: e.tensor_tensor(
                    out=xT[:, fo, tt * 512:(tt + 1) * 512], in0=pd[i][:, :], in1=xT[:, fo, tt * 512:(tt + 1) * 512],
                    op=ALU.add), reads=[("pd", i), ("x", fo, tt)], writes=[("x", fo, tt)])

    for G in range(NG):
        b = G % 2
        if G + 1 < NG:
            load_w(G + 1)
        for ft in range(NFT):
            fi = G * NFT + ft
            gb = fi % 2
            for tt in range(2):
                def mmg(e, tt=tt, ft=ft, b=b):
                    ins = None
                    for kc in range(KC):
                        ins = e.matmul(pg[tt][:, :], lhsT=wg[b][:, kc, ft * 128:(ft + 1) * 128],
                                       rhs=hT[:, kc, tt * 512:(tt + 1) * 512], start=(kc == 0), stop=(kc == KC - 1))
                    return ins
                P.op(PE, mmg, reads=[("wg", b)] + hkeys(tt), writes=[("pg", tt)])
                P.op(ACT, lambda e, tt=tt, gb=gb: e.activation(out=gsb[gb][:, 2 + tt * 512:2 + (tt + 1) * 512],
                                                               in_=pg[tt][:, :], func=AF.Copy),
                     reads=[("pg", tt)], writes=[("gsb", gb, tt)])

            def mmh(e, ft=ft, b=b):
                ins = None
                for kc in range(KC):
                    ins = e.matmul(ph[:, 0:2], lhsT=wg[b][:, kc, ft * 128:(ft + 1) * 128], rhs=hh[:, kc, :],
                                   start=(kc == 0), stop=(kc == KC - 1))
                return ins
            P.op(PE, mmh, reads=[("wg", b)] + hhkeys, writes=["ph"])
            P.op(ACT, lambda e, gb=gb: e.activation(out=gsb[gb][:, 0:2], in_=ph[:, 0:2], func=AF.Copy),
                 reads=["ph"], writes=[("gsb", gb, "h")])
            for tt in range(2):
                def mmv(e, tt=tt, ft=ft, b=b):
                    ins = None
                    for kc in range(KC):
                        ins = e.matmul(pv[tt][:, :], lhsT=wv[b][:, kc, ft * 128:(ft + 1) * 128],
                                       rhs=hT[:, kc, tt * 512:(tt + 1) * 512], start=(kc == 0), stop=(kc == KC - 1))
                    return ins
                P.op(PE, mmv, reads=[("wv", b)] + hkeys(tt), writes=[("pv", tt)])
            if ft == 0:
                if G > 0:
                    down(G - 1)
                if G + 1 < NG:
                    load_wd(G + 1)
            gk = [("gsb", gb, 0), ("gsb", gb, 1), ("gsb", gb, "h")]
            P.op(DVE, lambda e, gb=gb, fi=fi: e.tensor_scalar(out=t1[gb][:, :], in0=gsb[gb][:, 2:TOK + 2],
                                                              scalar1=cw[:, fi, 2:3], scalar2=cw[:, fi, 3:4],
                                                              op0=ALU.mult, op1=ALU.add),
                 reads=gk + ["cw"], writes=[("t1", gb)])
            P.op(DVE, lambda e, gb=gb, fi=fi: e.scalar_tensor_tensor(out=t2[gb][:, :], in0=gsb[gb][:, 1:TOK + 1],
                                                                     scalar=cw[:, fi, 1:2], in1=t1[gb][:, :],
                                                                     op0=ALU.mult, op1=ALU.add),
                 reads=gk + ["cw", ("t1", gb)], writes=[("t2", gb)])
            P.op(DVE, lambda e, gb=gb, fi=fi: e.scalar_tensor_tensor(out=t1[gb][:, :], in0=gsb[gb][:, 0:TOK],
                                                                     scalar=cw[:, fi, 0:1], in1=t2[gb][:, :],
                                                                     op0=ALU.mult, op1=ALU.add),
                 reads=gk + ["cw", ("t2", gb)], writes=[("t1", gb)])
            P.op(ACT, lambda e, gb=gb: e.activation(out=t2[gb][:, :], in_=t1[gb][:, :], func=AF.Silu),
                 reads=[("t1", gb)], writes=[("t2", gb)])
            for tt in range(2):
                P.op(DVE, lambda e, gb=gb, tt=tt, ft=ft, b=b: e.tensor_tensor(
                    out=uT[b][:, ft, tt * 512:(tt + 1) * 512], in0=pv[tt][:, :], in1=t2[gb][:, tt * 512:(tt + 1) * 512],
                    op=ALU.mult), reads=[("pv", tt), ("t2", gb)], writes=[("u", b, ft, tt)])
    down(NG - 1)

    xo_v = xo_d.rearrange("(k p) t -> p k t", p=128)
    for kc in range(KC):
        P.dma(SP, xo_v[:, kc, :], xT[:, kc, :], reads=[("x", kc, 0), ("x", kc, 1)])
    if final_norm:
        gfin = P.sb("gfin_sb", [128, KC], F32)
        P.dma(SP, gfin[:, :], gfin_d[:, :], writes=["gfin"])
        stg = [(t1[0], ("t1", 0)), (t1[1], ("t1", 1)), (t2[0], ("t2", 0)), (t2[1], ("t2", 1))]
        fo_v = fo_d.rearrange("(k p) t -> p k t", p=128)
        for tt in range(2):
            emit_rmsnorm(P, "gfin", lambda kc, tt=tt: xT[:, kc, tt * 512:(tt + 1) * 512],
                         [("x", kc, tt) for kc in range(KC)], KC, 512, DM, gfin, lambda kc: stg[kc % 4][0][:, 0:512],
                         [stg[kc % 4][1] for kc in range(KC)], ones, ph, "ph", sq, rs1, rs2, post=lambda kc, tt=tt: P.dma(
                             SP, fo_v[:, kc, tt * 512:(tt + 1) * 512], stg[kc % 4][0][:, 0:512], reads=[stg[kc % 4][1]]))
    return P.build()


def ffn_host_inputs(x_tok, w_gate, w_val, conv_w, conv_b, w_down, g_ffn, g_final=None):
    S = x_tok.shape[0]
    wg = np.ascontiguousarray(w_gate.reshape(KC, 128, NG, FG).transpose(2, 1, 0, 3))
    wv = np.ascontiguousarray(w_val.reshape(KC, 128, NG, FG).transpose(2, 1, 0, 3))
    cw = np.ascontiguousarray(np.concatenate([conv_w, conv_b[None, :]], axis=0).reshape(4, DFF // 128, 128).transpose(2, 1, 0))
    gf = np.ascontiguousarray(g_ffn.reshape(KC, 128).T)
    maps = []
    for c in range(NCORES):
        xs = x_tok[c * TOK:(c + 1) * TOK]
        halo = np.zeros((2, DM), np.float32)
        if c > 0:
            halo[:] = x_tok[c * TOK - 2:c * TOK]
        m = {"xT": np.ascontiguousarray(xs.T), "xh": np.ascontiguousarray(halo.T), "wg": wg, "wv": wv,
             "wd": w_down, "cw": cw, "gf": gf}
        if g_final is not None:
            m["gfin"] = np.ascontiguousarray(g_final.reshape(KC, 128).T)
        maps.append(m)
    return maps


TWO_PI = float(2.0 * np.pi)


def emit_rope_tables(P, NP, T, pos_ap, poskey, inv, nsgn, negpi, posf, ang, r1, r2, cos2, sinS, kpre, ki):
    P.op(DVE, lambda e: e.tensor_copy(out=posf[0:NP, 0:T], in_=pos_ap), reads=[poskey], writes=[(kpre, "posf")])
    P.op(DVE, lambda e: e.tensor_scalar(out=ang[0:NP, 0:T], in0=posf[0:NP, 0:T], scalar1=inv[0:NP, 0:1], scalar2=None,
                                        op0=ALU.mult), reads=[(kpre, "posf"), "ropec"], writes=[(kpre, "ang")])
    for which, (rr, dst) in enumerate(((r1, sinS), (r2, cos2))):
        rk = (kpre, "r", which)
        if which == 1:
            P.op(DVE, lambda e: e.tensor_scalar(out=ang[0:NP, 0:T], in0=ang[0:NP, 0:T], scalar1=0.25, scalar2=None,
                                                op0=ALU.add), reads=[(kpre, "ang")], writes=[(kpre, "ang")])
        P.op(DVE, lambda e: e.tensor_copy(out=ki[0:NP, 0:T], in_=ang[0:NP, 0:T]), reads=[(kpre, "ang")], writes=[(kpre, "ki")])
        P.op(DVE, lambda e, rr=rr: e.tensor_copy(out=rr[0:NP, 0:T], in_=ki[0:NP, 0:T]), reads=[(kpre, "ki")], writes=[rk])
        P.op(DVE, lambda e, rr=rr: e.tensor_tensor(out=rr[0:NP, 0:T], in0=ang[0:NP, 0:T], in1=rr[0:NP, 0:T],
                                                   op=ALU.subtract), reads=[(kpre, "ang"), rk], writes=[rk])
        P.op(DVE, lambda e, rr=rr: e.scalar_tensor_tensor(out=rr[0:NP, 0:T], in0=rr[0:NP, 0:T], scalar=0.5,
                                                          in1=rr[0:NP, 0:T], op0=ALU.is_ge, op1=ALU.subtract),
             reads=[rk], writes=[rk])
        if which == 0:
            P.op(ACT, lambda e, rr=rr, dst=dst: e.activation(out=dst[0:NP, 0:T], in_=rr[0:NP, 0:T], func=AF.Sin,
                                                             scale=nsgn[0:NP, 0:1]),
                 reads=[rk, "ropec"], writes=[(kpre, "sinS")])
        else:
            P.op(ACT, lambda e, rr=rr, dst=dst: e.activation(out=dst[0:NP, 0:T], in_=rr[0:NP, 0:T], func=AF.Sin,
                                                             scale=-TWO_PI),
                 reads=[rk], writes=[(kpre, "cos2")])


def emit_rope_apply(P, NP, T, src, srckey, cos2, sinS, kpre, qf, qsw, ta, tb, tkey, out_ap, outkeys, in_view=None,
                    eng2=POOL):
    H = NP // 2
    P.op(ACT, lambda e: e.activation(out=qf[0:NP, 0:T], in_=src(0, NP), func=AF.Copy),
         reads=[srckey], writes=[(tkey, "qf")])
    P.op(ACT, lambda e: e.activation(out=qsw[0:H, 0:T], in_=src(H, NP), func=AF.Copy),
         reads=[srckey], writes=[(tkey, "qsw0")])
    P.op(ACT, lambda e: e.activation(out=qsw[H:NP, 0:T], in_=src(0, H), func=AF.Copy),
         reads=[srckey], writes=[(tkey, "qsw1")])
    P.op(DVE, lambda e: e.tensor_tensor(out=ta[0:NP, 0:T], in0=qf[0:NP, 0:T], in1=cos2[0:NP, 0:T], op=ALU.mult),
         reads=[(tkey, "qf"), (kpre, "cos2")], writes=[(tkey, "a")])
    P.op(eng2, lambda e: e.tensor_tensor(out=tb[0:NP, 0:T], in0=qsw[0:NP, 0:T], in1=sinS[0:NP, 0:T], op=ALU.mult),
         reads=[(tkey, "qsw0"), (tkey, "qsw1"), (kpre, "sinS")], writes=[(tkey, "b")])
    v = in_view if in_view is not None else (lambda a: a)
    P.op(DVE, lambda e: e.tensor_tensor(out=out_ap, in0=v(ta[0:NP, 0:T]), in1=v(tb[0:NP, 0:T]), op=ALU.add),
         reads=[(tkey, "a"), (tkey, "b")], writes=outkeys)


def rope_consts_host(NP, half):
    j = np.arange(NP) % half
    inv = (np.float32(10000.0) ** (-(j.astype(np.float32)) / np.float32(half))).astype(np.float32)
    sgn = np.where(np.arange(NP) % (2 * half) < half, -1.0, 1.0).astype(np.float32)
    c = np.zeros((128, 4), np.float32)
    c[:NP, 0] = (inv.astype(np.float64) / (2.0 * np.pi)).astype(np.float32)
    c[:NP, 1] = (-2.0 * np.pi * sgn).astype(np.float32)
    return c


MEMT = 256
MH = 4


def emit_mem_kv(P, memT_d, wmk_d, ones, wbuf, wkeyfn, pj, pjkeys):
    memT = P.sb("memT_sb", [128, KC, MEMT], BF16)
    mkT = P.sb("mkT_sb", [128, MH, MEMT], BF16)
    mv = P.sb("mv_sb", [128, 2, MH * 128], BF16)
    P.dma(POOL, memT[:, :, :], memT_d[:, :, :], writes=["memT"])
    for part in range(2):
        P.dma(POOL, wbuf[:, :, :], wmk_d[:, :, part * 512:(part + 1) * 512], writes=[wkeyfn()])
        if part == 0:
            for h in range(MH):
                i = h % 2

                def mm(e, h=h, i=i):
                    ins = None
                    for kc in range(KC):
                        ins = e.matmul(pj[i][:, 0:MEMT], lhsT=wbuf[:, kc, h * 128:(h + 1) * 128], rhs=memT[:, kc, :],
                                       start=(kc == 0), stop=(kc == KC - 1))
                    return ins
                P.op(PE, mm, reads=[wkeyfn(), "memT"], writes=[pjkeys[i]])
                P.op(ACT, lambda e, h=h, i=i: e.activation(out=mkT[:, h, :], in_=pj[i][:, 0:MEMT], func=AF.Copy),
                     reads=[pjkeys[i]], writes=[("mkT", h)])
        else:
            for mt in range(2):
                i = mt % 2

                def mm(e, mt=mt, i=i):
                    ins = None
                    for kc in range(KC):
                        ins = e.matmul(pj[i][:, :], lhsT=memT[:, kc, mt * 128:(mt + 1) * 128], rhs=wbuf[:, kc, :],
                                       start=(kc == 0), stop=(kc == KC - 1))
                    return ins
                P.op(PE, mm, reads=[wkeyfn(), "memT"], writes=[pjkeys[i]])
                P.op(ACT, lambda e, mt=mt, i=i: e.activation(out=mv[:, mt, :], in_=pj[i][:, :], func=AF.Copy),
                     reads=[pjkeys[i]], writes=[("mv", mt)])
    return mkT, mv


def emit_mem_attn(P, qmT, qmkeyfn, mkT, mv, ones, headsT, hbase, pS, pO, pL, PT, rinv, pOkey="pO", PTkey="PTm"):
    scale = 128.0 ** -0.5
    for h in range(MH):
        for tt in range(TOK // 512):
            sl = slice(tt * 512, (tt + 1) * 512)
            for mt in range(2):
                P.op(PE, lambda e, h=h, mt=mt, sl=sl: e.matmul(pS[mt][:, :], lhsT=mkT[:, h, mt * 128:(mt + 1) * 128],
                                                              rhs=qmT[:, h, sl], start=True, stop=True),
                     reads=[("mkT", h), qmkeyfn(h, tt)], writes=[("pS", mt)])
                P.op(ACT, lambda e, mt=mt: e.activation(out=PT[:, mt, :], in_=pS[mt][:, :], func=AF.Exp, scale=scale),
                     reads=[("pS", mt)], writes=[(PTkey, mt)])

            def mmo(e, h=h):
                ins = None
                for mt in range(2):
                    ins = e.matmul(pO[:, :], lhsT=mv[:, mt, h * 128:(h + 1) * 128], rhs=PT[:, mt, :],
                                   start=(mt == 0), stop=(mt == 1))
                return ins
            P.op(PE, mmo, reads=[("mv", 0), ("mv", 1), (PTkey, 0), (PTkey, 1)], writes=[pOkey])

            def mml(e):
                ins = None
                for mt in range(2):
                    ins = e.matmul(pL[:, :], lhsT=ones[:], rhs=PT[:, mt, :], start=(mt == 0), stop=(mt == 1))
                return ins
            P.op(PE, mml, reads=["ones", (PTkey, 0), (PTkey, 1)], writes=["pL"])
            P.op(DVE, lambda e: e.reciprocal(out=rinv[:, :], in_=pL[:, :]), reads=["pL"], writes=["rinv"])
            P.op(DVE, lambda e, h=h, sl=sl: e.tensor_tensor(out=headsT[:, hbase + h, sl], in0=pO[:, :], in1=rinv[:, :],
                                                            op=ALU.mult),
                 reads=[pOkey, "rinv"], writes=[("heads", hbase + h, tt)])


def emit_out_proj(P, headsT, NHC, wo_d, wo_bufs, xres, xkeyfn, pj, pjkeys):
    cnt = 0
    for fq in range(DM // 512):
        b = fq % 2
        P.dma(POOL, wo_bufs[b][:, :, :], wo_d[:, :, fq * 512:(fq + 1) * 512], writes=[("wo", b)])
        for f4 in range(4):
            fo = fq * 4 + f4
            for tt in range(TOK // 512):
                i = cnt % 2
                cnt += 1
                sl = slice(tt * 512, (tt + 1) * 512)

                def mm(e, f4=f4, sl=sl, i=i, b=b):
                    ins = None
                    for kc in range(NHC):
                        ins = e.matmul(pj[i][:, :], lhsT=wo_bufs[b][:, kc, f4 * 128:(f4 + 1) * 128], rhs=headsT[:, kc, sl],
                                       start=(kc == 0), stop=(kc == NHC - 1))
                    return ins
                P.op(PE, mm, reads=[("wo", b)] + [("heads", kc, tt) for kc in range(NHC)], writes=[pjkeys[i]])
                P.op(DVE, lambda e, fo=fo, sl=sl, i=i: e.tensor_tensor(out=xres[:, fo, sl], in0=pj[i][:, :],
                                                                       in1=xres[:, fo, sl], op=ALU.add),
                     reads=[pjkeys[i], xkeyfn(fo, tt)], writes=[xkeyfn(fo, tt)])


def build_a2():
    P = Prog()
    NHC = 12
    xT_d = P.din("xT", [DM, TOK])
    att_d = P.din("attT", [8 * 128, TOK], BF16)
    wq_d = P.din("wq", [128, KC, 512])
    wmk_d = P.din("wmk", [128, KC, 1024])
    memT_d = P.din("memT", [128, KC, MEMT])
    wo_d = P.din("wo", [128, NHC, DM])
    ga_d = P.din("ga", [128, KC])
    xo_d = P.dout("xo", [DM, TOK])

    ones = emit_consts(P)
    xT = P.sb("xT_sb", [128, KC, TOK], F32)
    hT = P.sb("hT_sb", [128, KC, TOK], BF16)
    ga = P.sb("ga_sb", [128, KC], F32)
    sq = P.sb("sq_sb", [128, KC, 512], BF16)
    rs1 = P.sb("rs1_sb", [128, 512], F32)
    rs2 = P.sb("rs2_sb", [128, 512], F32)
    wbuf = P.sb("wbuf_sb", [128, KC, 512], BF16)
    qmT = P.sb("qmT_sb", [128, MH, TOK], BF16)
    headsT = P.sb("headsT_sb", [128, NHC, TOK], BF16)
    wo_bufs = [P.sb("wo_sb%d" % i, [128, NHC, 512], BF16) for i in range(2)]
    PT = P.sb("PT_sb", [128, 2, 512], BF16)
    rinv = P.sb("rinv_sb", [128, 512], F32)
    pj = [P.ps("pj%d" % i, [128, 512]) for i in range(2)]
    pS = [P.ps("pS%d" % i, [128, 512]) for i in range(2)]
    pO = P.ps("pO", [128, 512])
    pL = P.ps("pL", [128, 512])
    ph = P.ps("ph", [128, 512])
    pjkeys = [("pj", 0), ("pj", 1)]
    wver = [0]

    xT_v = xT_d.rearrange("(k p) t -> p k t", p=128)
    for kc in range(KC):
        P.dma(SP, xT[:, kc, :], xT_v[:, kc, :], writes=[("x", kc, 0), ("x", kc, 1)])
    P.dma(SP, ga[:, :], ga_d[:, :], writes=["ga"])
    att_v = att_d.rearrange("(k p) t -> p k t", p=128)
    for kc in range(8):
        P.dma(SP, headsT[:, kc, :], att_v[:, kc, :], writes=[("heads", kc, 0), ("heads", kc, 1)])
    P.dma(POOL, wbuf[:, :, :], wq_d[:, :, :], writes=["wbuf"])
    for tt in range(2):
        emit_rmsnorm(P, "ga", lambda kc, tt=tt: xT[:, kc, tt * 512:(tt + 1) * 512], [("x", kc, tt) for kc in range(KC)],
                     KC, 512, DM, ga, lambda kc, tt=tt: hT[:, kc, tt * 512:(tt + 1) * 512],
                     [("h", kc, tt) for kc in range(KC)], ones, ph, "ph", sq, rs1, rs2)
    cnt = 0
    for h in range(MH):
        for tt in range(2):
            i = cnt % 2
            cnt += 1
            sl = slice(tt * 512, (tt + 1) * 512)

            def mm(e, h=h, sl=sl, i=i):
                ins = None
                for kc in range(KC):
                    ins = e.matmul(pj[i][:, :], lhsT=wbuf[:, kc, h * 128:(h + 1) * 128], rhs=hT[:, kc, sl],
                                   start=(kc == 0), stop=(kc == KC - 1))
                return ins
            P.op(PE, mm, reads=["wbuf"] + [("h", kc, tt) for kc in range(KC)], writes=[pjkeys[i]])
            P.op(ACT, lambda e, h=h, sl=sl, i=i: e.activation(out=qmT[:, h, sl], in_=pj[i][:, :], func=AF.Copy),
                 reads=[pjkeys[i]], writes=[("qm", h, tt)])
    mkT, mv = emit_mem_kv(P, memT_d, wmk_d, ones, wbuf, lambda: "wbuf", pj, pjkeys)
    emit_mem_attn(P, qmT, lambda h, tt: ("qm", h, tt), mkT, mv, ones, headsT, 8, pS, pO, pL, PT, rinv)
    emit_out_proj(P, headsT, NHC, wo_d, wo_bufs, xT, lambda fo, tt: ("x", fo, tt), pj, pjkeys)
    xo_v = xo_d.rearrange("(k p) t -> p k t", p=128)
    for kc in range(KC):
        P.dma(SP, xo_v[:, kc, :], xT[:, kc, :], reads=[("x", kc, 0), ("x", kc, 1)])
    return P.build()


def pkn(w):
    K_ = w.shape[0] // 128
    return np.ascontiguousarray(w.reshape(K_, 128, w.shape[1]).transpose(1, 0, 2))


def gvec(g):
    return np.ascontiguousarray(g.reshape(-1, 128).T)


def a2_host_inputs(x_tok, att_tok_bf16, w_in, w_mem_kv, mem, w_out, g_attn):
    wq = pkn(w_in[:, 9216:9728])
    wmk = pkn(w_mem_kv)
    memT = pkn(np.ascontiguousarray(mem.T))
    wo = pkn(w_out)
    ga = gvec(g_attn)
    maps = []
    for c in range(NCORES):
        sl = slice(c * TOK, (c + 1) * TOK)
        maps.append({"xT": np.ascontiguousarray(x_tok[sl].T), "attT": np.ascontiguousarray(att_tok_bf16[sl].T),
                     "wq": wq, "wmk": wmk, "memT": memT, "wo": wo, "ga": ga})
    return maps


SEQ = 8192
DIL = (1, 4, 16)
A1_TT = 256
A1_SPAN = 2048


def build_a1():
    P = Prog()
    TT = A1_TT
    TPS = A1_SPAN // TT
    NSP = SEQ // A1_SPAN
    xT_d = P.din("xT", [SEQ // A1_TT, 128, KC, A1_TT])
    w_d = P.din("w", [128, KC, 1152])
    ga_d = P.din("ga", [128, KC])
    posb_d = P.din("posb", [128, SEQ], I32)
    rc_d = P.din("rc", [128, 4])
    mk_d = P.din("mk", [128, 2, 2, 128])
    id_d = P.din("ident", [128, 128])
    att_d = P.dout("attT", [128, SEQ], BF16)

    ones = emit_consts(P)
    w_sb = P.sb("w_sb", [128, KC, 1152], BF16)
    ga = P.sb("ga_sb", [128, KC], F32)
    rc = P.sb("rc_sb", [128, 4], F32)
    mk = P.sb("mk_sb", [128, 2, 2, 128], BF16)
    ident = P.sb("ident_sb", [128, 128], BF16)
    x_sb = [P.sb("x_sb%d" % i, [128, KC, TT], F32) for i in range(2)]
    sq = P.sb("sq_sb", [128, KC, TT], BF16)
    hT = [P.sb("hT_sb%d" % i, [128, KC, TT], BF16) for i in range(2)]
    rs1 = P.sb("rs1_sb", [128, TT], F32)
    rs2 = P.sb("rs2_sb", [128, TT], F32)
    posi = P.sb("posi_sb", [128, TT], I32)
    posf = P.sb("posf_sb", [128, TT], F32)
    kint = P.sb("kint_sb", [128, TT], I32)
    ang = P.sb("ang_sb", [128, TT], F32)
    r1 = P.sb("r1_sb", [128, TT], F32)
    r2 = P.sb("r2_sb", [128, TT], F32)
    cos2 = [P.sb("cos2_sb%d" % i, [128, TT], F32) for i in range(2)]
    sinS = [P.sb("sinS_sb%d" % i, [128, TT], F32) for i in range(2)]
    ta = [P.sb("ta_sb%d" % i, [128, TT], F32) for i in range(2)]
    tb = [P.sb("tb_sb%d" % i, [128, TT], F32) for i in range(2)]
    qkf = [P.sb("qkf_sb%d" % i, [128, TT], F32) for i in range(2)]
    qsw = [P.sb("qsw_sb%d" % i, [128, TT], F32) for i in range(2)]
    QT = [P.sb("QT_sb%d" % g, [128, A1_SPAN], BF16) for g in range(3)]
    KT = [[P.sb("KT_sb%d_%d" % (p, g), [128, A1_SPAN], BF16) for g in range(3)] for p in range(2)]
    VT = [P.sb("VT_sb%d" % g, [128, A1_SPAN], BF16) for g in range(3)]
    V = [P.sb("V_sb%d" % p, [128, 3, 16, 128], BF16) for p in range(2)]
    acc = P.sb("acc_sb", [128, A1_SPAN], F32)
    lacc = P.sb("lacc_sb", [128, A1_SPAN], F32)
    ob = P.sb("ob_sb", [128, A1_SPAN], BF16)
    PT = [P.sb("PT_sb%d" % i, [128, 2, 128], BF16) for i in range(2)]
    ph = P.ps("ph", [128, 512])
    pjb = [P.ps("pjb%d" % i, [128, 512]) for i in range(2)]
    pSb = [P.ps("pSb%d" % i, [128, 4, 128]) for i in range(2)]
    pOL = [P.ps("pOL%d" % i, [128, 4, 128]) for i in range(2)]
    pTb = P.ps("pTb", [128, 8, 128], BF16)
    pj = [pjb[0][:, 0:256], pjb[1][:, 0:256]]
    pT = [pTb[:, 0, :], ph[:, :].bitcast(BF16)[:, 0:128]]
    pTk = ["pT", "ph"]

    P.dma(POOL, w_sb[:, :, :], w_d[:, :, :], writes=["w"])
    P.dma(SP, ga[:, :], ga_d[:, :], writes=["ga"])
    P.dma(SP, rc[:, :], rc_d[:, :], writes=["ropec"])
    P.dma(POOL, mk[:, :, :, :], mk_d[:, :, :, :], writes=["mk"])
    P.dma(POOL, ident[:, :], id_d[:, :], writes=["ident"])
    for g in range(3):
        P.op(DVE, lambda e, g=g: e.memset(KT[1][g][:, :], 0.0), writes=[("KT", 1, g)])
    P.op(DVE, lambda e: e.memset(V[1][:, :, :, :], 0.0), writes=[("V", 1, g, b) for g in range(3) for b in range(16)])

    NT = SEQ // TT

    def load(t):
        P.dma(SP, x_sb[t % 2][:, :, :], xT_d[t], writes=[("x", t % 2)])

    def norm(t):
        b = t % 2
        emit_rmsnorm(P, "ga", lambda kc: x_sb[b][:, kc, :], [("x", b)] * KC, KC, TT, DM, ga,
                     lambda kc: hT[b][:, kc, :], [("h", b, kc) for kc in range(KC)], ones, ph, "ph", sq, rs1, rs2)
        P.dma(SP, posi[:, :], posb_d[:, t * TT:(t + 1) * TT], writes=["posi"])
        emit_rope_tables(P, 128, TT, posi[:, :], "posi", rc[:, 0:1], rc[:, 1:2], rc[:, 2:3], posf, ang, r1, r2,
                         cos2[b], sinS[b], ("rt", b), kint)

    pjc = [0]

    def proj(t):
        b = t % 2
        n = t // TPS
        i = t % TPS
        par = n % 2
        hk = [("h", b, kc) for kc in range(KC)]
        for g in range(3):
            d = DIL[g]
            for tq in range(3):
                col = (g * 3 + tq) * 128
                s = pjc[0] % 2
                s3 = pjc[0] % 3
                pjc[0] += 1

                def mm(e, col=col, s=s, b=b):
                    ins = None
                    for kc in range(KC):
                        ins = e.matmul(pj[s], lhsT=w_sb[:, kc, col:col + 128], rhs=hT[b][:, kc, :],
                                       start=(kc == 0), stop=(kc == KC - 1))
                    return ins
                P.op(PE, mm, reads=["w"] + hk, writes=[("pj", s)])
                if tq == 2:
                    P.op(ACT, lambda e, g=g, i=i, s=s: e.activation(out=VT[g][:, i * TT:(i + 1) * TT], in_=pj[s],
                                                                    func=AF.Copy),
                         reads=[("pj", s)], writes=[("VT", g, i)])
                    continue
                dst = QT[g] if tq == 0 else KT[par][g]
                dkey = ("QT", g, i) if tq == 0 else ("KT", par, g, i)
                if d == 1:
                    out_ap = dst[:, i * TT:(i + 1) * TT]
                    view = None
                else:
                    a = TT // d
                    out_ap = dst[:, :].rearrange("p (r a) -> p r a", r=d)[:, :, a * i:a * (i + 1)]
                    view = (lambda ap, d=d: ap.rearrange("p (a r) -> p r a", r=d))
                emit_rope_apply(P, 128, TT, lambda lo, hi, s=s: pj[s][lo:hi, :], ("pj", s), cos2[b], sinS[b],
                                ("rt", b), qkf[s], qsw[s], ta[s], tb[s], ("rtmp", s), out_ap, [dkey], in_view=view,
                                eng2=POOL)

    sc = [0]
    scale = 128.0 ** -0.5

    def attention(n):
        par = n % 2
        tcount = 0
        for g in range(3):
            d = DIL[g]
            nb = 16 // d
            for r in range(d):
                for m in range(nb):
                    blk = r * nb + m
                    s = tcount % 2
                    tcount += 1
                    st = r + d * 128 * m
                    P.op(PE, lambda e, g=g, s=s, st=st, d=d: e.transpose(out=pT[s],
                                                                         in_=VT[g][:, st:st + d * 127 + 1:d],
                                                                         identity=ident[:, :]),
                         reads=[("VT", g, i) for i in range(TPS)] + ["ident"], writes=[pTk[s]])
                    P.op(ACT, lambda e, g=g, s=s, blk=blk, par=par: e.activation(out=V[par][:, g, blk, :],
                                                                                 in_=pT[s], func=AF.Copy),
                         reads=[pTk[s]], writes=[("V", par, g, blk)])
        for g in range(3):
            d = DIL[g]
            nb = 16 // d
            L = A1_SPAN // d
            kview = lambda p_, g=g, d=d: KT[p_][g][:, :].rearrange("p (r a) -> p r a", r=d)
            qview = QT[g][:, :].rearrange("p (r a) -> p r a", r=d)
            for r in range(d):
                for m in range(nb):
                    blk = r * nb + m
                    s = sc[0] % 2
                    sc[0] += 1
                    kcur = kview(par)[:, r, 128 * m:128 * (m + 1)]
                    if m > 0:
                        kprev = kview(par)[:, r, 128 * (m - 1):128 * m]
                        vprev = V[par][:, g, blk - 1, :]
                        vpk = ("V", par, g, blk - 1)
                        kpk = [("KT", par, g, i) for i in range(TPS)]
                    else:
                        kprev = kview(1 - par)[:, r, L - 128:L]
                        vprev = V[1 - par][:, g, r * nb + nb - 1, :]
                        vpk = ("V", 1 - par, g, r * nb + nb - 1)
                        kpk = [("KT", 1 - par, g, i) for i in range(TPS)] if n > 0 else [("KT", 1, g)]
                    q = qview[:, r, 128 * m:128 * (m + 1)]
                    variant = 1 if (n == 0 and m == 0) else 0

                    def mms(e, s=s, kprev=kprev, kcur=kcur, q=q):
                        e.matmul(pSb[s][:, 0, :], lhsT=kprev, rhs=q, start=True, stop=True)
                        return e.matmul(pSb[s][:, 1, :], lhsT=kcur, rhs=q, start=True, stop=True)
                    P.op(PE, mms, reads=kpk + [("KT", par, g, i) for i in range(TPS)] + [("QT", g, i) for i in range(TPS)],
                         writes=[("pS", s)])
                    P.op(ACT, lambda e, s=s: e.activation(out=PT[s][:, :, :], in_=pSb[s][:, 0:2, :], func=AF.Exp, scale=scale),
                         reads=[("pS", s)], writes=[("PT", s)])
                    P.op(DVE, lambda e, s=s, variant=variant: e.tensor_tensor(out=PT[s][:, :, :], in0=PT[s][:, :, :],
                                                                              in1=mk[:, variant, :, :], op=ALU.mult),
                         reads=[("PT", s), "mk"], writes=[("PT", s)])

                    def mmo(e, s=s, vprev=vprev, g=g, blk=blk, par=par):
                        e.matmul(pOL[s][:, 0, :], lhsT=vprev, rhs=PT[s][:, 0, :], start=True, stop=False)
                        e.matmul(pOL[s][:, 0, :], lhsT=V[par][:, g, blk, :], rhs=PT[s][:, 1, :], start=False, stop=True)
                        e.matmul(pOL[s][:, 1, :], lhsT=ones[:], rhs=PT[s][:, 0, :], start=True, stop=False)
                        return e.matmul(pOL[s][:, 1, :], lhsT=ones[:], rhs=PT[s][:, 1, :], start=False, stop=True)
                    P.op(PE, mmo, reads=[("PT", s), vpk, ("V", par, g, blk), "ones"], writes=[("pOL", s)])
                    st = r + d * 128 * m
                    cols = slice(st, st + d * 127 + 1, d)
                    if g == 0:
                        pbs = [m]
                    elif g == 1:
                        pbs = [4 * m + j for j in range(4)]
                    else:
                        pbs = list(range(16))
                    akeys = [("acc", pb) for pb in pbs]
                    if g == 0:
                        P.op(ACT, lambda e, s=s, cols=cols: e.activation(out=acc[:, cols], in_=pOL[s][:, 0, :], func=AF.Copy),
                             reads=[("pOL", s)], writes=akeys)
                        P.op(ACT, lambda e, s=s, cols=cols: e.activation(out=lacc[:, cols], in_=pOL[s][:, 1, :], func=AF.Copy),
                             reads=[("pOL", s)], writes=[("lacc", pb) for pb in pbs])
                    else:
                        P.op(DVE, lambda e, s=s, cols=cols: e.tensor_tensor(out=acc[:, cols], in0=pOL[s][:, 0, :],
                                                                            in1=acc[:, cols], op=ALU.add),
                             reads=[("pOL", s)] + akeys, writes=akeys)
                        P.op(DVE, lambda e, s=s, cols=cols: e.tensor_tensor(out=lacc[:, cols], in0=pOL[s][:, 1, :],
                                                                            in1=lacc[:, cols], op=ALU.add),
                             reads=[("pOL", s)] + [("lacc", pb) for pb in pbs], writes=[("lacc", pb) for pb in pbs])
        allk = [("acc", pb) for pb in range(16)]
        alll = [("lacc", pb) for pb in range(16)]
        P.op(DVE, lambda e: e.reciprocal(out=lacc[:, :], in_=lacc[:, :]), reads=alll, writes=alll)
        P.op(DVE, lambda e: e.tensor_tensor(out=ob[:, :], in0=acc[:, :], in1=lacc[:, :], op=ALU.mult),
             reads=allk + alll, writes=["ob"])
        P.dma(SP, att_d[:, n * A1_SPAN:(n + 1) * A1_SPAN], ob[:, :], reads=["ob"])

    load(0)
    load(1)
    norm(0)
    for t in range(NT):
        if t + 1 < NT:
            norm(t + 1)
        if t + 2 < NT:
            load(t + 2)
        proj(t)
        if t % TPS == TPS - 1:
            attention(t // TPS)
    return P.build()


def a1_host_inputs(x_tok, positions, w_in, g_attn):
    xT = np.ascontiguousarray(x_tok.reshape(SEQ // A1_TT, A1_TT, KC, 128).transpose(0, 3, 2, 1))
    posb = np.ascontiguousarray(np.broadcast_to(positions.astype(np.int32)[None, :], (128, SEQ)))
    rc = rope_consts_host(128, 64)
    k = np.arange(128)[:, None]
    q = np.arange(128)[None, :]
    mk = np.zeros((128, 2, 2, 128), np.float32)
    mk[:, 0, 0, :] = (k >= q)
    mk[:, 0, 1, :] = (k <= q)
    mk[:, 1, 1, :] = (k <= q)
    ident = np.eye(128, dtype=np.float32)
    ga = gvec(g_attn)
    w4 = w_in[:, :9216].reshape(DM, 3, 3, 8, 128)
    maps = []
    for c in range(NCORES):
        w = pkn(np.ascontiguousarray(w4[:, :, :, c, :]).reshape(DM, 1152))
        maps.append({"xT": xT, "w": w, "ga": ga, "posb": posb, "rc": rc, "mk": mk, "ident": ident})
    return maps


NHB = 12


def build_proj(kind):
    P = Prog()
    xT_d = P.din("xT", [DM, TOK])
    gx_d = P.din("gx", [128, KC])
    gl_d = P.din("gl", [128, 4])
    posb_d = P.din("posb", [128, TOK], I32)
    rc_d = P.din("rc", [128, 4])
    if kind == "kv":
        w1_d = P.din("w1", [128, KC, 576])
        wuk_d = P.din("wuk", [128, 4, 1536])
        wuv_d = P.din("wuv", [128, 4, 1536])
        kn_d = P.dout("knT", [NHB, 128, TOK], BF16)
        kr_d = P.dout("krT", [64, TOK], BF16)
        v_d = P.dout("v", [TOK // 128, 128, 1536], BF16)
    else:
        w1_d = P.din("w1", [128, KC, 1024])
        wuq_d = P.din("wuq", [128, 4, NHB * 192])
        qn_d = P.dout("qnT", [NHB, 128, TOK], BF16)
        qr_d = P.dout("qrT", [NHB, 64, TOK], BF16)
        qm_d = P.dout("qmT", [MH, 128, TOK], BF16)

    ones = emit_consts(P)
    xT = P.sb("xT_sb", [128, KC, TOK], F32)
    hT = P.sb("hT_sb", [128, KC, TOK], BF16)
    gx = P.sb("gx_sb", [128, KC], F32)
    gl = P.sb("gl_sb", [128, 4], F32)
    rc = P.sb("rc_sb", [128, 4], F32)
    sq = P.sb("sq_sb", [128, KC, 512], BF16)
    rs1 = P.sb("rs1_sb", [128, 512], F32)
    rs2 = P.sb("rs2_sb", [128, 512], F32)
    wbuf = P.sb("wbuf_sb", [128, KC, 576], BF16)
    cl = P.sb("cl_sb", [128, 4, TOK], F32)
    cn = P.sb("cn_sb", [128, 4, TOK], BF16)
    wu = P.sb("wu_sb", [128, 4, NHB * 192], BF16)
    posi = P.sb("posi_sb", [128, 512], I32)
    posf = P.sb("posf_sb", [128, 512], F32)
    kint = P.sb("kint_sb", [128, 512], I32)
    ang = P.sb("ang_sb", [128, 512], F32)
    r1 = P.sb("r1_sb", [128, 512], F32)
    r2 = P.sb("r2_sb", [128, 512], F32)
    cos2 = [P.sb("cos2_sb%d" % i, [128, 512], F32) for i in range(2)]
    sinS = [P.sb("sinS_sb%d" % i, [128, 512], F32) for i in range(2)]
    qf = P.sb("qf_sb", [128, 512], F32)
    qsw = P.sb("qsw_sb", [128, 512], F32)
    ta = P.sb("ta_sb", [128, 512], F32)
    tb = P.sb("tb_sb", [128, 512], F32)
    stg = [P.sb("stg_sb%d" % i, [128, 512], BF16) for i in range(3)]
    pj = [P.ps("pj%d" % i, [128, 512]) for i in range(3)]
    ph = P.ps("ph", [128, 512])
    pjk = [("pj", i) for i in range(3)]
    cnt = [0]
    scnt = [0]

    xT_v = xT_d.rearrange("(k p) t -> p k t", p=128)
    for kc in range(KC):
        P.dma(SP, xT[:, kc, :], xT_v[:, kc, :], writes=[("x", kc, 0), ("x", kc, 1)])
    P.dma(SP, gx[:, :], gx_d[:, :], writes=["gx"])
    P.dma(SP, gl[:, :], gl_d[:, :], writes=["gl"])
    P.dma(SP, rc[:, :], rc_d[:, :], writes=["ropec"])
    NW1 = 576 if kind == "kv" else 512
    P.dma(POOL, wbuf[:, :, 0:NW1], w1_d[:, :, 0:NW1], writes=["wbuf"])
    if kind == "kv":
        P.dma(POOL, wu[:, :, 0:1536], wuk_d[:, :, :], writes=["wu"])
    else:
        P.dma(POOL, wu[:, :, :], wuq_d[:, :, :], writes=["wu"])
    for tt in range(2):
        emit_rmsnorm(P, "gx", lambda kc, tt=tt: xT[:, kc, tt * 512:(tt + 1) * 512], [("x", kc, tt) for kc in range(KC)],
                     KC, 512, DM, gx, lambda kc, tt=tt: hT[:, kc, tt * 512:(tt + 1) * 512],
                     [("h", kc, tt) for kc in range(KC)], ones, ph, "ph", sq, rs1, rs2)
    for tt in range(2):
        P.dma(SP, posi[:, :], posb_d[:, tt * 512:(tt + 1) * 512], writes=["posi"])
        emit_rope_tables(P, 64, 512, posi[0:64, :], "posi", rc[:, 0:1], rc[:, 1:2], rc[:, 2:3], posf, ang, r1, r2,
                         cos2[tt], sinS[tt], ("rt", tt), kint)

    def hk(tt):
        return [("h", kc, tt) for kc in range(KC)]

    def mm_in(col, M, tt, s):
        def mm(e):
            ins = None
            for kc in range(KC):
                ins = e.matmul(pj[s][0:M, :], lhsT=wbuf[:, kc, col:col + M], rhs=hT[:, kc, tt * 512:(tt + 1) * 512],
                               start=(kc == 0), stop=(kc == KC - 1))
            return ins
        return mm

    def out_bf16(src_ap, srckey, dst_ap, NP=128):
        i = scnt[0] % 3
        scnt[0] += 1
        P.op(ACT, lambda e: e.activation(out=stg[i][0:NP, :], in_=src_ap, func=AF.Copy), reads=[srckey],
             writes=[("stg", i)])
        P.dma(SP, dst_ap, stg[i][0:NP, :], reads=[("stg", i)])

    def rope_out(s, tt, dst_ap):
        i = scnt[0] % 3
        scnt[0] += 1
        emit_rope_apply(P, 64, 512, lambda lo, hi: pj[s][lo:hi, :], pjk[s], cos2[tt], sinS[tt], ("rt", tt), qf, qsw, ta, tb,
                        "rtmp", stg[i][0:64, :], [("stg", i)])
        P.dma(SP, dst_ap, stg[i][0:64, :], reads=[("stg", i)])

    for j in range(4):
        for tt in range(2):
            s = cnt[0] % 3
            cnt[0] += 1
            P.op(PE, mm_in(j * 128, 128, tt, s), reads=["wbuf"] + hk(tt), writes=[pjk[s]])
            P.op(ACT, lambda e, j=j, tt=tt, s=s: e.activation(out=cl[:, j, tt * 512:(tt + 1) * 512], in_=pj[s][:, :],
                                                              func=AF.Copy), reads=[pjk[s]], writes=[("cl", j, tt)])
    if kind == "kv":
        for tt in range(2):
            s = cnt[0] % 3
            cnt[0] += 1
            P.op(PE, mm_in(512, 64, tt, s), reads=["wbuf"] + hk(tt), writes=[pjk[s]])
            rope_out(s, tt, kr_d[:, tt * 512:(tt + 1) * 512])
    else:
        P.dma(POOL, wbuf[:, :, 0:512], w1_d[:, :, 512:1024], writes=["wbuf"])
        for h in range(MH):
            for tt in range(2):
                s = cnt[0] % 3
                cnt[0] += 1
                P.op(PE, mm_in(h * 128, 128, tt, s), reads=["wbuf"] + hk(tt), writes=[pjk[s]])
                out_bf16(pj[s][:, :], pjk[s], qm_d[h][:, tt * 512:(tt + 1) * 512])
    for tt in range(2):
        emit_rmsnorm(P, "gl", lambda kc, tt=tt: cl[:, kc, tt * 512:(tt + 1) * 512], [("cl", kc, tt) for kc in range(4)],
                     4, 512, 512, gl, lambda kc, tt=tt: cn[:, kc, tt * 512:(tt + 1) * 512],
                     [("cn", kc, tt) for kc in range(4)], ones, ph, "ph", sq, rs1, rs2)

    def mm_up(col, M, tt, s):
        def mm(e):
            ins = None
            for kc in range(4):
                ins = e.matmul(pj[s][0:M, :], lhsT=wu[:, kc, col:col + M], rhs=cn[:, kc, tt * 512:(tt + 1) * 512],
                               start=(kc == 0), stop=(kc == 3))
            return ins
        return mm

    cnk = lambda tt: [("cn", kc, tt) for kc in range(4)]
    if kind == "kv":
        for h in range(NHB):
            for tt in range(2):
                s = cnt[0] % 3
                cnt[0] += 1
                P.op(PE, mm_up(h * 128, 128, tt, s), reads=["wu"] + cnk(tt), writes=[pjk[s]])
                out_bf16(pj[s][:, :], pjk[s], kn_d[h][:, tt * 512:(tt + 1) * 512])
        P.dma(POOL, wu[:, :, 0:1536], wuv_d[:, :, :], writes=["wu"])
        for blk in range(TOK // 128):
            tt = blk // 4
            for nt in range(3):
                s = cnt[0] % 3
                cnt[0] += 1

                def mm(e, blk=blk, nt=nt, s=s):
                    ins = None
                    for kc in range(4):
                        ins = e.matmul(pj[s][:, :], lhsT=cn[:, kc, blk * 128:(blk + 1) * 128],
                                       rhs=wu[:, kc, nt * 512:(nt + 1) * 512], start=(kc == 0), stop=(kc == 3))
                    return ins
                P.op(PE, mm, reads=["wu"] + cnk(tt), writes=[pjk[s]])
                out_bf16(pj[s][:, :], pjk[s], v_d[blk][:, nt * 512:(nt + 1) * 512])
    else:
        for h in range(NHB):
            for tt in range(2):
                s = cnt[0] % 3
                cnt[0] += 1
                P.op(PE, mm_up(h * 192, 128, tt, s), reads=["wu"] + cnk(tt), writes=[pjk[s]])
                out_bf16(pj[s][:, :], pjk[s], qn_d[h][:, tt * 512:(tt + 1) * 512])
                s = cnt[0] % 3
                cnt[0] += 1
                P.op(PE, mm_up(h * 192 + 128, 64, tt, s), reads=["wu"] + cnk(tt), writes=[pjk[s]])
                rope_out(s, tt, qr_d[h][:, tt * 512:(tt + 1) * 512])
    return P.build()


def posb_of(pos_tok):
    return np.ascontiguousarray(np.broadcast_to(pos_tok.astype(np.int32)[None, :], (128, pos_tok.shape[0])))


def zz_block(c, m):
    return 8 * m + (c if m % 2 == 0 else 7 - c)


def build_b1b():
    P = Prog()
    NHC = 16
    NKB = SEQ // 128
    qn_d = P.din("qnT", [NHB, 128, TOK], BF16)
    qr_d = P.din("qrT", [NHB, 64, TOK], BF16)
    qm_d = P.din("qmT", [MH, 128, TOK], BF16)
    kn_d = P.din("knT", [NHB, 128, SEQ], BF16)
    kr_d = P.din("krT", [64, SEQ], BF16)
    vv_d = P.din("vv", [NHB, 128, NKB, 128], BF16)
    mask_d = P.din("mask", [128, 8, 8, 128], BF16)
    memT_d = P.din("memT", [128, KC, MEMT])
    wmk_d = P.din("wmk", [128, KC, 1024])
    wo_d = P.din("wo", [128, NHC, DM])
    xT_d = P.din("xT", [DM, TOK])
    xo_d = P.dout("xo", [DM, TOK])

    ones = emit_consts(P)
    ones32 = P.sb("ones32_sb", [128, 128], F32)
    P.op(DVE, lambda e: e.memset(ones32[:], 1.0), writes=["ones32"])
    krT = P.sb("krT_sb", [64, SEQ], BF16)
    Kb = [P.sb("Kb_sb%d" % i, [128, 32 * 128], BF16) for i in range(2)]
    Vb = [P.sb("Vb_sb%d" % i, [128, 32, 128], BF16) for i in range(2)]
    qn = [P.sb("qn_sb%d" % i, [128, TOK], BF16) for i in range(2)]
    qr = [P.sb("qr_sb%d" % i, [64, TOK], BF16) for i in range(2)]
    qmT = P.sb("qmT_sb", [128, MH, TOK], BF16)
    mask = P.sb("mask_sb", [128, 8, 8, 128], BF16)
    headsT = P.sb("headsT_sb", [128, NHC, TOK], BF16)
    PT = [P.sb("PT_sb%d" % i, [128, 512], BF16) for i in range(2)]
    PTm = P.sb("PTm_sb", [128, 2, 512], BF16)
    lacc = [P.sb("lacc_sb%d" % i, [128, 512], F32) for i in range(2)]
    rinv = P.sb("rinv_sb", [128, 512], F32)
    wbuf = P.sb("wbuf_sb", [128, KC, 512], BF16)
    wo_bufs = [P.sb("wo_sb%d" % i, [128, NHC, 512], BF16) for i in range(2)]
    xs = [P.sb("xs_sb%d" % i, [128, 512], F32) for i in range(2)]
    pS = [P.ps("pS%d" % i, [128, 512]) for i in range(2)]
    pO = [P.ps("pO%d" % i, [128, 512]) for i in range(2)]
    pL = P.ps("pL", [128, 512])
    pj = [P.ps("pj%d" % i, [128, 512]) for i in range(2)]
    pjkeys = [("pj", 0), ("pj", 1)]

    P.dma(SP, krT[:, :], kr_d[:, :], writes=["krT"])
    P.dma(SP, mask[:, :, :, :], mask_d[:, :, :, :], writes=["mask"])
    for h in range(MH):
        P.dma(SP, qmT[:, h, :], qm_d[h], writes=[("qm", h, 0), ("qm", h, 1)])

    scale = 192.0 ** -0.5
    sc = [0]

    def load_kv(h, half):
        seq = 2 * h + half
        b = seq % 2
        P.dma(SP, Kb[b][:, :], kn_d[h][:, half * 4096:(half + 1) * 4096], writes=[("K", b)])
        P.dma(SP, Vb[b][:, :, :], vv_d[h][:, half * 32:(half + 1) * 32, :], writes=[("V", b)])

    def load_q(h):
        b = h % 2
        P.dma(SP, qn[b][:, :], qn_d[h], writes=[("qn", b)])
        P.dma(SP, qr[b][:, :], qr_d[h], writes=[("qr", b)])

    def attend(h, X, kbs, first, last):
        o = (2 * h + X) % 2
        qb = h % 2
        for kb in kbs:
            half = kb // 32
            b = (2 * h + half) % 2
            m0 = kb // 8
            masked = (m0 >= 4 * X) and (m0 < 4 * X + 4)
            c0 = 128 * (m0 - 4 * X) if masked else 0
            N = 512 - c0
            s = sc[0] % 2
            sc[0] += 1
            q0 = X * 512 + c0
            kl = kb % 32

            def mms(e, s=s, b=b, kl=kl, kb=kb, q0=q0, N=N, qb=qb):
                e.matmul(pS[s][:, 0:N], lhsT=Kb[b][:, kl * 128:(kl + 1) * 128], rhs=qn[qb][:, q0:q0 + N],
                         start=True, stop=False)
                return e.matmul(pS[s][:, 0:N], lhsT=krT[0:64, kb * 128:(kb + 1) * 128], rhs=qr[qb][0:64, q0:q0 + N],
                                start=False, stop=True)
            P.op(PE, mms, reads=[("K", b), "krT", ("qn", qb), ("qr", qb)], writes=[("pS", s)])
            P.op(ACT, lambda e, s=s, N=N: e.activation(out=PT[s][:, 0:N], in_=pS[s][:, 0:N], func=AF.Exp, scale=scale),
                 reads=[("pS", s)], writes=[("PT", s)])
            if masked:
                P.op(POOL, lambda e, s=s, m0=m0, kb=kb: e.tensor_tensor(out=PT[s][:, 0:128], in0=PT[s][:, 0:128],
                                                                         in1=mask[:, m0, kb - 8 * m0, :], op=ALU.mult),
                     reads=[("PT", s), "mask"], writes=[("PT", s)])
            isf = first and kb == kbs[0]
            isl = last and kb == kbs[-1]
            P.op(PE, lambda e, s=s, b=b, kl=kl, c0=c0, N=N, o=o, isf=isf, isl=isl: e.matmul(
                pO[o][:, c0:512], lhsT=Vb[b][:, kl, :], rhs=PT[s][:, 0:N], start=isf, stop=isl),
                reads=[("V", b), ("PT", s)], writes=[("pO", o)])
            if isf:
                P.op(DVE, lambda e, s=s, o=o: e.tensor_copy(out=lacc[o][:, :], in_=PT[s][:, :]),
                     reads=[("PT", s)], writes=[("lacc", o)])
            else:
                P.op(DVE, lambda e, s=s, o=o, c0=c0, N=N: e.tensor_tensor(out=lacc[o][:, c0:512], in0=lacc[o][:, c0:512],
                                                                          in1=PT[s][:, 0:N], op=ALU.add),
                     reads=[("PT", s), ("lacc", o)], writes=[("lacc", o)])

    def finalize(h, X):
        o = (2 * h + X) % 2
        P.op(PE, lambda e, o=o: e.matmul(pL[:, :], lhsT=ones32[:], rhs=lacc[o][:, :], start=True, stop=True),
             reads=["ones32", ("lacc", o)], writes=["pL"])
        P.op(DVE, lambda e: e.reciprocal(out=rinv[:, :], in_=pL[:, :]), reads=["pL"], writes=["rinv"])
        P.op(DVE, lambda e, h=h, X=X, o=o: e.tensor_tensor(out=headsT[:, h, X * 512:(X + 1) * 512], in0=pO[o][:, :],
                                                           in1=rinv[:, :], op=ALU.mult),
             reads=[("pO", o), "rinv"], writes=[("heads", h, X)])

    load_q(0)
    load_kv(0, 0)
    for h in range(NHB):
        load_kv(h, 1)
        attend(h, 0, list(range(0, 32)), True, True)
        finalize(h, 0)
        attend(h, 1, list(range(0, 32)), True, False)
        if h + 1 < NHB:
            load_q(h + 1)
            load_kv(h + 1, 0)
        attend(h, 1, list(range(32, 64)), False, True)
        finalize(h, 1)

    mkT, mv = emit_mem_kv(P, memT_d, wmk_d, ones, wbuf, lambda: "wbuf", pj, pjkeys)
    emit_mem_attn(P, qmT, lambda h, tt: ("qm", h, tt), mkT, mv, ones, headsT, NHB, pS, pO[0], pL, PTm, rinv,
                  pOkey=("pO", 0))

    xT_v = xT_d.rearrange("(k p) t -> p k t", p=128)
    xo_v = xo_d.rearrange("(k p) t -> p k t", p=128)
    cnt = 0
    for fq in range(DM // 512):
        b = fq % 2
        P.dma(POOL, wo_bufs[b][:, :, :], wo_d[:, :, fq * 512:(fq + 1) * 512], writes=[("wo", b)])
        for f4 in range(4):
            fo = fq * 4 + f4
            for tt in range(2):
                i = cnt % 2
                cnt += 1
                sl = slice(tt * 512, (tt + 1) * 512)
                P.dma(SP, xs[i][:, :], xT_v[:, fo, sl], writes=[("xs", i)])

                def mm(e, f4=f4, sl=sl, i=i, b=b):
                    ins = None
                    for kc in range(NHC):
                        ins = e.matmul(pj[i][:, :], lhsT=wo_bufs[b][:, kc, f4 * 128:(f4 + 1) * 128], rhs=headsT[:, kc, sl],
                                       start=(kc == 0), stop=(kc == NHC - 1))
                    return ins
                P.op(PE, mm, reads=[("wo", b)] + [("heads", kc, tt) for kc in range(NHC)], writes=[pjkeys[i]])
                P.op(DVE, lambda e, i=i: e.tensor_tensor(out=xs[i][:, :], in0=pj[i][:, :], in1=xs[i][:, :], op=ALU.add),
                     reads=[pjkeys[i], ("xs", i)], writes=[("xs", i)])
                P.dma(SP, xo_v[:, fo, sl], xs[i][:, :], reads=[("xs", i)])
    return P.build()


import ml_dtypes

BF16_NP = ml_dtypes.bfloat16


def _run(nc, maps):
    res = run_bass_kernel_spmd(nc, maps, core_ids=list(range(NCORES)))
    return res.results


def zz_index(c):
    return np.concatenate([np.arange(zz_block(c, m) * 128, zz_block(c, m) * 128 + 128) for m in range(8)])


def zz_mask(c):
    k = np.arange(128)[:, None]
    q = np.arange(128)[None, :]
    tri = (k <= q).astype(np.float32)
    mk = np.zeros((128, 8, 8, 128), np.float32)
    for m in range(8):
        bm = zz_block(c, m)
        for j in range(8):
            kb = 8 * m + j
            if kb < bm:
                mk[:, m, j, :] = 1.0
            elif kb == bm:
                mk[:, m, j, :] = tri
    return mk.astype(BF16_NP)


def run_a_layer(x, pos, mem, w_in, w_mem_kv, w_out, g_attn):
    res = _run(build_a1(), a1_host_inputs(x, pos, w_in, g_attn))
    att_tok = np.concatenate([r["attT"].T for r in res], axis=1)
    res = _run(build_a2(), a2_host_inputs(x, att_tok, w_in, w_mem_kv, mem, w_out, g_attn))
    return np.concatenate([r["xo"].T for r in res], axis=0)


def run_ffn(x, w_gate, w_val, conv_w, conv_b, w_down, g_ffn, g_final=None):
    res = _run(build_ffn(final_norm=g_final is not None),
               ffn_host_inputs(x, w_gate, w_val, conv_w, conv_b, w_down, g_ffn, g_final))
    xo = np.concatenate([r["xo"].T for r in res], axis=0)
    fin = None
    if g_final is not None:
        fin = np.concatenate([r["fin"].T for r in res], axis=0)
    return xo, fin


def run_kv(x, pos, g_norm, w_dkv, g_latent, w_uk, w_uv):
    rc = rope_consts_host(64, 32)
    maps = []
    for c in range(NCORES):
        sl = slice(c * TOK, (c + 1) * TOK)
        maps.append({"xT": np.ascontiguousarray(x[sl].T), "gx": gvec(g_norm), "gl": gvec(g_latent),
                     "posb": posb_of(pos[sl]), "rc": rc, "w1": pkn(w_dkv), "wuk": pkn(w_uk), "wuv": pkn(w_uv)})
    res = _run(build_proj("kv"), maps)
    knT = np.ascontiguousarray(np.concatenate([r["knT"] for r in res], axis=2))
    krT = np.ascontiguousarray(np.concatenate([r["krT"] for r in res], axis=1))
    v = np.concatenate([r["v"].reshape(TOK, NHB, 128) for r in res], axis=0)
    vv = np.ascontiguousarray(v.reshape(SEQ // 128, 128, NHB, 128).transpose(2, 1, 0, 3))
    return knT, krT, vv


def run_b_layer(x, pos, mem, kvs, w_in, g_qnorm, w_uq, w_mem_kv, w_out, g_attn):
    knT, krT, vv = kvs
    rc = rope_consts_host(64, 32)
    idx = [zz_index(c) for c in range(NCORES)]
    xTs = [np.ascontiguousarray(x[idx[c]].T) for c in range(NCORES)]
    maps = []
    for c in range(NCORES):
        maps.append({"xT": xTs[c], "gx": gvec(g_attn), "gl": gvec(g_qnorm), "posb": posb_of(pos[idx[c]]), "rc": rc,
                     "w1": pkn(w_in), "wuq": pkn(w_uq)})
    rq = _run(build_proj("q"), maps)
    memT = pkn(np.ascontiguousarray(mem.T))
    wmk = pkn(w_mem_kv)
    wo = pkn(w_out)
    maps = []
    for c in range(NCORES):
        maps.append({"qnT": rq[c]["qnT"], "qrT": rq[c]["qrT"], "qmT": rq[c]["qmT"], "knT": knT, "krT": krT, "vv": vv,
                     "mask": zz_mask(c), "memT": memT, "wmk": wmk, "wo": wo, "xT": xTs[c]})
    res = _run(build_b1b(), maps)
    xo = np.empty_like(x)
    for c in range(NCORES):
        xo[idx[c]] = res[c]["xo"].T
    return xo


def kernel(x, mem, positions, a_w_in, a_w_mem_kv, a_w_out, b_w_in, b_g_qnorm, b_w_uq, b_w_mem_kv, b_w_out,
           kv_g_norm, kv_w_dkv, kv_g_latent, kv_w_uk, kv_w_uv, g_attn, g_ffn, ffn_w_gate, ffn_w_val, ffn_conv_w,
           ffn_conv_b, ffn_w_down, g_final):
    f = lambda a: np.asarray(a, dtype=np.float32)
    xs = f(x)[0]
    memh = f(mem)[0]
    pos = np.asarray(positions)[0].astype(np.int32)
    fin = None
    kvs = None
    for layer in range(4):
        if layer < 2:
            xs = run_a_layer(xs, pos, memh, f(a_w_in[layer]), f(a_w_mem_kv[layer]), f(a_w_out[layer]), f(g_attn[layer]))
        else:
            if kvs is None:
                kvs = run_kv(xs, pos, f(kv_g_norm), f(kv_w_dkv), f(kv_g_latent), f(kv_w_uk), f(kv_w_uv))
            i = layer - 2
            xs = run_b_layer(xs, pos, memh, kvs, f(b_w_in[i]), f(b_g_qnorm[i]), f(b_w_uq[i]), f(b_w_mem_kv[i]),
                             f(b_w_out[i]), f(g_attn[layer]))
        xs, fin = run_ffn(xs, f(ffn_w_gate[layer]), f(ffn_w_val[layer]), f(ffn_conv_w[layer]), f(ffn_conv_b[layer]),
                          f(ffn_w_down[layer]), f(g_ffn[layer]), f(g_final) if layer == 3 else None)
    return np.ascontiguousarray(fin[None]).astype(np.float32)
```

```python
import numpy as np
from contextlib import ExitStack
import concourse.bass as bass
import concourse.mybir as mybir
from concourse.bass_utils import run_bass_kernel_spmd

F32 = mybir.dt.float32
BF16 = mybir.dt.bfloat16
I32 = mybir.dt.int32
AF = mybir.ActivationFunctionType
ALU = mybir.AluOpType

PE, ACT, DVE, POOL, SP = "pe", "act", "dve", "pool", "sp"
NCORES = 8
SAME_ENGINE_SYNC = True
N_DMA_SEMS = 8


class Op:
    __slots__ = ("eng", "fn", "deps", "signal", "ticket", "is_dma", "dsem", "dcount", "presem", "idx")

    def __init__(self, eng, fn, is_dma):
        self.eng = eng
        self.fn = fn
        self.deps = ()
        self.signal = False
        self.ticket = 0
        self.is_dma = is_dma
        self.dsem = None
        self.dcount = 0
        self.presem = None


class Prog:
    def __init__(self):
        self.nc = bass.Bass("TRN2", target_bir_lowering=False)
        self.ops = []
        self.last_w = {}
        self.rd_eng = {}
        self.rd_dma = {}
        self.stack = ExitStack()
        self.uid = 0

    def din(self, name, shape, dt=F32):
        return self.nc.dram_tensor(name, list(shape), dt, kind="ExternalInput").ap()

    def dout(self, name, shape, dt=F32):
        return self.nc.dram_tensor(name, list(shape), dt, kind="ExternalOutput").ap()

    def sb(self, name, shape, dt):
        return self.stack.enter_context(self.nc.sbuf_tensor(name, list(shape), dt))

    def ps(self, name, shape, dt=F32):
        return self.stack.enter_context(self.nc.psum_tensor(name, list(shape), dt))

    def op(self, eng, fn, reads=(), writes=(), is_dma=False):
        o = Op(eng, fn, is_dma)
        deps = set()
        for k in reads:
            w = self.last_w.get(k)
            if w is not None:
                deps.add(w)
        for k in writes:
            w = self.last_w.get(k)
            if w is not None:
                deps.add(w)
            for r in self.rd_eng.get(k, {}).values():
                deps.add(r)
            for r in self.rd_dma.get(k, ()):
                deps.add(r)
        for k in reads:
            if is_dma:
                self.rd_dma.setdefault(k, []).append(o)
            else:
                self.rd_eng.setdefault(k, {})[eng] = o
        for k in writes:
            self.last_w[k] = o
            self.rd_eng[k] = {}
            self.rd_dma[k] = []
        deps.discard(o)
        o.deps = tuple(deps)
        self.ops.append(o)
        return o

    def dma(self, queue, out, in_, reads=(), writes=()):
        return self.op(queue, lambda e: e.dma_start(out=out, in_=in_), reads, writes, is_dma=True)

    def build(self):
        nc = self.nc
        for i, o in enumerate(self.ops):
            o.idx = i
        for o in self.ops:
            latest = {}
            keep = []
            for d in o.deps:
                if d.is_dma:
                    keep.append(d)
                    continue
                if d.eng == o.eng and (d.eng == PE or not SAME_ENGINE_SYNC):
                    continue
                if d.eng not in latest or latest[d.eng].idx < d.idx:
                    latest[d.eng] = d
            for d in latest.values():
                d.signal = True
                keep.append(d)
            o.deps = tuple(keep)
        cnt = {}
        dma_i = {}
        for o in self.ops:
            if o.is_dma:
                i = dma_i.get(o.eng, 0)
                dma_i[o.eng] = i + 1
                o.dsem = (o.eng, i % N_DMA_SEMS)
                o.dcount = 16 * (i // N_DMA_SEMS + 1)
                if i >= N_DMA_SEMS:
                    o.presem = (o.dsem, o.dcount - 16)
            elif o.signal:
                cnt[o.eng] = cnt.get(o.eng, 0) + 1
                o.ticket = cnt[o.eng]
        sems = {}
        for eng in (PE, ACT, DVE, POOL):
            sems[eng] = self.stack.enter_context(nc.semaphore("s_" + eng))
        for q in (SP, POOL, ACT):
            if q in dma_i:
                for i in range(min(N_DMA_SEMS, dma_i[q])):
                    sems[(q, i)] = self.stack.enter_context(nc.semaphore("d_%s%d" % (q, i)))
        final_dma = {}
        for o in self.ops:
            if o.is_dma:
                final_dma[o.dsem] = o.dcount
        block = self.stack.enter_context(nc.Block())

        def make_body(eng_name):
            ops_e = [o for o in self.ops if o.eng == eng_name]

            def body(e):
                waited = {}
                for o in ops_e:
                    need = {}
                    for d in o.deps:
                        if d.is_dma:
                            k, v = d.dsem, d.dcount
                        else:
                            if d.eng == eng_name and (eng_name == PE or not SAME_ENGINE_SYNC):
                                continue
                            k, v = d.eng, d.ticket
                        if waited.get(k, 0) >= v:
                            continue
                        if need.get(k, 0) < v:
                            need[k] = v
                    if o.presem is not None:
                        k, v = o.presem
                        if waited.get(k, 0) < v and need.get(k, 0) < v:
                            need[k] = v
                    for k, v in need.items():
                        e.wait_ge(sems[k], v)
                        waited[k] = v
                    ins = o.fn(e)
                    if o.is_dma:
                        ins.then_inc(sems[o.dsem], 16)
                    elif o.signal:
                        ins.then_inc(sems[eng_name], 1)
                for k, v in final_dma.items():
                    if k[0] == eng_name and waited.get(k, 0) < v:
                        e.wait_ge(sems[k], v)
            return body

        for eng_name, deco in ((PE, block.tensor), (ACT, block.scalar), (DVE, block.vector),
                               (POOL, block.gpsimd), (SP, block.sync)):
            if any(o.eng == eng_name for o in self.ops):
                deco(make_body(eng_name))
        self.stack.close()
        return nc


def emit_consts(P):
    ones = P.sb("ones_bf", [128, 128], BF16)
    P.op(DVE, lambda e: e.memset(ones[:], 1.0), writes=["ones"])
    return ones


def emit_rmsnorm(P, gkey, xin, xkeys, KC, T, D, g_sb, out, okeys, ones, ps_ss, pskey, sq, rs1, rs2, eps=1e-6,
                 post=None):
    for kc in range(KC):
        P.op(ACT, lambda e, kc=kc: e.activation(out=sq[:, kc, 0:T], in_=xin(kc), func=AF.Square),
             reads=[xkeys[kc]], writes=[("nsq", kc)])

    def mm(e):
        ins = None
        for kc in range(KC):
            ins = e.matmul(ps_ss[:, 0:T], lhsT=ones[:], rhs=sq[:, kc, 0:T], start=(kc == 0), stop=(kc == KC - 1))
        return ins
    P.op(PE, mm, reads=[("nsq", kc) for kc in range(KC)] + ["ones"], writes=[pskey])
    P.op(ACT, lambda e: e.activation(out=rs1[:, 0:T], in_=ps_ss[:, 0:T], func=AF.Sqrt, scale=1.0 / D, bias=eps),
         reads=[pskey], writes=["rs1"])
    P.op(DVE, lambda e: e.reciprocal(out=rs2[:, 0:T], in_=rs1[:, 0:T]), reads=["rs1"], writes=["rs2"])
    for kc in range(KC):
        P.op(DVE, lambda e, kc=kc: e.scalar_tensor_tensor(out=out(kc), in0=xin(kc), scalar=g_sb[:, kc:kc + 1],
                                                          in1=rs2[:, 0:T], op0=ALU.mult, op1=ALU.mult),
             reads=[xkeys[kc], "rs2", gkey], writes=[okeys[kc]])
        if post is not None:
            post(kc)


TOK = 1024
DM = 2048
KC = 16
DFF = 5632
FG = 256
NG = DFF // FG
NFT = FG // 128


def build_ffn(final_norm=False):
    P = Prog()
    xT_d = P.din("xT", [DM, TOK])
    xh_d = P.din("xh", [DM, 2])
    wg_d = P.din("wg", [NG, 128, KC, FG])
    wv_d = P.din("wv", [NG, 128, KC, FG])
    wd_d = P.din("wd", [DFF, DM])
    cw_d = P.din("cw", [128, DFF // 128, 4])
    gf_d = P.din("gf", [128, KC])
    xo_d = P.dout("xo", [DM, TOK])
    if final_norm:
        gfin_d = P.din("gfin", [128, KC])
        fo_d = P.dout("fin", [DM, TOK])

    ones = emit_consts(P)
    xT = P.sb("xT_sb", [128, KC, TOK], F32)
    xh = P.sb("xh_sb", [128, KC, 2], F32)
    hT = P.sb("hT_sb", [128, KC, TOK], BF16)
    hh = P.sb("hh_sb", [128, KC, 2], BF16)
    gf = P.sb("gf_sb", [128, KC], F32)
    cw = P.sb("cw_sb", [128, DFF // 128, 4], F32)
    sq = P.sb("sq_sb", [128, KC, 512], BF16)
    rs1 = P.sb("rs1_sb", [128, 512], F32)
    rs2 = P.sb("rs2_sb", [128, 512], F32)
    wg = [P.sb("wg_sb%d" % i, [128, KC, FG], BF16) for i in range(2)]
    wv = [P.sb("wv_sb%d" % i, [128, KC, FG], BF16) for i in range(2)]
    wd = [P.sb("wd_sb%d" % i, [128, NFT, DM], BF16) for i in range(2)]
    gsb = [P.sb("g_sb%d" % i, [128, TOK + 2], F32) for i in range(2)]
    t1 = [P.sb("t1_sb%d" % i, [128, TOK], F32) for i in range(2)]
    t2 = [P.sb("t2_sb%d" % i, [128, TOK], F32) for i in range(2)]
    uT = [P.sb("uT_sb%d" % i, [128, NFT, TOK], BF16) for i in range(2)]
    pg = [P.ps("pg%d" % i, [128, 512]) for i in range(2)]
    pv = [P.ps("pv%d" % i, [128, 512]) for i in range(2)]
    pd = [P.ps("pd%d" % i, [128, 512]) for i in range(2)]
    ph = P.ps("ph", [128, 512])

    xT_v = xT_d.rearrange("(k p) t -> p k t", p=128)
    for kc in range(KC):
        P.dma(SP, xT[:, kc, :], xT_v[:, kc, :], writes=[("x", kc, 0), ("x", kc, 1)])
    P.dma(SP, xh[:, :, :], xh_d.rearrange("(k p) t -> p k t", p=128), writes=["xh"])
    P.dma(SP, gf[:, :], gf_d[:, :], writes=["gf"])
    P.dma(SP, cw[:, :, :], cw_d[:, :, :], writes=["cw"])

    def load_w(G):
        b = G % 2
        P.dma(POOL, wg[b][:, :, :], wg_d[G], writes=[("wg", b)])
        P.dma(POOL, wv[b][:, :, :], wv_d[G], writes=[("wv", b)])

    def load_wd(G):
        b = G % 2
        P.dma(POOL, wd[b][:, :, :], wd_d[G * FG:(G + 1) * FG, :].rearrange("(f p) n -> p f n", p=128),
              writes=[("wd", b)])

    load_w(0)
    load_wd(0)
    for tt in range(2):
        emit_rmsnorm(P, "gf", lambda kc, tt=tt: xT[:, kc, tt * 512:(tt + 1) * 512], [("x", kc, tt) for kc in range(KC)],
                     KC, 512, DM, gf, lambda kc, tt=tt: hT[:, kc, tt * 512:(tt + 1) * 512],
                     [("h", kc, tt) for kc in range(KC)], ones, ph, "ph", sq, rs1, rs2)
    emit_rmsnorm(P, "gf", lambda kc: xh[:, kc, :], ["xh"] * KC, KC, 2, DM, gf, lambda kc: hh[:, kc, :],
                 [("hh", kc) for kc in range(KC)], ones, ph, "ph", sq, rs1, rs2)

    hkeys = lambda tt: [("h", kc, tt) for kc in range(KC)]
    hhkeys = [("hh", kc) for kc in range(KC)]

    def down(G):
        b = G % 2
        for fo in range(KC):
            for tt in range(2):
                i = (fo * 2 + tt) % 2

                def mm(e, fo=fo, tt=tt, i=i, b=b):
                    ins = None
                    for ft in range(NFT):
                        ins = e.matmul(pd[i][:, :], lhsT=wd[b][:, ft, fo * 128:(fo + 1) * 128],
                                       rhs=uT[b][:, ft, tt * 512:(tt + 1) * 512], start=(ft == 0), stop=(ft == NFT - 1))
                    return ins
                P.op(PE, mm, reads=[("wd", b)] + [("u", b, ft, tt) for ft in range(NFT)], writes=[("pd", i)])
                P.op(DVE, lambda e, fo=fo, tt=tt, i=i: e.tensor_tensor(
                    out=xT[:, fo, tt * 512:(tt + 1) * 512], in0=pd[i][:, :], in1=xT[:, fo, tt * 512:(tt + 1) * 512],
                    op=ALU.add), reads=[("pd", i), ("x", fo, tt)], writes=[("x", fo, tt)])

    for G in range(NG):
        b = G % 2
        if G + 1 < NG:
            load_w(G + 1)
        for ft in range(NFT):
            fi = G * NFT + ft
            gb = fi % 2
            for tt in range(2):
                def mmg(e, tt=tt, ft=ft, b=b):
                    ins = None
                    for kc in range(KC):
                        ins = e.matmul(pg[tt][:, :], lhsT=wg[b][:, kc, ft * 128:(ft + 1) * 128],
                                       rhs=hT[:, kc, tt * 512:(tt + 1) * 512], start=(kc == 0), stop=(kc == KC - 1))
                    return ins
                P.op(PE, mmg, reads=[("wg", b)] + hkeys(tt), writes=[("pg", tt)])
                P.op(ACT, lambda e, tt=tt, gb=gb: e.activation(out=gsb[gb][:, 2 + tt * 512:2 + (tt + 1) * 512],
                                                               in_=pg[tt][:, :], func=AF.Copy),
                     reads=[("pg", tt)], writes=[("gsb", gb, tt)])

            def mmh(e, ft=ft, b=b):
                ins = None
                for kc in range(KC):
                    ins = e.matmul(ph[:, 0:2], lhsT=wg[b][:, kc, ft * 128:(ft + 1) * 128], rhs=hh[:, kc, :],
                                   start=(kc == 0), stop=(kc == KC - 1))
                return ins
            P.op(PE, mmh, reads=[("wg", b)] + hhkeys, writes=["ph"])
            P.op(ACT, lambda e, gb=gb: e.activation(out=gsb[gb][:, 0:2], in_=ph[:, 0:2], func=AF.Copy),
                 reads=["ph"], writes=[("gsb", gb, "h")])
            for tt in range(2):
                def mmv(e, tt=tt, ft=ft, b=b):
                    ins = None
                    for kc in range(KC):
                        ins = e.matmul(pv[tt][:, :], lhsT=wv[b][:, kc, ft * 128:(ft + 1) * 128],
                                       rhs=hT[:, kc, tt * 512:(tt + 1) * 512], start=(kc == 0), stop=(kc == KC - 1))
                    return ins
                P.op(PE, mmv, reads=[("wv", b)] + hkeys(tt), writes=[("pv", tt)])
            if ft == 0:
                if G > 0:
                    down(G - 1)
                if G + 1 < NG:
                    load_wd(G + 1)
            gk = [("gsb", gb, 0), ("gsb", gb, 1), ("gsb", gb, "h")]
            P.op(DVE, lambda e, gb=gb, fi=fi: e.tensor_scalar(out=t1[gb][:, :], in0=gsb[gb][:, 2:TOK + 2],
                                                              scalar1=cw[:, fi, 2:3], scalar2=cw[:, fi, 3:4],
                                                              op0=ALU.mult, op1=ALU.add),
                 reads=gk + ["cw"], writes=[("t1", gb)])
            P.op(DVE, lambda e, gb=gb, fi=fi: e.scalar_tensor_tensor(out=t2[gb][:, :], in0=gsb[gb][:, 1:TOK + 1],
                                                                     scalar=cw[:, fi, 1:2], in1=t1[gb][:, :],
                                                                     op0=ALU.mult, op1=ALU.add),
                 reads=gk + ["cw", ("t1", gb)], writes=[("t2", gb)])
            P.op(DVE, lambda e, gb=gb, fi=fi: e.scalar_tensor_tensor(out=t1[gb][:, :], in0=gsb[gb][:, 0:TOK],
                                                                     scalar=cw[:, fi, 0:1], in1=t2[gb][:, :],
                                                                     op0=ALU.mult, op1=ALU.add),
                 reads=gk + ["cw", ("t2", gb)], writes=[("t1", gb)])
            P.op(ACT, lambda e, gb=gb: e.activation(out=t2[gb][:, :], in_=t1[gb][:, :], func=AF.Silu),
                 reads=[("t1", gb)], writes=[("t2", gb)])
            for tt in range(2):
                P.op(DVE, lambda e, gb=gb, tt=tt, ft=ft, b=b: e.tensor_tensor(
                    out=uT[b][:, ft, tt * 512:(tt + 1) * 512], in0=pv[tt][:, :], in1=t2[gb][:, tt * 512:(tt + 1) * 512],
                    op=ALU.mult), reads=[("pv", tt), ("t2", gb)], writes=[("u", b, ft, tt)])
    down(NG - 1)

    xo_v = xo_d.rearrange("(k p) t -> p k t", p=128)
    for kc in range(KC):
        P.dma(SP, xo_v[:, kc, :], xT[:, kc, :], reads=[("x", kc, 0), ("x", kc, 1)])
    if final_norm:
        gfin = P.sb("gfin_sb", [128, KC], F32)
        P.dma(SP, gfin[:, :], gfin_d[:, :], writes=["gfin"])
        stg = [(t1[0], ("t1", 0)), (t1[1], ("t1", 1)), (t2[0], ("t2", 0)), (t2[1], ("t2", 1))]
        fo_v = fo_d.rearrange("(k p) t -> p k t", p=128)
        for tt in range(2):
            emit_rmsnorm(P, "gfin", lambda kc, tt=tt: xT[:, kc, tt * 512:(tt + 1) * 512],
                         [("x", kc, tt) for kc in range(KC)], KC, 512, DM, gfin, lambda kc: stg[kc % 4][0][:, 0:512],
                         [stg[kc % 4][1] for kc in range(KC)], ones, ph, "ph", sq, rs1, rs2, post=lambda kc, tt=tt: P.dma(
                             SP, fo_v[:, kc, tt * 512:(tt + 1) * 512], stg[kc % 4][0][:, 0:512], reads=[stg[kc % 4][1]]))
    return P.build()


def ffn_host_inputs(x_tok, w_gate, w_val, conv_w, conv_b, w_down, g_ffn, g_final=None):
    S = x_tok.shape[0]
    wg = np.ascontiguousarray(w_gate.reshape(KC, 128, NG, FG).transpose(2, 1, 0, 3))
    wv = np.ascontiguousarray(w_val.reshape(KC, 128, NG, FG).transpose(2, 1, 0, 3))
    cw = np.ascontiguousarray(np.concatenate([conv_w, conv_b[None, :]], axis=0).reshape(4, DFF // 128, 128).transpose(2, 1, 0))
    gf = np.ascontiguousarray(g_ffn.reshape(KC, 128).T)
    maps = []
    for c in range(NCORES):
        xs = x_tok[c * TOK:(c + 1) * TOK]
        halo = np.zeros((2, DM), np.float32)
        if c > 0:
            halo[:] = x_tok[c * TOK - 2:c * TOK]
        m = {"xT": np.ascontiguousarray(xs.T), "xh": np.ascontiguousarray(halo.T), "wg": wg, "wv": wv,
             "wd": w_down, "cw": cw, "gf": gf}
        if g_final is not None:
            m["gfin"] = np.ascontiguousarray(g_final.reshape(KC, 128).T)
        maps.append(m)
    return maps


TWO_PI = float(2.0 * np.pi)


def emit_rope_tables(P, NP, T, pos_ap, poskey, inv, nsgn, negpi, posf, ang, r1, r2, cos2, sinS, kpre, ki):
    P.op(DVE, lambda e: e.tensor_copy(out=posf[0:NP, 0:T], in_=pos_ap), reads=[poskey], writes=[(kpre, "posf")])
    P.op(DVE, lambda e: e.tensor_scalar(out=ang[0:NP, 0:T], in0=posf[0:NP, 0:T], scalar1=inv[0:NP, 0:1], scalar2=None,
                                        op0=ALU.mult), reads=[(kpre, "posf"), "ropec"], writes=[(kpre, "ang")])
    for which, (rr, dst) in enumerate(((r1, sinS), (r2, cos2))):
        rk = (kpre, "r", which)
        if which == 1:
            P.op(DVE, lambda e: e.tensor_scalar(out=ang[0:NP, 0:T], in0=ang[0:NP, 0:T], scalar1=0.25, scalar2=None,
                                                op0=ALU.add), reads=[(kpre, "ang")], writes=[(kpre, "ang")])
        P.op(DVE, lambda e: e.tensor_copy(out=ki[0:NP, 0:T], in_=ang[0:NP, 0:T]), reads=[(kpre, "ang")], writes=[(kpre, "ki")])
        P.op(DVE, lambda e, rr=rr: e.tensor_copy(out=rr[0:NP, 0:T], in_=ki[0:NP, 0:T]), reads=[(kpre, "ki")], writes=[rk])
        P.op(DVE, lambda e, rr=rr: e.tensor_tensor(out=rr[0:NP, 0:T], in0=ang[0:NP, 0:T], in1=rr[0:NP, 0:T],
                                                   op=ALU.subtract), reads=[(kpre, "ang"), rk], writes=[rk])
        P.op(DVE, lambda e, rr=rr: e.scalar_tensor_tensor(out=rr[0:NP, 0:T], in0=rr[0:NP, 0:T], scalar=0.5,
                                                          in1=rr[0:NP, 0:T], op0=ALU.is_ge, op1=ALU.subtract),
             reads=[rk], writes=[rk])
        if which == 0:
            P.op(ACT, lambda e, rr=rr, dst=dst: e.activation(out=dst[0:NP, 0:T], in_=rr[0:NP, 0:T], func=AF.Sin,
                                                             scale=nsgn[0:NP, 0:1]),
                 reads=[rk, "ropec"], writes=[(kpre, "sinS")])
        else:
            P.op(ACT, lambda e, rr=rr, dst=dst: e.activation(out=dst[0:NP, 0:T], in_=rr[0:NP, 0:T], func=AF.Sin,
                                                             scale=-TWO_PI),
                 reads=[rk], writes=[(kpre, "cos2")])


def emit_rope_apply(P, NP, T, src, srckey, cos2, sinS, kpre, qf, qsw, ta, tb, tkey, out_ap, outkeys, in_view=None,
                    eng2=POOL):
    H = NP // 2
    P.op(ACT, lambda e: e.activation(out=qsw[0:H, 0:T], in_=src(H, NP), func=AF.Copy),
         reads=[srckey], writes=[(tkey, "qsw0")])
    P.op(ACT, lambda e: e.activation(out=qsw[H:NP, 0:T], in_=src(0, H), func=AF.Copy),
         reads=[srckey], writes=[(tkey, "qsw1")])
    P.op(DVE, lambda e: e.tensor_tensor(out=ta[0:NP, 0:T], in0=src(0, NP), in1=cos2[0:NP, 0:T], op=ALU.mult),
         reads=[(kpre, "cos2")], writes=[(tkey, "a"), srckey])
    P.op(eng2, lambda e: e.tensor_tensor(out=tb[0:NP, 0:T], in0=qsw[0:NP, 0:T], in1=sinS[0:NP, 0:T], op=ALU.mult),
         reads=[(tkey, "qsw0"), (tkey, "qsw1"), (kpre, "sinS")], writes=[(tkey, "b")])
    v = in_view if in_view is not None else (lambda a: a)
    P.op(DVE, lambda e: e.tensor_tensor(out=out_ap, in0=v(ta[0:NP, 0:T]), in1=v(tb[0:NP, 0:T]), op=ALU.add),
         reads=[(tkey, "a"), (tkey, "b")], writes=outkeys)


def rope_consts_host(NP, half):
    j = np.arange(NP) % half
    inv = (np.float32(10000.0) ** (-(j.astype(np.float32)) / np.float32(half))).astype(np.float32)
    sgn = np.where(np.arange(NP) % (2 * half) < half, -1.0, 1.0).astype(np.float32)
    c = np.zeros((128, 4), np.float32)
    c[:NP, 0] = (inv.astype(np.float64) / (2.0 * np.pi)).astype(np.float32)
    c[:NP, 1] = (-2.0 * np.pi * sgn).astype(np.float32)
    return c


MEMT = 256
MH = 4


def emit_mem_kv(P, memT_d, wmk_d, ones, wbuf, wkeyfn, pj, pjkeys):
    memT = P.sb("memT_sb", [128, KC, MEMT], BF16)
    mkT = P.sb("mkT_sb", [128, MH, MEMT], BF16)
    mv = P.sb("mv_sb", [128, 2, MH * 128], BF16)
    P.dma(POOL, memT[:, :, :], memT_d[:, :, :], writes=["memT"])
    for part in range(2):
        P.dma(POOL, wbuf[:, :, :], wmk_d[:, :, part * 512:(part + 1) * 512], writes=[wkeyfn()])
        if part == 0:
            for h in range(MH):
                i = h % 2

                def mm(e, h=h, i=i):
                    ins = None
                    for kc in range(KC):
                        ins = e.matmul(pj[i][:, 0:MEMT], lhsT=wbuf[:, kc, h * 128:(h + 1) * 128], rhs=memT[:, kc, :],
                                       start=(kc == 0), stop=(kc == KC - 1))
                    return ins
                P.op(PE, mm, reads=[wkeyfn(), "memT"], writes=[pjkeys[i]])
                P.op(ACT, lambda e, h=h, i=i: e.activation(out=mkT[:, h, :], in_=pj[i][:, 0:MEMT], func=AF.Copy),
                     reads=[pjkeys[i]], writes=[("mkT", h)])
        else:
            for mt in range(2):
                i = mt % 2

                def mm(e, mt=mt, i=i):
                    ins = None
                    for kc in range(KC):
                        ins = e.matmul(pj[i][:, :], lhsT=memT[:, kc, mt * 128:(mt + 1) * 128], rhs=wbuf[:, kc, :],
                                       start=(kc == 0), stop=(kc == KC - 1))
                    return ins
                P.op(PE, mm, reads=[wkeyfn(), "memT"], writes=[pjkeys[i]])
                P.op(ACT, lambda e, mt=mt, i=i: e.activation(out=mv[:, mt, :], in_=pj[i][:, :], func=AF.Copy),
                     reads=[pjkeys[i]], writes=[("mv", mt)])
    return mkT, mv


def emit_mem_attn(P, qmT, qmkeyfn, mkT, mv, ones, headsT, hbase, pS, pO, pL, PT, rinv, pOkey="pO", PTkey="PTm"):
    scale = 128.0 ** -0.5
    for h in range(MH):
        for tt in range(TOK // 512):
            sl = slice(tt * 512, (tt + 1) * 512)
            for mt in range(2):
                P.op(PE, lambda e, h=h, mt=mt, sl=sl: e.matmul(pS[mt][:, :], lhsT=mkT[:, h, mt * 128:(mt + 1) * 128],
                                                              rhs=qmT[:, h, sl], start=True, stop=True),
                     reads=[("mkT", h), qmkeyfn(h, tt)], writes=[("pS", mt)])
                P.op(ACT, lambda e, mt=mt: e.activation(out=PT[:, mt, :], in_=pS[mt][:, :], func=AF.Exp, scale=scale),
                     reads=[("pS", mt)], writes=[(PTkey, mt)])

            def mmo(e, h=h):
                ins = None
                for mt in range(2):
                    ins = e.matmul(pO[:, :], lhsT=mv[:, mt, h * 128:(h + 1) * 128], rhs=PT[:, mt, :],
                                   start=(mt == 0), stop=(mt == 1))
                return ins
            P.op(PE, mmo, reads=[("mv", 0), ("mv", 1), (PTkey, 0), (PTkey, 1)], writes=[pOkey])

            def mml(e):
                ins = None
                for mt in range(2):
                    ins = e.matmul(pL[:, :], lhsT=ones[:], rhs=PT[:, mt, :], start=(mt == 0), stop=(mt == 1))
                return ins
            P.op(PE, mml, reads=["ones", (PTkey, 0), (PTkey, 1)], writes=["pL"])
            P.op(DVE, lambda e: e.reciprocal(out=rinv[:, :], in_=pL[:, :]), reads=["pL"], writes=["rinv"])
            P.op(DVE, lambda e, h=h, sl=sl: e.tensor_tensor(out=headsT[:, hbase + h, sl], in0=pO[:, :], in1=rinv[:, :],
                                                            op=ALU.mult),
                 reads=[pOkey, "rinv"], writes=[("heads", hbase + h, tt)])


def emit_out_proj(P, headsT, NHC, wo_d, wo_bufs, xres, xkeyfn, pj, pjkeys):
    cnt = 0
    for fq in range(DM // 512):
        b = fq % 2
        P.dma(POOL, wo_bufs[b][:, :, :], wo_d[:, :, fq * 512:(fq + 1) * 512], writes=[("wo", b)])
        for f4 in range(4):
            fo = fq * 4 + f4
            for tt in range(TOK // 512):
                i = cnt % 2
                cnt += 1
                sl = slice(tt * 512, (tt + 1) * 512)

                def mm(e, f4=f4, sl=sl, i=i, b=b):
                    ins = None
                    for kc in range(NHC):
                        ins = e.matmul(pj[i][:, :], lhsT=wo_bufs[b][:, kc, f4 * 128:(f4 + 1) * 128], rhs=headsT[:, kc, sl],
                                       start=(kc == 0), stop=(kc == NHC - 1))
                    return ins
                P.op(PE, mm, reads=[("wo", b)] + [("heads", kc, tt) for kc in range(NHC)], writes=[pjkeys[i]])
                P.op(DVE, lambda e, fo=fo, sl=sl, i=i: e.tensor_tensor(out=xres[:, fo, sl], in0=pj[i][:, :],
                                                                       in1=xres[:, fo, sl], op=ALU.add),
                     reads=[pjkeys[i], xkeyfn(fo, tt)], writes=[xkeyfn(fo, tt)])


def build_a2():
    P = Prog()
    NHC = 12
    xT_d = P.din("xT", [DM, TOK])
    att_d = P.din("attT", [8 * 128, TOK], BF16)
    wq_d = P.din("wq", [128, KC, 512])
    wmk_d = P.din("wmk", [128, KC, 1024])
    memT_d = P.din("memT", [128, KC, MEMT])
    wo_d = P.din("wo", [128, NHC, DM])
    ga_d = P.din("ga", [128, KC])
    xo_d = P.dout("xo", [DM, TOK])

    ones = emit_consts(P)
    xT = P.sb("xT_sb", [128, KC, TOK], F32)
    hT = P.sb("hT_sb", [128, KC, TOK], BF16)
    ga = P.sb("ga_sb", [128, KC], F32)
    sq = P.sb("sq_sb", [128, KC, 512], BF16)
    rs1 = P.sb("rs1_sb", [128, 512], F32)
    rs2 = P.sb("rs2_sb", [128, 512], F32)
    wbuf = P.sb("wbuf_sb", [128, KC, 512], BF16)
    qmT = P.sb("qmT_sb", [128, MH, TOK], BF16)
    headsT = P.sb("headsT_sb", [128, NHC, TOK], BF16)
    wo_bufs = [P.sb("wo_sb%d" % i, [128, NHC, 512], BF16) for i in range(2)]
    PT = P.sb("PT_sb", [128, 2, 512], BF16)
    rinv = P.sb("rinv_sb", [128, 512], F32)
    pj = [P.ps("pj%d" % i, [128, 512]) for i in range(2)]
    pS = [P.ps("pS%d" % i, [128, 512]) for i in range(2)]
    pO = P.ps("pO", [128, 512])
    pL = P.ps("pL", [128, 512])
    ph = P.ps("ph", [128, 512])
    pjkeys = [("pj", 0), ("pj", 1)]
    wver = [0]

    xT_v = xT_d.rearrange("(k p) t -> p k t", p=128)
    for kc in range(KC):
        P.dma(SP, xT[:, kc, :], xT_v[:, kc, :], writes=[("x", kc, 0), ("x", kc, 1)])
    P.dma(SP, ga[:, :], ga_d[:, :], writes=["ga"])
    att_v = att_d.rearrange("(k p) t -> p k t", p=128)
    for kc in range(8):
        P.dma(SP, headsT[:, kc, :], att_v[:, kc, :], writes=[("heads", kc, 0), ("heads", kc, 1)])
    P.dma(POOL, wbuf[:, :, :], wq_d[:, :, :], writes=["wbuf"])
    for tt in range(2):
        emit_rmsnorm(P, "ga", lambda kc, tt=tt: xT[:, kc, tt * 512:(tt + 1) * 512], [("x", kc, tt) for kc in range(KC)],
                     KC, 512, DM, ga, lambda kc, tt=tt: hT[:, kc, tt * 512:(tt + 1) * 512],
                     [("h", kc, tt) for kc in range(KC)], ones, ph, "ph", sq, rs1, rs2)
    cnt = 0
    for h in range(MH):
        for tt in range(2):
            i = cnt % 2
            cnt += 1
            sl = slice(tt * 512, (tt + 1) * 512)

            def mm(e, h=h, sl=sl, i=i):
                ins = None
                for kc in range(KC):
                    ins = e.matmul(pj[i][:, :], lhsT=wbuf[:, kc, h * 128:(h + 1) * 128], rhs=hT[:, kc, sl],
                                   start=(kc == 0), stop=(kc == KC - 1))
                return ins
            P.op(PE, mm, reads=["wbuf"] + [("h", kc, tt) for kc in range(KC)], writes=[pjkeys[i]])
            P.op(ACT, lambda e, h=h, sl=sl, i=i: e.activation(out=qmT[:, h, sl], in_=pj[i][:, :], func=AF.Copy),
                 reads=[pjkeys[i]], writes=[("qm", h, tt)])
    mkT, mv = emit_mem_kv(P, memT_d, wmk_d, ones, wbuf, lambda: "wbuf", pj, pjkeys)
    emit_mem_attn(P, qmT, lambda h, tt: ("qm", h, tt), mkT, mv, ones, headsT, 8, pS, pO, pL, PT, rinv)
    emit_out_proj(P, headsT, NHC, wo_d, wo_bufs, xT, lambda fo, tt: ("x", fo, tt), pj, pjkeys)
    xo_v = xo_d.rearrange("(k p) t -> p k t", p=128)
    for kc in range(KC):
        P.dma(SP, xo_v[:, kc, :], xT[:, kc, :], reads=[("x", kc, 0), ("x", kc, 1)])
    return P.build()


def pkn(w):
    K_ = w.shape[0] // 128
    return np.ascontiguousarray(w.reshape(K_, 128, w.shape[1]).transpose(1, 0, 2))


def gvec(g):
    return np.ascontiguousarray(g.reshape(-1, 128).T)


def a2_host_inputs(x_tok, att_tok_bf16, w_in, w_mem_kv, mem, w_out, g_attn):
    wq = pkn(w_in[:, 9216:9728])
    wmk = pkn(w_mem_kv)
    memT = pkn(np.ascontiguousarray(mem.T))
    wo = pkn(w_out)
    ga = gvec(g_attn)
    maps = []
    for c in range(NCORES):
        sl = slice(c * TOK, (c + 1) * TOK)
        maps.append({"xT": np.ascontiguousarray(x_tok[sl].T), "attT": np.ascontiguousarray(att_tok_bf16[sl].T),
                     "wq": wq, "wmk": wmk, "memT": memT, "wo": wo, "ga": ga})
    return maps


SEQ = 8192
DIL = (1, 4, 16)
A1_TT = 256
A1_SPAN = 2048


def build_a1():
    P = Prog()
    TT = A1_TT
    TPS = A1_SPAN // TT
    NSP = SEQ // A1_SPAN
    xT_d = P.din("xT", [SEQ // A1_TT, 128, KC, A1_TT])
    w_d = P.din("w", [128, KC, 1152])
    ga_d = P.din("ga", [128, KC])
    posb_d = P.din("posb", [128, SEQ], I32)
    rc_d = P.din("rc", [128, 4])
    mk_d = P.din("mk", [128, 2, 2, 128])
    id_d = P.din("ident", [128, 128])
    att_d = P.dout("attT", [128, SEQ], BF16)

    ones = emit_consts(P)
    w_sb = P.sb("w_sb", [128, KC, 1152], BF16)
    ga = P.sb("ga_sb", [128, KC], F32)
    rc = P.sb("rc_sb", [128, 4], F32)
    mk = P.sb("mk_sb", [128, 2, 2, 128], BF16)
    ident = P.sb("ident_sb", [128, 128], BF16)
    x_sb = [P.sb("x_sb%d" % i, [128, KC, TT], F32) for i in range(2)]
    sq = P.sb("sq_sb", [128, KC, TT], BF16)
    hT = [P.sb("hT_sb%d" % i, [128, KC, TT], BF16) for i in range(2)]
    rs1 = P.sb("rs1_sb", [128, TT], F32)
    rs2 = P.sb("rs2_sb", [128, TT], F32)
    posi = P.sb("posi_sb", [128, TT], I32)
    posf = P.sb("posf_sb", [128, TT], F32)
    kint = P.sb("kint_sb", [128, TT], I32)
    ang = P.sb("ang_sb", [128, TT], F32)
    r1 = P.sb("r1_sb", [128, TT], F32)
    r2 = P.sb("r2_sb", [128, TT], F32)
    cos2 = [P.sb("cos2_sb%d" % i, [128, TT], F32) for i in range(2)]
    sinS = [P.sb("sinS_sb%d" % i, [128, TT], F32) for i in range(2)]
    ta = [P.sb("ta_sb%d" % i, [128, TT], F32) for i in range(2)]
    tb = [P.sb("tb_sb%d" % i, [128, TT], F32) for i in range(2)]
    qkf = [P.sb("qkf_sb%d" % i, [128, TT], F32) for i in range(2)]
    qsw = [P.sb("qsw_sb%d" % i, [128, TT], F32) for i in range(2)]
    QT = [P.sb("QT_sb%d" % g, [128, A1_SPAN], BF16) for g in range(3)]
    KT = [[P.sb("KT_sb%d_%d" % (p, g), [128, A1_SPAN], BF16) for g in range(3)] for p in range(2)]
    VT = [P.sb("VT_sb%d" % g, [128, A1_SPAN], BF16) for g in range(3)]
    V = [P.sb("V_sb%d" % p, [128, 3, 16, 128], BF16) for p in range(2)]
    acc = P.sb("acc_sb", [128, A1_SPAN], F32)
    lacc = P.sb("lacc_sb", [128, A1_SPAN], F32)
    ob = P.sb("ob_sb", [128, A1_SPAN], BF16)
    PT = [P.sb("PT_sb%d" % i, [128, 2, 128], BF16) for i in range(2)]
    ph = P.ps("ph", [128, 512])
    pjb = [P.ps("pjb%d" % i, [128, 512]) for i in range(2)]
    pSb = [P.ps("pSb%d" % i, [128, 4, 128]) for i in range(2)]
    pOL = [P.ps("pOL%d" % i, [128, 4, 128]) for i in range(2)]
    pTb = P.ps("pTb", [128, 8, 128], BF16)
    pj = [pjb[0][:, 0:256], pjb[1][:, 0:256], pSb[0][:, 0:2, :].rearrange("p a b -> p (a b)"),
          pSb[1][:, 0:2, :].rearrange("p a b -> p (a b)")]
    pjkeys = [("pj", 0), ("pj", 1), ("pS", 0), ("pS", 1)]
    pT = [pTb[:, 0, :], ph[:, :].bitcast(BF16)[:, 0:128]]
    pTk = ["pT", "ph"]

    P.dma(POOL, w_sb[:, :, :], w_d[:, :, :], writes=["w"])
    P.dma(SP, ga[:, :], ga_d[:, :], writes=["ga"])
    P.dma(SP, rc[:, :], rc_d[:, :], writes=["ropec"])
    P.dma(POOL, mk[:, :, :, :], mk_d[:, :, :, :], writes=["mk"])
    P.dma(POOL, ident[:, :], id_d[:, :], writes=["ident"])
    for g in range(3):
        P.op(DVE, lambda e, g=g: e.memset(KT[1][g][:, :], 0.0), writes=[("KT", 1, g)])
    P.op(DVE, lambda e: e.memset(V[1][:, :, :, :], 0.0), writes=[("V", 1, g, b) for g in range(3) for b in range(16)])

    NT = SEQ // TT

    def load(t):
        P.dma(SP, x_sb[t % 2][:, :, :], xT_d[t], writes=[("x", t % 2)])

    def norm(t):
        b = t % 2
        emit_rmsnorm(P, "ga", lambda kc: x_sb[b][:, kc, :], [("x", b)] * KC, KC, TT, DM, ga,
                     lambda kc: hT[b][:, kc, :], [("h", b, kc) for kc in range(KC)], ones, ph, "ph", sq, rs1, rs2)
        P.dma(SP, posi[:, :], posb_d[:, t * TT:(t + 1) * TT], writes=["posi"])
        emit_rope_tables(P, 128, TT, posi[:, :], "posi", rc[:, 0:1], rc[:, 1:2], rc[:, 2:3], posf, ang, r1, r2,
                         cos2[b], sinS[b], ("rt", b), kint)

    pjc = [0]

    def proj(t):
        b = t % 2
        n = t // TPS
        i = t % TPS
        par = n % 2
        hk = [("h", b, kc) for kc in range(KC)]
        for g in range(3):
            d = DIL[g]
            for tq in range(3):
                col = (g * 3 + tq) * 128
                s = pjc[0] % 4
                s2 = pjc[0] % 2
                pjc[0] += 1

                def mm(e, col=col, s=s, b=b):
                    ins = None
                    for kc in range(KC):
                        ins = e.matmul(pj[s], lhsT=w_sb[:, kc, col:col + 128], rhs=hT[b][:, kc, :],
                                       start=(kc == 0), stop=(kc == KC - 1))
                    return ins
                P.op(PE, mm, reads=["w"] + hk, writes=[pjkeys[s]])
                if tq == 2:
                    P.op(ACT, lambda e, g=g, i=i, s=s: e.activation(out=VT[g][:, i * TT:(i + 1) * TT], in_=pj[s],
                                                                    func=AF.Copy),
                         reads=[pjkeys[s]], writes=[("VT", g, i)])
                    continue
                dst = QT[g] if tq == 0 else KT[par][g]
                dkey = ("QT", g, i) if tq == 0 else ("KT", par, g, i)
                if d == 1:
                    out_ap = dst[:, i * TT:(i + 1) * TT]
                    view = None
                else:
                    a = TT // d
                    out_ap = dst[:, :].rearrange("p (r a) -> p r a", r=d)[:, :, a * i:a * (i + 1)]
                    view = (lambda ap, d=d: ap.rearrange("p (a r) -> p r a", r=d))
                emit_rope_apply(P, 128, TT, lambda lo, hi, s=s: pj[s][lo:hi, :], pjkeys[s], cos2[b], sinS[b],
                                ("rt", b), qkf[s2], qsw[s2], ta[s2], tb[s2], ("rtmp", s2), out_ap, [dkey], in_view=view,
                                eng2=POOL)

    sc = [0]
    scale = 128.0 ** -0.5

    def attention(n):
        par = n % 2
        tcount = 0
        for g in range(3):
            d = DIL[g]
            nb = 16 // d
            for r in range(d):
                for m in range(nb):
                    blk = r * nb + m
                    s = tcount % 2
                    tcount += 1
                    st = r + d * 128 * m
                    P.op(PE, lambda e, g=g, s=s, st=st, d=d: e.transpose(out=pT[s],
                                                                         in_=VT[g][:, st:st + d * 127 + 1:d],
                                                                         identity=ident[:, :]),
                         reads=[("VT", g, i) for i in range(TPS)] + ["ident"], writes=[pTk[s]])
                    P.op(ACT, lambda e, g=g, s=s, blk=blk, par=par: e.activation(out=V[par][:, g, blk, :],
                                                                                 in_=pT[s], func=AF.Copy),
                         reads=[pTk[s]], writes=[("V", par, g, blk)])
        items = []
        for g in range(3):
            d = DIL[g]
            nb = 16 // d
            for r in range(d):
                for m in range(nb):
                    items.append((g, r, m))

        def stage1(it):
            g, r, m = it
            d = DIL[g]
            nb = 16 // d
            L = A1_SPAN // d
            kview = lambda p_: KT[p_][g][:, :].rearrange("p (r a) -> p r a", r=d)
            qview = QT[g][:, :].rearrange("p (r a) -> p r a", r=d)
            s = sc[0] % 2
            sc[0] += 1
            kcur = kview(par)[:, r, 128 * m:128 * (m + 1)]
            if m > 0:
                kprev = kview(par)[:, r, 128 * (m - 1):128 * m]
                kpk = []
            else:
                kprev = kview(1 - par)[:, r, L - 128:L]
                kpk = [("KT", 1 - par, g, i) for i in range(TPS)] if n > 0 else [("KT", 1, g)]
            q = qview[:, r, 128 * m:128 * (m + 1)]
            variant = 1 if (n == 0 and m == 0) else 0

            def mms(e):
                e.matmul(pSb[s][:, 0, :], lhsT=kprev, rhs=q, start=True, stop=True)
                return e.matmul(pSb[s][:, 1, :], lhsT=kcur, rhs=q, start=True, stop=True)
            P.op(PE, mms, reads=kpk + [("KT", par, g, i) for i in range(TPS)] + [("QT", g, i) for i in range(TPS)],
                 writes=[("pS", s)])
            P.op(ACT, lambda e: e.activation(out=PT[s][:, :, :], in_=pSb[s][:, 0:2, :], func=AF.Exp, scale=scale),
                 reads=[("pS", s)], writes=[("PT", s)])
            P.op(DVE, lambda e: e.tensor_tensor(out=PT[s][:, :, :], in0=PT[s][:, :, :], in1=mk[:, variant, :, :],
                                                op=ALU.mult), reads=[("PT", s), "mk"], writes=[("PT", s)])
            return s

        def stage2(it, s):
            g, r, m = it
            d = DIL[g]
            nb = 16 // d
            blk = r * nb + m
            if m > 0:
                vprev = V[par][:, g, blk - 1, :]
                vpk = ("V", par, g, blk - 1)
            else:
                vprev = V[1 - par][:, g, r * nb + nb - 1, :]
                vpk = ("V", 1 - par, g, r * nb + nb - 1)

            def mmo(e):
                e.matmul(pOL[s][:, 0, :], lhsT=vprev, rhs=PT[s][:, 0, :], start=True, stop=False)
                e.matmul(pOL[s][:, 0, :], lhsT=V[par][:, g, blk, :], rhs=PT[s][:, 1, :], start=False, stop=True)
                e.matmul(pOL[s][:, 1, :], lhsT=ones[:], rhs=PT[s][:, 0, :], start=True, stop=False)
                return e.matmul(pOL[s][:, 1, :], lhsT=ones[:], rhs=PT[s][:, 1, :], start=False, stop=True)
            P.op(PE, mmo, reads=[("PT", s), vpk, ("V", par, g, blk), "ones"], writes=[("pOL", s)])
            st = r + d * 128 * m
            cols = slice(st, st + d * 127 + 1, d)
            if g == 0:
                pbs = [m]
            elif g == 1:
                pbs = [4 * m + j for j in range(4)]
            else:
                pbs = list(range(16))
            akeys = [("acc", pb) for pb in pbs]
            lkeys = [("lacc", pb) for pb in pbs]
            if g == 0:
                P.op(ACT, lambda e: e.activation(out=acc[:, cols], in_=pOL[s][:, 0, :], func=AF.Copy),
                     reads=[("pOL", s)], writes=akeys)
                P.op(ACT, lambda e: e.activation(out=lacc[:, cols], in_=pOL[s][:, 1, :], func=AF.Copy),
                     reads=[("pOL", s)], writes=lkeys)
            else:
                P.op(DVE, lambda e: e.tensor_tensor(out=acc[:, cols], in0=pOL[s][:, 0, :], in1=acc[:, cols], op=ALU.add),
                     reads=[("pOL", s)] + akeys, writes=akeys)
                P.op(DVE, lambda e: e.tensor_tensor(out=lacc[:, cols], in0=pOL[s][:, 1, :], in1=lacc[:, cols],
                                                    op=ALU.add), reads=[("pOL", s)] + lkeys, writes=lkeys)

        prev = None
        for i in range(len(items) + 1):
            cur = None
            if i < len(items):
                cur = (items[i], stage1(items[i]))
            if prev is not None:
                stage2(*prev)
            prev = cur
        allk = [("acc", pb) for pb in range(16)]
        alll = [("lacc", pb) for pb in range(16)]
        P.op(DVE, lambda e: e.reciprocal(out=lacc[:, :], in_=lacc[:, :]), reads=alll, writes=alll)
        P.op(DVE, lambda e: e.tensor_tensor(out=ob[:, :], in0=acc[:, :], in1=lacc[:, :], op=ALU.mult),
             reads=allk + alll, writes=["ob"])
        P.dma(SP, att_d[:, n * A1_SPAN:(n + 1) * A1_SPAN], ob[:, :], reads=["ob"])

    load(0)
    load(1)
    norm(0)
    for t in range(NT):
        if t + 1 < NT:
            norm(t + 1)
        if t + 2 < NT:
            load(t + 2)
        proj(t)
        if t % TPS == TPS - 1:
            attention(t // TPS)
    return P.build()


def a1_host_inputs(x_tok, positions, w_in, g_attn):
    xT = np.ascontiguousarray(x_tok.reshape(SEQ // A1_TT, A1_TT, KC, 128).transpose(0, 3, 2, 1))
    posb = np.ascontiguousarray(np.broadcast_to(positions.astype(np.int32)[None, :], (128, SEQ)))
    rc = rope_consts_host(128, 64)
    k = np.arange(128)[:, None]
    q = np.arange(128)[None, :]
    mk = np.zeros((128, 2, 2, 128), np.float32)
    mk[:, 0, 0, :] = (k >= q)
    mk[:, 0, 1, :] = (k <= q)
    mk[:, 1, 1, :] = (k <= q)
    ident = np.eye(128, dtype=np.float32)
    ga = gvec(g_attn)
    w4 = w_in[:, :9216].reshape(DM, 3, 3, 8, 128)
    maps = []
    for c in range(NCORES):
        w = pkn(np.ascontiguousarray(w4[:, :, :, c, :]).reshape(DM, 1152))
        maps.append({"xT": xT, "w": w, "ga": ga, "posb": posb, "rc": rc, "mk": mk, "ident": ident})
    return maps


NHB = 12


def build_proj(kind):
    P = Prog()
    xT_d = P.din("xT", [DM, TOK])
    gx_d = P.din("gx", [128, KC])
    gl_d = P.din("gl", [128, 4])
    posb_d = P.din("posb", [128, TOK], I32)
    rc_d = P.din("rc", [128, 4])
    if kind == "kv":
        w1_d = P.din("w1", [128, KC, 576])
        wuk_d = P.din("wuk", [128, 4, 1536])
        wuv_d = P.din("wuv", [128, 4, 1536])
        kn_d = P.dout("knT", [NHB, 128, TOK], BF16)
        kr_d = P.dout("krT", [64, TOK], BF16)
        v_d = P.dout("v", [TOK // 128, 128, 1536], BF16)
    else:
        w1_d = P.din("w1", [128, KC, 1024])
        wuq_d = P.din("wuq", [128, 4, NHB * 192])
        qn_d = P.dout("qnT", [NHB, 128, TOK], BF16)
        qr_d = P.dout("qrT", [NHB, 64, TOK], BF16)
        qm_d = P.dout("qmT", [MH, 128, TOK], BF16)

    ones = emit_consts(P)
    xT = P.sb("xT_sb", [128, KC, TOK], F32)
    hT = P.sb("hT_sb", [128, KC, TOK], BF16)
    gx = P.sb("gx_sb", [128, KC], F32)
    gl = P.sb("gl_sb", [128, 4], F32)
    rc = P.sb("rc_sb", [128, 4], F32)
    sq = P.sb("sq_sb", [128, KC, 512], BF16)
    rs1 = P.sb("rs1_sb", [128, 512], F32)
    rs2 = P.sb("rs2_sb", [128, 512], F32)
    wbuf = P.sb("wbuf_sb", [128, KC, 576], BF16)
    cl = P.sb("cl_sb", [128, 4, TOK], F32)
    cn = P.sb("cn_sb", [128, 4, TOK], BF16)
    wu = P.sb("wu_sb", [128, 4, NHB * 192], BF16)
    posi = P.sb("posi_sb", [128, 512], I32)
    posf = P.sb("posf_sb", [128, 512], F32)
    kint = P.sb("kint_sb", [128, 512], I32)
    ang = P.sb("ang_sb", [128, 512], F32)
    r1 = P.sb("r1_sb", [128, 512], F32)
    r2 = P.sb("r2_sb", [128, 512], F32)
    cos2 = [P.sb("cos2_sb%d" % i, [128, 512], F32) for i in range(2)]
    sinS = [P.sb("sinS_sb%d" % i, [128, 512], F32) for i in range(2)]
    qf = P.sb("qf_sb", [128, 512], F32)
    qsw = P.sb("qsw_sb", [128, 512], F32)
    ta = P.sb("ta_sb", [128, 512], F32)
    tb = P.sb("tb_sb", [128, 512], F32)
    stg = [P.sb("stg_sb%d" % i, [128, 512], BF16) for i in range(3)]
    pj = [P.ps("pj%d" % i, [128, 512]) for i in range(3)]
    ph = P.ps("ph", [128, 512])
    pjk = [("pj", i) for i in range(3)]
    cnt = [0]
    scnt = [0]

    xT_v = xT_d.rearrange("(k p) t -> p k t", p=128)
    for kc in range(KC):
        P.dma(SP, xT[:, kc, :], xT_v[:, kc, :], writes=[("x", kc, 0), ("x", kc, 1)])
    P.dma(SP, gx[:, :], gx_d[:, :], writes=["gx"])
    P.dma(SP, gl[:, :], gl_d[:, :], writes=["gl"])
    P.dma(SP, rc[:, :], rc_d[:, :], writes=["ropec"])
    NW1 = 576 if kind == "kv" else 512
    P.dma(POOL, wbuf[:, :, 0:NW1], w1_d[:, :, 0:NW1], writes=["wbuf"])
    if kind == "kv":
        P.dma(POOL, wu[:, :, 0:1536], wuk_d[:, :, :], writes=["wu"])
    else:
        P.dma(POOL, wu[:, :, :], wuq_d[:, :, :], writes=["wu"])
    for tt in range(2):
        emit_rmsnorm(P, "gx", lambda kc, tt=tt: xT[:, kc, tt * 512:(tt + 1) * 512], [("x", kc, tt) for kc in range(KC)],
                     KC, 512, DM, gx, lambda kc, tt=tt: hT[:, kc, tt * 512:(tt + 1) * 512],
                     [("h", kc, tt) for kc in range(KC)], ones, ph, "ph", sq, rs1, rs2)
    for tt in range(2):
        P.dma(SP, posi[:, :], posb_d[:, tt * 512:(tt + 1) * 512], writes=["posi"])
        emit_rope_tables(P, 64, 512, posi[0:64, :], "posi", rc[:, 0:1], rc[:, 1:2], rc[:, 2:3], posf, ang, r1, r2,
                         cos2[tt], sinS[tt], ("rt", tt), kint)

    def hk(tt):
        return [("h", kc, tt) for kc in range(KC)]

    def mm_in(col, M, tt, s):
        def mm(e):
            ins = None
            for kc in range(KC):
                ins = e.matmul(pj[s][0:M, :], lhsT=wbuf[:, kc, col:col + M], rhs=hT[:, kc, tt * 512:(tt + 1) * 512],
                               start=(kc == 0), stop=(kc == KC - 1))
            return ins
        return mm

    def out_bf16(src_ap, srckey, dst_ap, NP=128):
        i = scnt[0] % 3
        scnt[0] += 1
        P.op(ACT, lambda e: e.activation(out=stg[i][0:NP, :], in_=src_ap, func=AF.Copy), reads=[srckey],
             writes=[("stg", i)])
        P.dma(SP, dst_ap, stg[i][0:NP, :], reads=[("stg", i)])

    def rope_out(s, tt, dst_ap):
        i = scnt[0] % 3
        scnt[0] += 1
        emit_rope_apply(P, 64, 512, lambda lo, hi: pj[s][lo:hi, :], pjk[s], cos2[tt], sinS[tt], ("rt", tt), qf, qsw, ta, tb,
                        "rtmp", stg[i][0:64, :], [("stg", i)])
        P.dma(SP, dst_ap, stg[i][0:64, :], reads=[("stg", i)])

    for j in range(4):
        for tt in range(2):
            s = cnt[0] % 3
            cnt[0] += 1
            P.op(PE, mm_in(j * 128, 128, tt, s), reads=["wbuf"] + hk(tt), writes=[pjk[s]])
            P.op(ACT, lambda e, j=j, tt=tt, s=s: e.activation(out=cl[:, j, tt * 512:(tt + 1) * 512], in_=pj[s][:, :],
                                                              func=AF.Copy), reads=[pjk[s]], writes=[("cl", j, tt)])
    if kind == "kv":
        for tt in range(2):
            s = cnt[0] % 3
            cnt[0] += 1
            P.op(PE, mm_in(512, 64, tt, s), reads=["wbuf"] + hk(tt), writes=[pjk[s]])
            rope_out(s, tt, kr_d[:, tt * 512:(tt + 1) * 512])
    else:
        P.dma(POOL, wbuf[:, :, 0:512], w1_d[:, :, 512:1024], writes=["wbuf"])
        for h in range(MH):
            for tt in range(2):
                s = cnt[0] % 3
                cnt[0] += 1
                P.op(PE, mm_in(h * 128, 128, tt, s), reads=["wbuf"] + hk(tt), writes=[pjk[s]])
                out_bf16(pj[s][:, :], pjk[s], qm_d[h][:, tt * 512:(tt + 1) * 512])
    for tt in range(2):
        emit_rmsnorm(P, "gl", lambda kc, tt=tt: cl[:, kc, tt * 512:(tt + 1) * 512], [("cl", kc, tt) for kc in range(4)],
                     4, 512, 512, gl, lambda kc, tt=tt: cn[:, kc, tt * 512:(tt + 1) * 512],
                     [("cn", kc, tt) for kc in range(4)], ones, ph, "ph", sq, rs1, rs2)

    def mm_up(col, M, tt, s):
        def mm(e):
            ins = None
            for kc in range(4):
                ins = e.matmul(pj[s][0:M, :], lhsT=wu[:, kc, col:col + M], rhs=cn[:, kc, tt * 512:(tt + 1) * 512],
                               start=(kc == 0), stop=(kc == 3))
            return ins
        return mm

    cnk = lambda tt: [("cn", kc, tt) for kc in range(4)]
    if kind == "kv":
        for h in range(NHB):
            for tt in range(2):
                s = cnt[0] % 3
                cnt[0] += 1
                P.op(PE, mm_up(h * 128, 128, tt, s), reads=["wu"] + cnk(tt), writes=[pjk[s]])
                out_bf16(pj[s][:, :], pjk[s], kn_d[h][:, tt * 512:(tt + 1) * 512])
        P.dma(POOL, wu[:, :, 0:1536], wuv_d[:, :, :], writes=["wu"])
        for blk in range(TOK // 128):
            tt = blk // 4
            for nt in range(3):
                s = cnt[0] % 3
                cnt[0] += 1

                def mm(e, blk=blk, nt=nt, s=s):
                    ins = None
                    for kc in range(4):
                        ins = e.matmul(pj[s][:, :], lhsT=cn[:, kc, blk * 128:(blk + 1) * 128],
                                       rhs=wu[:, kc, nt * 512:(nt + 1) * 512], start=(kc == 0), stop=(kc == 3))
                    return ins
                P.op(PE, mm, reads=["wu"] + cnk(tt), writes=[pjk[s]])
                out_bf16(pj[s][:, :], pjk[s], v_d[blk][:, nt * 512:(nt + 1) * 512])
    else:
        for h in range(NHB):
            for tt in range(2):
                s = cnt[0] % 3
                cnt[0] += 1
                P.op(PE, mm_up(h * 192, 128, tt, s), reads=["wu"] + cnk(tt), writes=[pjk[s]])
                out_bf16(pj[s][:, :], pjk[s], qn_d[h][:, tt * 512:(tt + 1) * 512])
                s = cnt[0] % 3
                cnt[0] += 1
                P.op(PE, mm_up(h * 192 + 128, 64, tt, s), reads=["wu"] + cnk(tt), writes=[pjk[s]])
                rope_out(s, tt, qr_d[h][:, tt * 512:(tt + 1) * 512])
    return P.build()


def posb_of(pos_tok):
    return np.ascontiguousarray(np.broadcast_to(pos_tok.astype(np.int32)[None, :], (128, pos_tok.shape[0])))


def zz_block(c, m):
    return 8 * m + (c if m % 2 == 0 else 7 - c)


def build_b1b():
    P = Prog()
    NHC = 16
    NKB = SEQ // 128
    qn_d = P.din("qnT", [NHB, 128, TOK], BF16)
    qr_d = P.din("qrT", [NHB, 64, TOK], BF16)
    qm_d = P.din("qmT", [MH, 128, TOK], BF16)
    kn_d = P.din("knT", [NHB, 128, SEQ], BF16)
    kr_d = P.din("krT", [64, SEQ], BF16)
    vv_d = P.din("vv", [NHB, 128, NKB, 128], BF16)
    mask_d = P.din("mask", [128, 8, 8, 128], BF16)
    memT_d = P.din("memT", [128, KC, MEMT])
    wmk_d = P.din("wmk", [128, KC, 1024])
    wo_d = P.din("wo", [128, NHC, DM])
    xT_d = P.din("xT", [DM, TOK])
    xo_d = P.dout("xo", [DM, TOK])

    ones = emit_consts(P)
    ones32 = P.sb("ones32_sb", [128, 128], F32)
    P.op(DVE, lambda e: e.memset(ones32[:], 1.0), writes=["ones32"])
    krT = P.sb("krT_sb", [64, SEQ], BF16)
    Kb = [P.sb("Kb_sb%d" % i, [128, 32 * 128], BF16) for i in range(2)]
    Vb = [P.sb("Vb_sb%d" % i, [128, 32, 128], BF16) for i in range(2)]
    qn = [P.sb("qn_sb%d" % i, [128, TOK], BF16) for i in range(2)]
    qr = [P.sb("qr_sb%d" % i, [64, TOK], BF16) for i in range(2)]
    qmT = P.sb("qmT_sb", [128, MH, TOK], BF16)
    mask = P.sb("mask_sb", [128, 8, 8, 128], BF16)
    headsT = P.sb("headsT_sb", [128, NHC, TOK], BF16)
    PT = [P.sb("PT_sb%d" % i, [128, 512], BF16) for i in range(3)]
    PTm = P.sb("PTm_sb", [128, 2, 512], BF16)
    lacc = [[P.sb("lacc_sb%d_%d" % (i, j), [128, 512], F32) for j in range(2)] for i in range(2)]
    rinv = P.sb("rinv_sb", [128, 512], F32)
    wbuf = P.sb("wbuf_sb", [128, KC, 512], BF16)
    wo_bufs = [P.sb("wo_sb%d" % i, [128, NHC, 512], BF16) for i in range(2)]
    xs = [P.sb("xs_sb%d" % i, [128, 512], F32) for i in range(2)]
    pS = [P.ps("pS%d" % i, [128, 512]) for i in range(3)]
    pO = [P.ps("pO%d" % i, [128, 512]) for i in range(2)]
    pL = P.ps("pL", [128, 512])
    pj = [P.ps("pj%d" % i, [128, 512]) for i in range(2)]
    pjkeys = [("pj", 0), ("pj", 1)]

    P.dma(SP, krT[:, :], kr_d[:, :], writes=["krT"])
    P.dma(SP, mask[:, :, :, :], mask_d[:, :, :, :], writes=["mask"])
    for h in range(MH):
        P.dma(SP, qmT[:, h, :], qm_d[h], writes=[("qm", h, 0), ("qm", h, 1)])

    scale = 192.0 ** -0.5
    sc = [0]

    def load_kv(h, half):
        seq = 2 * h + half
        b = seq % 2
        P.dma(SP, Kb[b][:, :], kn_d[h][:, half * 4096:(half + 1) * 4096], writes=[("K", b)])
        P.dma(SP, Vb[b][:, :, :], vv_d[h][:, half * 32:(half + 1) * 32, :], writes=[("V", b)])

    def load_q(h):
        b = h % 2
        P.dma(SP, qn[b][:, :], qn_d[h], writes=[("qn", b)])
        P.dma(SP, qr[b][:, :], qr_d[h], writes=[("qr", b)])

    NSL = 3
    items = []

    def seg(h, X, kbs, first, last, pre=None):
        for kb in kbs:
            items.append(dict(h=h, X=X, kb=kb, isf=(first and kb == kbs[0]), isl=(last and kb == kbs[-1]),
                              pre=(pre if kb == kbs[0] else None)))

    def stage1(it, idx):
        h, X, kb = it["h"], it["X"], it["kb"]
        qb = h % 2
        half = kb // 32
        b = (2 * h + half) % 2
        m0 = kb // 8
        masked = (m0 >= 4 * X) and (m0 < 4 * X + 4)
        c0 = 128 * (m0 - 4 * X) if masked else 0
        N = 512 - c0
        s = idx % NSL
        q0 = X * 512 + c0
        kl = kb % 32
        it.update(b=b, c0=c0, N=N, s=s, kl=kl)

        def mms(e):
            e.matmul(pS[s][:, 0:N], lhsT=Kb[b][:, kl * 128:(kl + 1) * 128], rhs=qn[qb][:, q0:q0 + N],
                     start=True, stop=False)
            return e.matmul(pS[s][:, 0:N], lhsT=krT[0:64, kb * 128:(kb + 1) * 128], rhs=qr[qb][0:64, q0:q0 + N],
                            start=False, stop=True)
        P.op(PE, mms, reads=[("K", b), "krT", ("qn", qb), ("qr", qb)], writes=[("pS", s)])
        P.op(ACT, lambda e: e.activation(out=PT[s][:, 0:N], in_=pS[s][:, 0:N], func=AF.Exp, scale=scale),
             reads=[("pS", s)], writes=[("PT", s)])
        if masked:
            P.op(POOL, lambda e: e.tensor_tensor(out=PT[s][:, 0:128], in0=PT[s][:, 0:128],
                                                 in1=mask[:, m0, kb - 8 * m0, :], op=ALU.mult),
                 reads=[("PT", s), "mask"], writes=[("PT", s)])

    def stage2(it):
        h, X, kb = it["h"], it["X"], it["kb"]
        b, c0, N, s, kl = it["b"], it["c0"], it["N"], it["s"], it["kl"]
        o = (2 * h + X) % 2
        par = kb % 2
        isf, isl = it["isf"], it["isl"]
        P.op(PE, lambda e: e.matmul(pO[o][:, c0:512], lhsT=Vb[b][:, kl, :], rhs=PT[s][:, 0:N], start=isf, stop=isl),
             reads=[("V", b), ("PT", s)], writes=[("pO", o)])
        if kb < 2:
            P.op(DVE, lambda e: e.tensor_copy(out=lacc[o][par][:, :], in_=PT[s][:, :]),
                 reads=[("PT", s)], writes=[("lacc", o, par)])
        else:
            P.op(DVE, lambda e: e.tensor_tensor(out=lacc[o][par][:, c0:512], in0=lacc[o][par][:, c0:512],
                                                in1=PT[s][:, 0:N], op=ALU.add),
                 reads=[("PT", s), ("lacc", o, par)], writes=[("lacc", o, par)])
        if isl:
            finalize(h, X)

    def finalize(h, X):
        o = (2 * h + X) % 2

        def mm(e):
            e.matmul(pL[:, :], lhsT=ones32[:], rhs=lacc[o][0][:, :], start=True, stop=False)
            return e.matmul(pL[:, :], lhsT=ones32[:], rhs=lacc[o][1][:, :], start=False, stop=True)
        P.op(PE, mm, reads=["ones32", ("lacc", o, 0), ("lacc", o, 1)], writes=["pL"])
        P.op(DVE, lambda e: e.reciprocal(out=rinv[:, :], in_=pL[:, :]), reads=["pL"], writes=["rinv"])
        P.op(DVE, lambda e: e.tensor_tensor(out=headsT[:, h, X * 512:(X + 1) * 512], in0=pO[o][:, :],
                                            in1=rinv[:, :], op=ALU.mult),
             reads=[("pO", o), "rinv"], writes=[("heads", h, X)])

    load_q(0)
    load_kv(0, 0)
    for h in range(NHB):
        seg(h, 0, list(range(0, 32)), True, True, pre=(lambda h=h: load_kv(h, 1)))
        seg(h, 1, list(range(0, 32)), True, False)
        if h + 1 < NHB:
            seg(h, 1, list(range(32, 64)), False, True, pre=(lambda h=h: (load_q(h + 1), load_kv(h + 1, 0))))
        else:
            seg(h, 1, list(range(32, 64)), False, True)
    for i in range(len(items) + 1):
        if i < len(items) and items[i]["pre"] is not None:
            if i >= 1:
                stage2(items[i - 1])
            items[i]["pre"]()
            stage1(items[i], i)
            continue
        if i < len(items):
            stage1(items[i], i)
        if i >= 1:
            stage2(items[i - 1])

    mkT, mv = emit_mem_kv(P, memT_d, wmk_d, ones, wbuf, lambda: "wbuf", pj, pjkeys)
    emit_mem_attn(P, qmT, lambda h, tt: ("qm", h, tt), mkT, mv, ones, headsT, NHB, pS, pO[0], pL, PTm, rinv,
                  pOkey=("pO", 0))

    xT_v = xT_d.rearrange("(k p) t -> p k t", p=128)
    xo_v = xo_d.rearrange("(k p) t -> p k t", p=128)
    cnt = 0
    for fq in range(DM // 512):
        b = fq % 2
        P.dma(POOL, wo_bufs[b][:, :, :], wo_d[:, :, fq * 512:(fq + 1) * 512], writes=[("wo", b)])
        for f4 in range(4):
            fo = fq * 4 + f4
            for tt in range(2):
                i = cnt % 2
                cnt += 1
                sl = slice(tt * 512, (tt + 1) * 512)
                P.dma(SP, xs[i][:, :], xT_v[:, fo, sl], writes=[("xs", i)])

                def mm(e, f4=f4, sl=sl, i=i, b=b):
                    ins = None
                    for kc in range(NHC):
                        ins = e.matmul(pj[i][:, :], lhsT=wo_bufs[b][:, kc, f4 * 128:(f4 + 1) * 128], rhs=headsT[:, kc, sl],
                                       start=(kc == 0), stop=(kc == NHC - 1))
                    return ins
                P.op(PE, mm, reads=[("wo", b)] + [("heads", kc, tt) for kc in range(NHC)], writes=[pjkeys[i]])
                P.op(DVE, lambda e, i=i: e.tensor_tensor(out=xs[i][:, :], in0=pj[i][:, :], in1=xs[i][:, :], op=ALU.add),
                     reads=[pjkeys[i], ("xs", i)], writes=[("xs", i)])
                P.dma(SP, xo_v[:, fo, sl], xs[i][:, :], reads=[("xs", i)])
    return P.build()


import ml_dtypes

BF16_NP = ml_dtypes.bfloat16


def _run(nc, maps):
    res = run_bass_kernel_spmd(nc, maps, core_ids=list(range(NCORES)))
    return res.results


def zz_index(c):
    return np.concatenate([np.arange(zz_block(c, m) * 128, zz_block(c, m) * 128 + 128) for m in range(8)])


def zz_mask(c):
    k = np.arange(128)[:, None]
    q = np.arange(128)[None, :]
    tri = (k <= q).astype(np.float32)
    mk = np.zeros((128, 8, 8, 128), np.float32)
    for m in range(8):
        bm = zz_block(c, m)
        for j in range(8):
            kb = 8 * m + j
            if kb < bm:
                mk[:, m, j, :] = 1.0
            elif kb == bm:
                mk[:, m, j, :] = tri
    return mk.astype(BF16_NP)


def run_a_layer(x, pos, mem, w_in, w_mem_kv, w_out, g_attn):
    res = _run(build_a1(), a1_host_inputs(x, pos, w_in, g_attn))
    att_tok = np.concatenate([r["attT"].T for r in res], axis=1)
    res = _run(build_a2(), a2_host_inputs(x, att_tok, w_in, w_mem_kv, mem, w_out, g_attn))
    return np.concatenate([r["xo"].T for r in res], axis=0)


def run_ffn(x, w_gate, w_val, conv_w, conv_b, w_down, g_ffn, g_final=None):
    res = _run(build_ffn(final_norm=g_final is not None),
               ffn_host_inputs(x, w_gate, w_val, conv_w, conv_b, w_down, g_ffn, g_final))
    xo = np.concatenate([r["xo"].T for r in res], axis=0)
    fin = None
    if g_final is not None:
        fin = np.concatenate([r["fin"].T for r in res], axis=0)
    return xo, fin


def run_kv(x, pos, g_norm, w_dkv, g_latent, w_uk, w_uv):
    rc = rope_consts_host(64, 32)
    maps = []
    for c in range(NCORES):
        sl = slice(c * TOK, (c + 1) * TOK)
        maps.append({"xT": np.ascontiguousarray(x[sl].T), "gx": gvec(g_norm), "gl": gvec(g_latent),
                     "posb": posb_of(pos[sl]), "rc": rc, "w1": pkn(w_dkv), "wuk": pkn(w_uk), "wuv": pkn(w_uv)})
    res = _run(build_proj("kv"), maps)
    knT = np.ascontiguousarray(np.concatenate([r["knT"] for r in res], axis=2))
    krT = np.ascontiguousarray(np.concatenate([r["krT"] for r in res], axis=1))
    v = np.concatenate([r["v"].reshape(TOK, NHB, 128) for r in res], axis=0)
    vv = np.ascontiguousarray(v.reshape(SEQ // 128, 128, NHB, 128).transpose(2, 1, 0, 3))
    return knT, krT, vv


def run_b_layer(x, pos, mem, kvs, w_in, g_qnorm, w_uq, w_mem_kv, w_out, g_attn):
    knT, krT, vv = kvs
    rc = rope_consts_host(64, 32)
    idx = [zz_index(c) for c in range(NCORES)]
    xTs = [np.ascontiguousarray(x[idx[c]].T) for c in range(NCORES)]
    maps = []
    for c in range(NCORES):
        maps.append({"xT": xTs[c], "gx": gvec(g_attn), "gl": gvec(g_qnorm), "posb": posb_of(pos[idx[c]]), "rc": rc,
                     "w1": pkn(w_in), "wuq": pkn(w_uq)})
    rq = _run(build_proj("q"), maps)
    memT = pkn(np.ascontiguousarray(mem.T))
    wmk = pkn(w_mem_kv)
    wo = pkn(w_out)
    maps = []
    for c in range(NCORES):
        maps.append({"qnT": rq[c]["qnT"], "qrT": rq[c]["qrT"], "qmT": rq[c]["qmT"], "knT": knT, "krT": krT, "vv": vv,
                     "mask": zz_mask(c), "memT": memT, "wmk": wmk, "wo": wo, "xT": xTs[c]})
    res = _run(build_b1b(), maps)
    xo = np.empty_like(x)
    for c in range(NCORES):
        xo[idx[c]] = res[c]["xo"].T
    return xo


def kernel(x, mem, positions, a_w_in, a_w_mem_kv, a_w_out, b_w_in, b_g_qnorm, b_w_uq, b_w_mem_kv, b_w_out,
           kv_g_norm, kv_w_dkv, kv_g_latent, kv_w_uk, kv_w_uv, g_attn, g_ffn, ffn_w_gate, ffn_w_val, ffn_conv_w,
           ffn_conv_b, ffn_w_down, g_final):
    f = lambda a: np.asarray(a, dtype=np.float32)
    xs = f(x)[0]
    memh = f(mem)[0]
    pos = np.asarray(positions)[0].astype(np.int32)
    fin = None
    kvs = None
    for layer in range(4):
        if layer < 2:
            xs = run_a_layer(xs, pos, memh, f(a_w_in[layer]), f(a_w_mem_kv[layer]), f(a_w_out[layer]), f(g_attn[layer]))
        else:
            if kvs is None:
                kvs = run_kv(xs, pos, f(kv_g_norm), f(kv_w_dkv), f(kv_g_latent), f(kv_w_uk), f(kv_w_uv))
            i = layer - 2
            xs = run_b_layer(xs, pos, memh, kvs, f(b_w_in[i]), f(b_g_qnorm[i]), f(b_w_uq[i]), f(b_w_mem_kv[i]),
                             f(b_w_out[i]), f(g_attn[layer]))
        xs, fin = run_ffn(xs, f(ffn_w_gate[layer]), f(ffn_w_val[layer]), f(ffn_conv_w[layer]), f(ffn_conv_b[layer]),
                          f(ffn_w_down[layer]), f(g_ffn[layer]), f(g_final) if layer == 3 else None)
    return np.ascontiguousarray(fin[None]).astype(np.float32)
```

```python
import numpy as np
from contextlib import ExitStack
import concourse.bass as bass
import concourse.mybir as mybir
from concourse.bass_utils import run_bass_kernel_spmd

F32 = mybir.dt.float32
BF16 = mybir.dt.bfloat16
I32 = mybir.dt.int32
AF = mybir.ActivationFunctionType
ALU = mybir.AluOpType

PE, ACT, DVE, POOL, SP = "pe", "act", "dve", "pool", "sp"
NCORES = 8
SAME_ENGINE_SYNC = True
N_DMA_SEMS = 8


class Op:
    __slots__ = ("eng", "fn", "deps", "signal", "ticket", "is_dma", "dsem", "dcount", "presem", "idx")

    def __init__(self, eng, fn, is_dma):
        self.eng = eng
        self.fn = fn
        self.deps = ()
        self.signal = False
        self.ticket = 0
        self.is_dma = is_dma
        self.dsem = None
        self.dcount = 0
        self.presem = None


class Prog:
    def __init__(self):
        self.nc = bass.Bass("TRN2", target_bir_lowering=False)
        self.ops = []
        self.last_w = {}
        self.rd_eng = {}
        self.rd_dma = {}
        self.stack = ExitStack()
        self.uid = 0

    def din(self, name, shape, dt=F32):
        return self.nc.dram_tensor(name, list(shape), dt, kind="ExternalInput").ap()

    def dout(self, name, shape, dt=F32):
        return self.nc.dram_tensor(name, list(shape), dt, kind="ExternalOutput").ap()

    def sb(self, name, shape, dt):
        return self.stack.enter_context(self.nc.sbuf_tensor(name, list(shape), dt))

    def ps(self, name, shape, dt=F32):
        return self.stack.enter_context(self.nc.psum_tensor(name, list(shape), dt))

    def op(self, eng, fn, reads=(), writes=(), is_dma=False):
        o = Op(eng, fn, is_dma)
        deps = set()
        for k in reads:
            w = self.last_w.get(k)
            if w is not None:
                deps.add(w)
        for k in writes:
            w = self.last_w.get(k)
            if w is not None:
                deps.add(w)
            for r in self.rd_eng.get(k, {}).values():
                deps.add(r)
            for r in self.rd_dma.get(k, ()):
                deps.add(r)
        for k in reads:
            if is_dma:
                self.rd_dma.setdefault(k, []).append(o)
            else:
                self.rd_eng.setdefault(k, {})[eng] = o
        for k in writes:
            self.last_w[k] = o
            self.rd_eng[k] = {}
            self.rd_dma[k] = []
        deps.discard(o)
        o.deps = tuple(deps)
        self.ops.append(o)
        return o

    def dma(self, queue, out, in_, reads=(), writes=()):
        return self.op(queue, lambda e: e.dma_start(out=out, in_=in_), reads, writes, is_dma=True)

    def build(self):
        nc = self.nc
        for i, o in enumerate(self.ops):
            o.idx = i
        for o in self.ops:
            latest = {}
            keep = []
            for d in o.deps:
                if d.is_dma:
                    keep.append(d)
                    continue
                if d.eng == o.eng and (d.eng == PE or not SAME_ENGINE_SYNC):
                    continue
                if d.eng not in latest or latest[d.eng].idx < d.idx:
                    latest[d.eng] = d
            for d in latest.values():
                d.signal = True
                keep.append(d)
            o.deps = tuple(keep)
        cnt = {}
        dma_i = {}
        for o in self.ops:
            if o.is_dma:
                i = dma_i.get(o.eng, 0)
                dma_i[o.eng] = i + 1
                o.dsem = (o.eng, i % N_DMA_SEMS)
                o.dcount = 16 * (i // N_DMA_SEMS + 1)
                if i >= N_DMA_SEMS:
                    o.presem = (o.dsem, o.dcount - 16)
            elif o.signal:
                cnt[o.eng] = cnt.get(o.eng, 0) + 1
                o.ticket = cnt[o.eng]
        sems = {}
        for eng in (PE, ACT, DVE, POOL):
            sems[eng] = self.stack.enter_context(nc.semaphore("s_" + eng))
        for q in (SP, POOL, ACT):
            if q in dma_i:
                for i in range(min(N_DMA_SEMS, dma_i[q])):
                    sems[(q, i)] = self.stack.enter_context(nc.semaphore("d_%s%d" % (q, i)))
        final_dma = {}
        for o in self.ops:
            if o.is_dma:
                final_dma[o.dsem] = o.dcount
        block = self.stack.enter_context(nc.Block())

        def make_body(eng_name):
            ops_e = [o for o in self.ops if o.eng == eng_name]

            def body(e):
                waited = {}
                for o in ops_e:
                    need = {}
                    for d in o.deps:
                        if d.is_dma:
                            k, v = d.dsem, d.dcount
                        else:
                            if d.eng == eng_name and (eng_name == PE or not SAME_ENGINE_SYNC):
                                continue
                            k, v = d.eng, d.ticket
                        if waited.get(k, 0) >= v:
                            continue
                        if need.get(k, 0) < v:
                            need[k] = v
                    if o.presem is not None:
                        k, v = o.presem
                        if waited.get(k, 0) < v and need.get(k, 0) < v:
                            need[k] = v
                    for k, v in need.items():
                        e.wait_ge(sems[k], v)
                        waited[k] = v
                    ins = o.fn(e)
                    if o.is_dma:
                        ins.then_inc(sems[o.dsem], 16)
                    elif o.signal:
                        ins.then_inc(sems[eng_name], 1)
                for k, v in final_dma.items():
                    if k[0] == eng_name and waited.get(k, 0) < v:
                        e.wait_ge(sems[k], v)
            return body

        for eng_name, deco in ((PE, block.tensor), (ACT, block.scalar), (DVE, block.vector),
                               (POOL, block.gpsimd), (SP, block.sync)):
            if any(o.eng == eng_name for o in self.ops):
                deco(make_body(eng_name))
        self.stack.close()
        return nc


def emit_consts(P):
    ones = P.sb("ones_bf", [128, 128], BF16)
    P.op(DVE, lambda e: e.memset(ones[:], 1.0), writes=["ones"])
    return ones


def emit_rmsnorm(P, gkey, xin, xkeys, KC, T, D, g_sb, out, okeys, ones, ps_ss, pskey, sq, rs1, rs2, eps=1e-6,
                 post=None, lnexp_eps=None):
    for kc in range(KC):
        P.op(ACT, lambda e, kc=kc: e.activation(out=sq[:, kc, 0:T], in_=xin(kc), func=AF.Square),
             reads=[xkeys[kc]], writes=[("nsq", kc)])

    def mm(e):
        ins = None
        for kc in range(KC):
            ins = e.matmul(ps_ss[:, 0:T], lhsT=ones[:], rhs=sq[:, kc, 0:T], start=(kc == 0), stop=(kc == KC - 1))
        return ins
    P.op(PE, mm, reads=[("nsq", kc) for kc in range(KC)] + ["ones"], writes=[pskey])
    if lnexp_eps is not None:
        P.op(ACT, lambda e: e.activation(out=rs1[:, 0:T], in_=ps_ss[:, 0:T], func=AF.Ln, scale=1.0 / D,
                                         bias=lnexp_eps[:, 0:1]), reads=[pskey, "epsc"], writes=["rs1"])
        P.op(ACT, lambda e: e.activation(out=rs2[:, 0:T], in_=rs1[:, 0:T], func=AF.Exp, scale=-0.5),
             reads=["rs1"], writes=["rs2"])
    else:
        P.op(ACT, lambda e: e.activation(out=rs1[:, 0:T], in_=ps_ss[:, 0:T], func=AF.Sqrt, scale=1.0 / D, bias=eps),
             reads=[pskey], writes=["rs1"])
        P.op(DVE, lambda e: e.reciprocal(out=rs2[:, 0:T], in_=rs1[:, 0:T]), reads=["rs1"], writes=["rs2"])
    for kc in range(KC):
        P.op(DVE, lambda e, kc=kc: e.scalar_tensor_tensor(out=out(kc), in0=xin(kc), scalar=g_sb[:, kc:kc + 1],
                                                          in1=rs2[:, 0:T], op0=ALU.mult, op1=ALU.mult),
             reads=[xkeys[kc], "rs2", gkey], writes=[okeys[kc]])
        if post is not None:
            post(kc)


TOK = 1024
DM = 2048
KC = 16
DFF = 5632
FG = 256
NG = DFF // FG
NFT = FG // 128


def build_ffn(final_norm=False):
    P = Prog()
    xT_d = P.din("xT", [DM, TOK])
    xh_d = P.din("xh", [DM, 2])
    wg_d = P.din("wg", [NG, 128, KC, FG])
    wv_d = P.din("wv", [NG, 128, KC, FG])
    wd_d = P.din("wd", [DFF, DM])
    cw_d = P.din("cw", [128, DFF // 128, 4])
    gf_d = P.din("gf", [128, KC])
    xo_d = P.dout("xo", [DM, TOK])
    if final_norm:
        gfin_d = P.din("gfin", [128, KC])
        fo_d = P.dout("fin", [DM, TOK])

    ones = emit_consts(P)
    xT = P.sb("xT_sb", [128, KC, TOK], F32)
    xh = P.sb("xh_sb", [128, KC, 2], F32)
    hT = P.sb("hT_sb", [128, KC, TOK], BF16)
    hh = P.sb("hh_sb", [128, KC, 2], BF16)
    gf = P.sb("gf_sb", [128, KC], F32)
    cw = P.sb("cw_sb", [128, DFF // 128, 4], F32)
    sq = P.sb("sq_sb", [128, KC, 512], BF16)
    rs1 = P.sb("rs1_sb", [128, 512], F32)
    rs2 = P.sb("rs2_sb", [128, 512], F32)
    wg = [P.sb("wg_sb%d" % i, [128, KC, FG], BF16) for i in range(2)]
    wv = [P.sb("wv_sb%d" % i, [128, KC, FG], BF16) for i in range(2)]
    wd = [P.sb("wd_sb%d" % i, [128, NFT, DM], BF16) for i in range(2)]
    gsb = [P.sb("g_sb%d" % i, [128, TOK + 2], F32) for i in range(2)]
    t1 = [P.sb("t1_sb%d" % i, [128, TOK], F32) for i in range(2)]
    t2 = [P.sb("t2_sb%d" % i, [128, TOK], F32) for i in range(2)]
    uT = [P.sb("uT_sb%d" % i, [128, NFT, TOK], BF16) for i in range(2)]
    pg = [P.ps("pg%d" % i, [128, 512]) for i in range(2)]
    pv = [P.ps("pv%d" % i, [128, 512]) for i in range(2)]
    pd = [P.ps("pd%d" % i, [128, 512]) for i in range(2)]
    ph = P.ps("ph", [128, 512])

    xT_v = xT_d.rearrange("(k p) t -> p k t", p=128)
    for kc in range(KC):
        P.dma(SP, xT[:, kc, :], xT_v[:, kc, :], writes=[("x", kc, 0), ("x", kc, 1)])
    P.dma(SP, xh[:, :, :], xh_d.rearrange("(k p) t -> p k t", p=128), writes=["xh"])
    P.dma(SP, gf[:, :], gf_d[:, :], writes=["gf"])
    P.dma(SP, cw[:, :, :], cw_d[:, :, :], writes=["cw"])

    def load_w(G):
        b = G % 2
        P.dma(POOL, wg[b][:, :, :], wg_d[G], writes=[("wg", b)])
        P.dma(POOL, wv[b][:, :, :], wv_d[G], writes=[("wv", b)])

    def load_wd(G):
        b = G % 2
        P.dma(POOL, wd[b][:, :, :], wd_d[G * FG:(G + 1) * FG, :].rearrange("(f p) n -> p f n", p=128),
              writes=[("wd", b)])

    load_w(0)
    load_wd(0)
    for tt in range(2):
        emit_rmsnorm(P, "gf", lambda kc, tt=tt: xT[:, kc, tt * 512:(tt + 1) * 512], [("x", kc, tt) for kc in range(KC)],
                     KC, 512, DM, gf, lambda kc, tt=tt: hT[:, kc, tt * 512:(tt + 1) * 512],
                     [("h", kc, tt) for kc in range(KC)], ones, ph, "ph", sq, rs1, rs2)
    emit_rmsnorm(P, "gf", lambda kc: xh[:, kc, :], ["xh"] * KC, KC, 2, DM, gf, lambda kc: hh[:, kc, :],
                 [("hh", kc) for kc in range(KC)], ones, ph, "ph", sq, rs1, rs2)

    hkeys = lambda tt: [("h", kc, tt) for kc in range(KC)]
    hhkeys = [("hh", kc) for kc in range(KC)]

    def down(G):
        b = G % 2
        for fo in range(KC):
            for tt in range(2):
                i = (fo * 2 + tt) % 2

                def mm(e, fo=fo, tt=tt, i=i, b=b):
                    ins = None
                    for ft in range(NFT):
                        ins = e.matmul(pd[i][:, :], lhsT=wd[b][:, ft, fo * 128:(fo + 1) * 128],
                                       rhs=uT[b][:, ft, tt * 512:(tt + 1) * 512], start=(ft == 0), stop=(ft == NFT - 1))
                    return ins
                P.op(PE, mm, reads=[("wd", b)] + [("u", b, ft, tt) for ft in range(NFT)], writes=[("pd", i)])
                P.op(DVE, lambda e, fo=fo, tt=tt, i=i: e.tensor_tensor(
                    out=xT[:, fo, tt * 512:(tt + 1) * 512], in0=pd[i][:, :], in1=xT[:, fo, tt * 512:(tt + 1) * 512],
                    op=ALU.add), reads=[("pd", i), ("x", fo, tt)], writes=[("x", fo, tt)])

    for G in range(NG):
        b = G % 2
        if G + 1 < NG:
            load_w(G + 1)
        for ft in range(NFT):
            fi = G * NFT + ft
            gb = fi % 2
            for tt in range(2):
                def mmg(e, tt=tt, ft=ft, b=b):
                    ins = None
                    for kc in range(KC):
                        ins = e.matmul(pg[tt][:, :], lhsT=wg[b][:, kc, ft * 128:(ft + 1) * 128],
                                       rhs=hT[:, kc, tt * 512:(tt + 1) * 512], start=(kc == 0), stop=(kc == KC - 1))
                    return ins
                P.op(PE, mmg, reads=[("wg", b)] + hkeys(tt), writes=[("pg", tt)])
                P.op(ACT, lambda e, tt=tt, gb=gb: e.activation(out=gsb[gb][:, 2 + tt * 512:2 + (tt + 1) * 512],
                                                               in_=pg[tt][:, :], func=AF.Copy),
                     reads=[("pg", tt)], writes=[("gsb", gb, tt)])

            def mmh(e, ft=ft, b=b):
                ins = None
                for kc in range(KC):
                    ins = e.matmul(ph[:, 0:2], lhsT=wg[b][:, kc, ft * 128:(ft + 1) * 128], rhs=hh[:, kc, :],
                                   start=(kc == 0), stop=(kc == KC - 1))
                return ins
            P.op(PE, mmh, reads=[("wg", b)] + hhkeys, writes=["ph"])
            P.op(ACT, lambda e, gb=gb: e.activation(out=gsb[gb][:, 0:2], in_=ph[:, 0:2], func=AF.Copy),
                 reads=["ph"], writes=[("gsb", gb, "h")])
            for tt in range(2):
                def mmv(e, tt=tt, ft=ft, b=b):
                    ins = None
                    for kc in range(KC):
                        ins = e.matmul(pv[tt][:, :], lhsT=wv[b][:, kc, ft * 128:(ft + 1) * 128],
                                       rhs=hT[:, kc, tt * 512:(tt + 1) * 512], start=(kc == 0), stop=(kc == KC - 1))
                    return ins
                P.op(PE, mmv, reads=[("wv", b)] + hkeys(tt), writes=[("pv", tt)])
            if ft == 0:
                if G > 0:
                    down(G - 1)
                if G + 1 < NG:
                    load_wd(G + 1)
            gk = [("gsb", gb, 0), ("gsb", gb, 1), ("gsb", gb, "h")]
            P.op(DVE, lambda e, gb=gb, fi=fi: e.tensor_scalar(out=t1[gb][:, :], in0=gsb[gb][:, 2:TOK + 2],
                                                              scalar1=cw[:, fi, 2:3], scalar2=cw[:, fi, 3:4],
                                                              op0=ALU.mult, op1=ALU.add),
                 reads=gk + ["cw"], writes=[("t1", gb)])
            P.op(DVE, lambda e, gb=gb, fi=fi: e.scalar_tensor_tensor(out=t2[gb][:, :], in0=gsb[gb][:, 1:TOK + 1],
                                                                     scalar=cw[:, fi, 1:2], in1=t1[gb][:, :],
                                                                     op0=ALU.mult, op1=ALU.add),
                 reads=gk + ["cw", ("t1", gb)], writes=[("t2", gb)])
            P.op(DVE, lambda e, gb=gb, fi=fi: e.scalar_tensor_tensor(out=t1[gb][:, :], in0=gsb[gb][:, 0:TOK],
                                                                     scalar=cw[:, fi, 0:1], in1=t2[gb][:, :],
                                                                     op0=ALU.mult, op1=ALU.add),
                 reads=gk + ["cw", ("t2", gb)], writes=[("t1", gb)])
            P.op(ACT, lambda e, gb=gb: e.activation(out=t2[gb][:, :], in_=t1[gb][:, :], func=AF.Silu),
                 reads=[("t1", gb)], writes=[("t2", gb)])
            for tt in range(2):
                P.op(DVE, lambda e, gb=gb, tt=tt, ft=ft, b=b: e.tensor_tensor(
                    out=uT[b][:, ft, tt * 512:(tt + 1) * 512], in0=pv[tt][:, :], in1=t2[gb][:, tt * 512:(tt + 1) * 512],
                    op=ALU.mult), reads=[("pv", tt), ("t2", gb)], writes=[("u", b, ft, tt)])
    down(NG - 1)

    xo_v = xo_d.rearrange("(k p) t -> p k t", p=128)
    for kc in range(KC):
        P.dma(SP, xo_v[:, kc, :], xT[:, kc, :], reads=[("x", kc, 0), ("x", kc, 1)])
    if final_norm:
        gfin = P.sb("gfin_sb", [128, KC], F32)
        P.dma(SP, gfin[:, :], gfin_d[:, :], writes=["gfin"])
        stg = [(t1[0], ("t1", 0)), (t1[1], ("t1", 1)), (t2[0], ("t2", 0)), (t2[1], ("t2", 1))]
        fo_v = fo_d.rearrange("(k p) t -> p k t", p=128)
        for tt in range(2):
            emit_rmsnorm(P, "gfin", lambda kc, tt=tt: xT[:, kc, tt * 512:(tt + 1) * 512],
                         [("x", kc, tt) for kc in range(KC)], KC, 512, DM, gfin, lambda kc: stg[kc % 4][0][:, 0:512],
                         [stg[kc % 4][1] for kc in range(KC)], ones, ph, "ph", sq, rs1, rs2, post=lambda kc, tt=tt: P.dma(
                             SP, fo_v[:, kc, tt * 512:(tt + 1) * 512], stg[kc % 4][0][:, 0:512], reads=[stg[kc % 4][1]]))
    return P.build()


def ffn_host_inputs(x_tok, w_gate, w_val, conv_w, conv_b, w_down, g_ffn, g_final=None):
    S = x_tok.shape[0]
    wg = np.ascontiguousarray(w_gate.reshape(KC, 128, NG, FG).transpose(2, 1, 0, 3))
    wv = np.ascontiguousarray(w_val.reshape(KC, 128, NG, FG).transpose(2, 1, 0, 3))
    cw = np.ascontiguousarray(np.concatenate([conv_w, conv_b[None, :]], axis=0).reshape(4, DFF // 128, 128).transpose(2, 1, 0))
    gf = np.ascontiguousarray(g_ffn.reshape(KC, 128).T)
    maps = []
    for c in range(NCORES):
        xs = x_tok[c * TOK:(c + 1) * TOK]
        halo = np.zeros((2, DM), np.float32)
        if c > 0:
            halo[:] = x_tok[c * TOK - 2:c * TOK]
        m = {"xT": np.ascontiguousarray(xs.T), "xh": np.ascontiguousarray(halo.T), "wg": wg, "wv": wv,
             "wd": w_down, "cw": cw, "gf": gf}
        if g_final is not None:
            m["gfin"] = np.ascontiguousarray(g_final.reshape(KC, 128).T)
        maps.append(m)
    return maps


TWO_PI = float(2.0 * np.pi)


def emit_rope_tables(P, NP, T, pos_ap, poskey, inv, nsgn, negpi, posf, ang, r1, r2, cos2, sinS, kpre, ki):
    P.op(DVE, lambda e: e.tensor_copy(out=posf[0:NP, 0:T], in_=pos_ap), reads=[poskey], writes=[(kpre, "posf")])
    P.op(DVE, lambda e: e.tensor_scalar(out=ang[0:NP, 0:T], in0=posf[0:NP, 0:T], scalar1=inv[0:NP, 0:1], scalar2=None,
                                        op0=ALU.mult), reads=[(kpre, "posf"), "ropec"], writes=[(kpre, "ang")])
    for which, (rr, dst) in enumerate(((r1, sinS), (r2, cos2))):
        rk = (kpre, "r", which)
        if which == 1:
            P.op(DVE, lambda e: e.tensor_scalar(out=ang[0:NP, 0:T], in0=ang[0:NP, 0:T], scalar1=0.25, scalar2=None,
                                                op0=ALU.add), reads=[(kpre, "ang")], writes=[(kpre, "ang")])
        P.op(DVE, lambda e: e.tensor_copy(out=ki[0:NP, 0:T], in_=ang[0:NP, 0:T]), reads=[(kpre, "ang")], writes=[(kpre, "ki")])
        P.op(DVE, lambda e, rr=rr: e.tensor_copy(out=rr[0:NP, 0:T], in_=ki[0:NP, 0:T]), reads=[(kpre, "ki")], writes=[rk])
        P.op(DVE, lambda e, rr=rr: e.tensor_tensor(out=rr[0:NP, 0:T], in0=ang[0:NP, 0:T], in1=rr[0:NP, 0:T],
                                                   op=ALU.subtract), reads=[(kpre, "ang"), rk], writes=[rk])
        P.op(DVE, lambda e, rr=rr: e.scalar_tensor_tensor(out=rr[0:NP, 0:T], in0=rr[0:NP, 0:T], scalar=0.5,
                                                          in1=rr[0:NP, 0:T], op0=ALU.is_ge, op1=ALU.subtract),
             reads=[rk], writes=[rk])
        if which == 0:
            P.op(ACT, lambda e, rr=rr, dst=dst: e.activation(out=dst[0:NP, 0:T], in_=rr[0:NP, 0:T], func=AF.Sin,
                                                             scale=nsgn[0:NP, 0:1]),
                 reads=[rk, "ropec"], writes=[(kpre, "sinS")])
        else:
            P.op(ACT, lambda e, rr=rr, dst=dst: e.activation(out=dst[0:NP, 0:T], in_=rr[0:NP, 0:T], func=AF.Sin,
                                                             scale=-TWO_PI),
                 reads=[rk], writes=[(kpre, "cos2")])


def emit_rope_apply(P, NP, T, src, srckey, cos2, sinS, kpre, qf, qsw, ta, tb, tkey, out_ap, outkeys, in_view=None,
                    eng2=POOL):
    H = NP // 2
    P.op(ACT, lambda e: e.activation(out=qsw[0:H, 0:T], in_=src(H, NP), func=AF.Copy),
         reads=[srckey], writes=[(tkey, "qsw0")])
    P.op(ACT, lambda e: e.activation(out=qsw[H:NP, 0:T], in_=src(0, H), func=AF.Copy),
         reads=[srckey], writes=[(tkey, "qsw1")])
    P.op(DVE, lambda e: e.tensor_tensor(out=ta[0:NP, 0:T], in0=src(0, NP), in1=cos2[0:NP, 0:T], op=ALU.mult),
         reads=[(kpre, "cos2")], writes=[(tkey, "a"), srckey])
    P.op(eng2, lambda e: e.tensor_tensor(out=tb[0:NP, 0:T], in0=qsw[0:NP, 0:T], in1=sinS[0:NP, 0:T], op=ALU.mult),
         reads=[(tkey, "qsw0"), (tkey, "qsw1"), (kpre, "sinS")], writes=[(tkey, "b")])
    v = in_view if in_view is not None else (lambda a: a)
    P.op(DVE, lambda e: e.tensor_tensor(out=out_ap, in0=v(ta[0:NP, 0:T]), in1=v(tb[0:NP, 0:T]), op=ALU.add),
         reads=[(tkey, "a"), (tkey, "b")], writes=outkeys)


def rope_consts_host(NP, half):
    j = np.arange(NP) % half
    inv = (np.float32(10000.0) ** (-(j.astype(np.float32)) / np.float32(half))).astype(np.float32)
    sgn = np.where(np.arange(NP) % (2 * half) < half, -1.0, 1.0).astype(np.float32)
    c = np.zeros((128, 4), np.float32)
    c[:NP, 0] = (inv.astype(np.float64) / (2.0 * np.pi)).astype(np.float32)
    c[:NP, 1] = (-2.0 * np.pi * sgn).astype(np.float32)
    return c


MEMT = 256
MH = 4


def emit_mem_kv(P, memT_d, wmk_d, ones, wbuf, wkeyfn, pj, pjkeys):
    memT = P.sb("memT_sb", [128, KC, MEMT], BF16)
    mkT = P.sb("mkT_sb", [128, MH, MEMT], BF16)
    mv = P.sb("mv_sb", [128, 2, MH * 128], BF16)
    P.dma(POOL, memT[:, :, :], memT_d[:, :, :], writes=["memT"])
    for part in range(2):
        P.dma(POOL, wbuf[:, :, :], wmk_d[:, :, part * 512:(part + 1) * 512], writes=[wkeyfn()])
        if part == 0:
            for h in range(MH):
                i = h % 2

                def mm(e, h=h, i=i):
                    ins = None
                    for kc in range(KC):
                        ins = e.matmul(pj[i][:, 0:MEMT], lhsT=wbuf[:, kc, h * 128:(h + 1) * 128], rhs=memT[:, kc, :],
                                       start=(kc == 0), stop=(kc == KC - 1))
                    return ins
                P.op(PE, mm, reads=[wkeyfn(), "memT"], writes=[pjkeys[i]])
                P.op(ACT, lambda e, h=h, i=i: e.activation(out=mkT[:, h, :], in_=pj[i][:, 0:MEMT], func=AF.Copy),
                     reads=[pjkeys[i]], writes=[("mkT", h)])
        else:
            for mt in range(2):
                i = mt % 2

                def mm(e, mt=mt, i=i):
                    ins = None
                    for kc in range(KC):
                        ins = e.matmul(pj[i][:, :], lhsT=memT[:, kc, mt * 128:(mt + 1) * 128], rhs=wbuf[:, kc, :],
                                       start=(kc == 0), stop=(kc == KC - 1))
                    return ins
                P.op(PE, mm, reads=[wkeyfn(), "memT"], writes=[pjkeys[i]])
                P.op(ACT, lambda e, mt=mt, i=i: e.activation(out=mv[:, mt, :], in_=pj[i][:, :], func=AF.Copy),
                     reads=[pjkeys[i]], writes=[("mv", mt)])
    return mkT, mv


def emit_mem_attn(P, qmT, qmkeyfn, mkT, mv, ones, headsT, hbase, pS, pO, pL, PT, rinv, pOkey="pO", PTkey="PTm"):
    scale = 128.0 ** -0.5
    for h in range(MH):
        for tt in range(TOK // 512):
            sl = slice(tt * 512, (tt + 1) * 512)
            for mt in range(2):
                P.op(PE, lambda e, h=h, mt=mt, sl=sl: e.matmul(pS[mt][:, :], lhsT=mkT[:, h, mt * 128:(mt + 1) * 128],
                                                              rhs=qmT[:, h, sl], start=True, stop=True),
                     reads=[("mkT", h), qmkeyfn(h, tt)], writes=[("pS", mt)])
                P.op(ACT, lambda e, mt=mt: e.activation(out=PT[:, mt, :], in_=pS[mt][:, :], func=AF.Exp, scale=scale),
                     reads=[("pS", mt)], writes=[(PTkey, mt)])

            def mmo(e, h=h):
                ins = None
                for mt in range(2):
                    ins = e.matmul(pO[:, :], lhsT=mv[:, mt, h * 128:(h + 1) * 128], rhs=PT[:, mt, :],
                                   start=(mt == 0), stop=(mt == 1))
                return ins
            P.op(PE, mmo, reads=[("mv", 0), ("mv", 1), (PTkey, 0), (PTkey, 1)], writes=[pOkey])

            def mml(e):
                ins = None
                for mt in range(2):
                    ins = e.matmul(pL[:, :], lhsT=ones[:], rhs=PT[:, mt, :], start=(mt == 0), stop=(mt == 1))
                return ins
            P.op(PE, mml, reads=["ones", (PTkey, 0), (PTkey, 1)], writes=["pL"])
            P.op(DVE, lambda e: e.reciprocal(out=rinv[:, :], in_=pL[:, :]), reads=["pL"], writes=["rinv"])
            P.op(DVE, lambda e, h=h, sl=sl: e.tensor_tensor(out=headsT[:, hbase + h, sl], in0=pO[:, :], in1=rinv[:, :],
                                                            op=ALU.mult),
                 reads=[pOkey, "rinv"], writes=[("heads", hbase + h, tt)])


def emit_out_proj(P, headsT, NHC, wo_d, wo_bufs, xres, xkeyfn, pj, pjkeys):
    cnt = 0
    for fq in range(DM // 512):
        b = fq % 2
        P.dma(POOL, wo_bufs[b][:, :, :], wo_d[:, :, fq * 512:(fq + 1) * 512], writes=[("wo", b)])
        for f4 in range(4):
            fo = fq * 4 + f4
            for tt in range(TOK // 512):
                i = cnt % 2
                cnt += 1
                sl = slice(tt * 512, (tt + 1) * 512)

                def mm(e, f4=f4, sl=sl, i=i, b=b):
                    ins = None
                    for kc in range(NHC):
                        ins = e.matmul(pj[i][:, :], lhsT=wo_bufs[b][:, kc, f4 * 128:(f4 + 1) * 128], rhs=headsT[:, kc, sl],
                                       start=(kc == 0), stop=(kc == NHC - 1))
                    return ins
                P.op(PE, mm, reads=[("wo", b)] + [("heads", kc, tt) for kc in range(NHC)], writes=[pjkeys[i]])
                P.op(DVE, lambda e, fo=fo, sl=sl, i=i: e.tensor_tensor(out=xres[:, fo, sl], in0=pj[i][:, :],
                                                                       in1=xres[:, fo, sl], op=ALU.add),
                     reads=[pjkeys[i], xkeyfn(fo, tt)], writes=[xkeyfn(fo, tt)])


def build_a2():
    P = Prog()
    NHC = 12
    xT_d = P.din("xT", [DM, TOK])
    att_d = P.din("attT", [8 * 128, TOK], BF16)
    wq_d = P.din("wq", [128, KC, 512])
    wmk_d = P.din("wmk", [128, KC, 1024])
    memT_d = P.din("memT", [128, KC, MEMT])
    wo_d = P.din("wo", [128, NHC, DM])
    ga_d = P.din("ga", [128, KC])
    xo_d = P.dout("xo", [DM, TOK])

    ones = emit_consts(P)
    xT = P.sb("xT_sb", [128, KC, TOK], F32)
    hT = P.sb("hT_sb", [128, KC, TOK], BF16)
    ga = P.sb("ga_sb", [128, KC], F32)
    sq = P.sb("sq_sb", [128, KC, 512], BF16)
    rs1 = P.sb("rs1_sb", [128, 512], F32)
    rs2 = P.sb("rs2_sb", [128, 512], F32)
    wbuf = P.sb("wbuf_sb", [128, KC, 512], BF16)
    qmT = P.sb("qmT_sb", [128, MH, TOK], BF16)
    headsT = P.sb("headsT_sb", [128, NHC, TOK], BF16)
    wo_bufs = [P.sb("wo_sb%d" % i, [128, NHC, 512], BF16) for i in range(2)]
    PT = P.sb("PT_sb", [128, 2, 512], BF16)
    rinv = P.sb("rinv_sb", [128, 512], F32)
    pj = [P.ps("pj%d" % i, [128, 512]) for i in range(2)]
    pS = [P.ps("pS%d" % i, [128, 512]) for i in range(2)]
    pO = P.ps("pO", [128, 512])
    pL = P.ps("pL", [128, 512])
    ph = P.ps("ph", [128, 512])
    pjkeys = [("pj", 0), ("pj", 1)]
    wver = [0]

    xT_v = xT_d.rearrange("(k p) t -> p k t", p=128)
    for kc in range(KC):
        P.dma(SP, xT[:, kc, :], xT_v[:, kc, :], writes=[("x", kc, 0), ("x", kc, 1)])
    P.dma(SP, ga[:, :], ga_d[:, :], writes=["ga"])
    att_v = att_d.rearrange("(k p) t -> p k t", p=128)
    for kc in range(8):
        P.dma(SP, headsT[:, kc, :], att_v[:, kc, :], writes=[("heads", kc, 0), ("heads", kc, 1)])
    P.dma(POOL, wbuf[:, :, :], wq_d[:, :, :], writes=["wbuf"])
    for tt in range(2):
        emit_rmsnorm(P, "ga", lambda kc, tt=tt: xT[:, kc, tt * 512:(tt + 1) * 512], [("x", kc, tt) for kc in range(KC)],
                     KC, 512, DM, ga, lambda kc, tt=tt: hT[:, kc, tt * 512:(tt + 1) * 512],
                     [("h", kc, tt) for kc in range(KC)], ones, ph, "ph", sq, rs1, rs2)
    cnt = 0
    for h in range(MH):
        for tt in range(2):
            i = cnt % 2
            cnt += 1
            sl = slice(tt * 512, (tt + 1) * 512)

            def mm(e, h=h, sl=sl, i=i):
                ins = None
                for kc in range(KC):
                    ins = e.matmul(pj[i][:, :], lhsT=wbuf[:, kc, h * 128:(h + 1) * 128], rhs=hT[:, kc, sl],
                                   start=(kc == 0), stop=(kc == KC - 1))
                return ins
            P.op(PE, mm, reads=["wbuf"] + [("h", kc, tt) for kc in range(KC)], writes=[pjkeys[i]])
            P.op(ACT, lambda e, h=h, sl=sl, i=i: e.activation(out=qmT[:, h, sl], in_=pj[i][:, :], func=AF.Copy),
                 reads=[pjkeys[i]], writes=[("qm", h, tt)])
    mkT, mv = emit_mem_kv(P, memT_d, wmk_d, ones, wbuf, lambda: "wbuf", pj, pjkeys)
    emit_mem_attn(P, qmT, lambda h, tt: ("qm", h, tt), mkT, mv, ones, headsT, 8, pS, pO, pL, PT, rinv)
    emit_out_proj(P, headsT, NHC, wo_d, wo_bufs, xT, lambda fo, tt: ("x", fo, tt), pj, pjkeys)
    xo_v = xo_d.rearrange("(k p) t -> p k t", p=128)
    for kc in range(KC):
        P.dma(SP, xo_v[:, kc, :], xT[:, kc, :], reads=[("x", kc, 0), ("x", kc, 1)])
    return P.build()


def pkn(w):
    K_ = w.shape[0] // 128
    return np.ascontiguousarray(w.reshape(K_, 128, w.shape[1]).transpose(1, 0, 2))


def gvec(g):
    return np.ascontiguousarray(g.reshape(-1, 128).T)


def a2_host_inputs(x_tok, att_tok_bf16, w_in, w_mem_kv, mem, w_out, g_attn):
    wq = pkn(w_in[:, 9216:9728])
    wmk = pkn(w_mem_kv)
    memT = pkn(np.ascontiguousarray(mem.T))
    wo = pkn(w_out)
    ga = gvec(g_attn)
    maps = []
    for c in range(NCORES):
        sl = slice(c * TOK, (c + 1) * TOK)
        maps.append({"xT": np.ascontiguousarray(x_tok[sl].T), "attT": np.ascontiguousarray(att_tok_bf16[sl].T),
                     "wq": wq, "wmk": wmk, "memT": memT, "wo": wo, "ga": ga})
    return maps


SEQ = 8192
DIL = (1, 4, 16)
A1_TT = 256
A1_SPAN = 2048


def build_a1():
    P = Prog()
    TT = A1_TT
    TPS = A1_SPAN // TT
    NSP = SEQ // A1_SPAN
    xT_d = P.din("xT", [SEQ // A1_TT, 128, KC, A1_TT])
    w_d = P.din("w", [128, KC, 1152])
    ga_d = P.din("ga", [128, KC])
    posb_d = P.din("posb", [128, SEQ], I32)
    rc_d = P.din("rc", [128, 4])
    mk_d = P.din("mk", [128, 2, 2, 128])
    id_d = P.din("ident", [128, 128])
    att_d = P.dout("attT", [128, SEQ], BF16)

    ones = emit_consts(P)
    w_sb = P.sb("w_sb", [128, KC, 1152], BF16)
    ga = P.sb("ga_sb", [128, KC], F32)
    rc = P.sb("rc_sb", [128, 4], F32)
    mk = P.sb("mk_sb", [128, 2, 2, 128], BF16)
    ident = P.sb("ident_sb", [128, 128], BF16)
    x_sb = [P.sb("x_sb0", [128, KC, TT], F32)]
    sq = P.sb("sq_sb", [128, KC, TT], BF16)
    hT = [P.sb("hT_sb%d" % i, [128, KC, 2 * TT], BF16) for i in range(2)]
    rs1 = P.sb("rs1_sb", [128, TT], F32)
    rs2 = P.sb("rs2_sb", [128, TT], F32)
    posi = P.sb("posi_sb", [128, TT], I32)
    posf = P.sb("posf_sb", [128, TT], F32)
    kint = P.sb("kint_sb", [128, TT], I32)
    ang = P.sb("ang_sb", [128, TT], F32)
    r1 = P.sb("r1_sb", [128, TT], F32)
    r2 = P.sb("r2_sb", [128, TT], F32)
    cos2 = [P.sb("cos2_sb%d" % i, [128, TT], F32) for i in range(4)]
    sinS = [P.sb("sinS_sb%d" % i, [128, TT], F32) for i in range(4)]
    ta = [P.sb("ta_sb%d" % i, [128, TT], F32) for i in range(2)]
    tb = [P.sb("tb_sb%d" % i, [128, TT], F32) for i in range(2)]
    qkf = [None, None]
    qsw = [P.sb("qsw_sb%d" % i, [128, TT], F32) for i in range(2)]
    QT = [P.sb("QT_sb%d" % g, [128, A1_SPAN], BF16) for g in range(3)]
    KT = [[P.sb("KT_sb%d_%d" % (p, g), [128, A1_SPAN], BF16) for g in range(3)] for p in range(2)]
    VT = [P.sb("VT_sb%d" % g, [128, A1_SPAN], BF16) for g in range(3)]
    V = [P.sb("V_sb%d" % p, [128, 3, 16, 128], BF16) for p in range(2)]
    acc = P.sb("acc_sb", [128, A1_SPAN], F32)
    lacc = P.sb("lacc_sb", [128, A1_SPAN], F32)
    ob = P.sb("ob_sb", [128, A1_SPAN // 2], BF16)
    PT = [P.sb("PT_sb%d" % i, [128, 2, 128], BF16) for i in range(2)]
    ph = P.ps("ph", [128, 512])
    pjb = [P.ps("pjb%d" % i, [128, 512]) for i in range(2)]
    pSb = [P.ps("pSb%d" % i, [128, 4, 128]) for i in range(2)]
    pOL = [P.ps("pOL%d" % i, [128, 4, 128]) for i in range(2)]
    pTb = P.ps("pTb", [128, 8, 128], BF16)
    pj = [pjb[0][:, :], pjb[1][:, :], pSb[0][:, :, :].rearrange("p a b -> p (a b)"),
          pSb[1][:, :, :].rearrange("p a b -> p (a b)")]
    pjkeys = [("pj", 0), ("pj", 1), ("pS", 0), ("pS", 1)]
    pT = [pTb[:, 0, :], ph[:, :].bitcast(BF16)[:, 0:128]]
    pTk = ["pT", "ph"]

    epsc = P.sb("epsc_sb", [128, 1], F32)
    P.op(DVE, lambda e: e.memset(epsc[:, :], 1e-6), writes=["epsc"])
    P.dma(POOL, w_sb[:, :, :], w_d[:, :, :], writes=["w"])
    P.dma(SP, ga[:, :], ga_d[:, :], writes=["ga"])
    P.dma(SP, rc[:, :], rc_d[:, :], writes=["ropec"])
    P.dma(POOL, mk[:, :, :, :], mk_d[:, :, :, :], writes=["mk"])
    P.dma(POOL, ident[:, :], id_d[:, :], writes=["ident"])
    for g in range(3):
        P.op(DVE, lambda e, g=g: e.memset(KT[1][g][:, :], 0.0), writes=[("KT", 1, g)])
    P.op(DVE, lambda e: e.memset(V[1][:, :, :, :], 0.0), writes=[("V", 1, g, b) for g in range(3) for b in range(16)])

    NT = SEQ // TT

    def load(t):
        P.dma(SP, x_sb[0][:, :, :], xT_d[t], writes=[("x", 0)])

    def norm(t):
        hb = (t // 2) % 2
        hf = t % 2
        ts = t % 4
        emit_rmsnorm(P, "ga", lambda kc: x_sb[0][:, kc, :], [("x", 0)] * KC, KC, TT, DM, ga,
                     lambda kc: hT[hb][:, kc, hf * TT:(hf + 1) * TT], [("h", hb, hf, kc) for kc in range(KC)], ones, ph,
                     "ph", sq, rs1, rs2, lnexp_eps=epsc)
        P.dma(SP, posi[:, :], posb_d[:, t * TT:(t + 1) * TT], writes=["posi"])
        emit_rope_tables(P, 128, TT, posi[:, :], "posi", rc[:, 0:1], rc[:, 1:2], rc[:, 2:3], posf, ang, r1, r2,
                         cos2[ts], sinS[ts], ("rt", ts), kint)

    pjc = [0]

    def proj(p, units):
        hb = p % 2
        n = (2 * p) // TPS
        par = n % 2
        hk = [("h", hb, hf, kc) for hf in range(2) for kc in range(KC)]
        for g in range(3):
            d = DIL[g]
            for tq in range(3):
                if (g * 3 + tq) not in units:
                    continue
                col = (g * 3 + tq) * 128
                s = pjc[0] % 4
                pjc[0] += 1

                def mm(e, col=col, s=s, hb=hb):
                    ins = None
                    for kc in range(KC):
                        ins = e.matmul(pj[s], lhsT=w_sb[:, kc, col:col + 128], rhs=hT[hb][:, kc, :],
                                       start=(kc == 0), stop=(kc == KC - 1))
                    return ins
                P.op(PE, mm, reads=["w"] + hk, writes=[pjkeys[s]])
                i0 = (2 * p) % TPS
                if tq == 2:
                    P.op(ACT, lambda e, g=g, i0=i0, s=s: e.activation(out=VT[g][:, i0 * TT:(i0 + 2) * TT], in_=pj[s],
                                                                      func=AF.Copy),
                         reads=[pjkeys[s]], writes=[("VT", g, i0), ("VT", g, i0 + 1)])
                    continue
                for hf in range(2):
                    t = 2 * p + hf
                    i = t % TPS
                    ts = t % 4
                    s2 = rtc[0] % 2
                    rtc[0] += 1
                    dst = QT[g] if tq == 0 else KT[par][g]
                    dkey = ("QT", g, i) if tq == 0 else ("KT", par, g, i)
                    if d == 1:
                        out_ap = dst[:, i * TT:(i + 1) * TT]
                        view = None
                    else:
                        a = TT // d
                        out_ap = dst[:, :].rearrange("p (r a) -> p r a", r=d)[:, :, a * i:a * (i + 1)]
                        view = (lambda ap, d=d: ap.rearrange("p (a r) -> p r a", r=d))
                    emit_rope_apply(P, 128, TT, lambda lo, hi, s=s, hf=hf: pj[s][lo:hi, hf * TT:(hf + 1) * TT], pjkeys[s],
                                    cos2[ts], sinS[ts], ("rt", ts), qkf[s2], qsw[s2], ta[s2], tb[s2], ("rtmp", s2),
                                    out_ap, [dkey], in_view=view, eng2=POOL)

    rtc = [0]
    sc = [0]
    scale = 128.0 ** -0.5

    def attention(n):
        par = n % 2
        tcount = 0
        for g in range(3):
            d = DIL[g]
            nb = 16 // d
            for r in range(d):
                for m in range(nb):
                    blk = r * nb + m
                    s = tcount % 2
                    tcount += 1
                    st = r + d * 128 * m
                    P.op(PE, lambda e, g=g, s=s, st=st, d=d: e.transpose(out=pT[s],
                                                                         in_=VT[g][:, st:st + d * 127 + 1:d],
                                                                         identity=ident[:, :]),
                         reads=[("VT", g, i) for i in range(TPS)] + ["ident"], writes=[pTk[s]])
                    P.op(ACT, lambda e, g=g, s=s, blk=blk, par=par: e.activation(out=V[par][:, g, blk, :],
                                                                                 in_=pT[s], func=AF.Copy),
                         reads=[pTk[s]], writes=[("V", par, g, blk)])
        items = []
        for g in range(3):
            d = DIL[g]
            nb = 16 // d
            for r in range(d):
                for m in range(nb):
                    items.append((g, r, m))

        def stage1(it):
            g, r, m = it
            d = DIL[g]
            nb = 16 // d
            L = A1_SPAN // d
            kview = lambda p_: KT[p_][g][:, :].rearrange("p (r a) -> p r a", r=d)
            qview = QT[g][:, :].rearrange("p (r a) -> p r a", r=d)
            s = sc[0] % 2
            sc[0] += 1
            kcur = kview(par)[:, r, 128 * m:128 * (m + 1)]
            if m > 0:
                kprev = kview(par)[:, r, 128 * (m - 1):128 * m]
                kpk = []
            else:
                kprev = kview(1 - par)[:, r, L - 128:L]
                kpk = [("KT", 1 - par, g, i) for i in range(TPS)] if n > 0 else [("KT", 1, g)]
            q = qview[:, r, 128 * m:128 * (m + 1)]
            variant = 1 if (n == 0 and m == 0) else 0

            def mms(e):
                e.matmul(pSb[s][:, 0, :], lhsT=kprev, rhs=q, start=True, stop=True)
                return e.matmul(pSb[s][:, 1, :], lhsT=kcur, rhs=q, start=True, stop=True)
            P.op(PE, mms, reads=kpk + [("KT", par, g, i) for i in range(TPS)] + [("QT", g, i) for i in range(TPS)],
                 writes=[("pS", s)])
            P.op(ACT, lambda e: e.activation(out=PT[s][:, :, :], in_=pSb[s][:, 0:2, :], func=AF.Exp, scale=scale),
                 reads=[("pS", s)], writes=[("PT", s)])
            P.op(DVE, lambda e: e.tensor_tensor(out=PT[s][:, :, :], in0=PT[s][:, :, :], in1=mk[:, variant, :, :],
                                                op=ALU.mult), reads=[("PT", s), "mk"], writes=[("PT", s)])
            return s

        def stage2(it, s):
            g, r, m = it
            d = DIL[g]
            nb = 16 // d
            blk = r * nb + m
            if m > 0:
                vprev = V[par][:, g, blk - 1, :]
                vpk = ("V", par, g, blk - 1)
            else:
                vprev = V[1 - par][:, g, r * nb + nb - 1, :]
                vpk = ("V", 1 - par, g, r * nb + nb - 1)

            def mmo(e):
                e.matmul(pOL[s][:, 0, :], lhsT=vprev, rhs=PT[s][:, 0, :], start=True, stop=False)
                e.matmul(pOL[s][:, 0, :], lhsT=V[par][:, g, blk, :], rhs=PT[s][:, 1, :], start=False, stop=True)
                e.matmul(pOL[s][:, 1, :], lhsT=ones[:], rhs=PT[s][:, 0, :], start=True, stop=False)
                return e.matmul(pOL[s][:, 1, :], lhsT=ones[:], rhs=PT[s][:, 1, :], start=False, stop=True)
            P.op(PE, mmo, reads=[("PT", s), vpk, ("V", par, g, blk), "ones"], writes=[("pOL", s)])
            st = r + d * 128 * m
            cols = slice(st, st + d * 127 + 1, d)
            if g == 0:
                pbs = [m]
            elif g == 1:
                pbs = [4 * m + j for j in range(4)]
            else:
                pbs = list(range(16))
            akeys = [("acc", pb) for pb in pbs]
            lkeys = [("lacc", pb) for pb in pbs]
            if g == 0:
                P.op(ACT, lambda e: e.activation(out=acc[:, cols], in_=pOL[s][:, 0, :], func=AF.Copy),
                     reads=[("pOL", s)], writes=akeys)
                P.op(ACT, lambda e: e.activation(out=lacc[:, cols], in_=pOL[s][:, 1, :], func=AF.Copy),
                     reads=[("pOL", s)], writes=lkeys)
            else:
                P.op(DVE, lambda e: e.tensor_tensor(out=acc[:, cols], in0=pOL[s][:, 0, :], in1=acc[:, cols], op=ALU.add),
                     reads=[("pOL", s)] + akeys, writes=akeys)
                P.op(DVE, lambda e: e.tensor_tensor(out=lacc[:, cols], in0=pOL[s][:, 1, :], in1=lacc[:, cols],
                                                    op=ALU.add), reads=[("pOL", s)] + lkeys, writes=lkeys)

        prev = None
        for i in range(len(items) + 1):
            cur = None
            if i < len(items):
                cur = (items[i], stage1(items[i]))
            if prev is not None:
                stage2(*prev)
            prev = cur
        allk = [("acc", pb) for pb in range(16)]
        alll = [("lacc", pb) for pb in range(16)]
        P.op(ACT, lambda e: e.activation(out=lacc[:, :], in_=lacc[:, :], func=AF.Ln), reads=alll, writes=alll)
        P.op(ACT, lambda e: e.activation(out=lacc[:, :], in_=lacc[:, :], func=AF.Exp, scale=-1.0), reads=alll, writes=alll)
        HS = A1_SPAN // 2
        for hh_ in range(2):
            P.op(DVE, lambda e, hh_=hh_: e.tensor_tensor(out=ob[:, :], in0=acc[:, hh_ * HS:(hh_ + 1) * HS],
                                                         in1=lacc[:, hh_ * HS:(hh_ + 1) * HS], op=ALU.mult),
                 reads=allk + alll, writes=["ob"])
            P.dma(SP, att_d[:, n * A1_SPAN + hh_ * HS:n * A1_SPAN + (hh_ + 1) * HS], ob[:, :], reads=["ob"])

    load(0)
    norm(0)
    load(1)
    norm(1)
    load(2)
    for p in range(NT // 2):
        proj(p, range(0, 3))
        if 2 * p + 2 < NT:
            norm(2 * p + 2)
            if 2 * p + 3 < NT:
                load(2 * p + 3)
        proj(p, range(3, 6))
        if 2 * p + 3 < NT:
            norm(2 * p + 3)
            if 2 * p + 4 < NT:
                load(2 * p + 4)
        proj(p, range(6, 9))
        if (2 * p + 1) % TPS == TPS - 1:
            attention((2 * p + 1) // TPS)
    return P.build()


def a1_host_inputs(x_tok, positions, w_in, g_attn):
    xT = np.ascontiguousarray(x_tok.reshape(SEQ // A1_TT, A1_TT, KC, 128).transpose(0, 3, 2, 1))
    posb = np.ascontiguousarray(np.broadcast_to(positions.astype(np.int32)[None, :], (128, SEQ)))
    rc = rope_consts_host(128, 64)
    k = np.arange(128)[:, None]
    q = np.arange(128)[None, :]
    mk = np.zeros((128, 2, 2, 128), np.float32)
    mk[:, 0, 0, :] = (k >= q)
    mk[:, 0, 1, :] = (k <= q)
    mk[:, 1, 1, :] = (k <= q)
    ident = np.eye(128, dtype=np.float32)
    ga = gvec(g_attn)
    w4 = w_in[:, :9216].reshape(DM, 3, 3, 8, 128)
    maps = []
    for c in range(NCORES):
        w = pkn(np.ascontiguousarray(w4[:, :, :, c, :]).reshape(DM, 1152))
        maps.append({"xT": xT, "w": w, "ga": ga, "posb": posb, "rc": rc, "mk": mk, "ident": ident})
    return maps


NHB = 12


def build_proj(kind):
    P = Prog()
    xT_d = P.din("xT", [DM, TOK])
    gx_d = P.din("gx", [128, KC])
    gl_d = P.din("gl", [128, 4])
    posb_d = P.din("posb", [128, TOK], I32)
    rc_d = P.din("rc", [128, 4])
    if kind == "kv":
        w1_d = P.din("w1", [128, KC, 576])
        wuk_d = P.din("wuk", [128, 4, 1536])
        wuv_d = P.din("wuv", [128, 4, 1536])
        kn_d = P.dout("knT", [NHB, 128, TOK], BF16)
        kr_d = P.dout("krT", [64, TOK], BF16)
        v_d = P.dout("v", [TOK // 128, 128, 1536], BF16)
    else:
        w1_d = P.din("w1", [128, KC, 1024])
        wuq_d = P.din("wuq", [128, 4, NHB * 192])
        qn_d = P.dout("qnT", [NHB, 128, TOK], BF16)
        qr_d = P.dout("qrT", [NHB, 64, TOK], BF16)
        qm_d = P.dout("qmT", [MH, 128, TOK], BF16)

    ones = emit_consts(P)
    xT = P.sb("xT_sb", [128, KC, TOK], F32)
    hT = P.sb("hT_sb", [128, KC, TOK], BF16)
    gx = P.sb("gx_sb", [128, KC], F32)
    gl = P.sb("gl_sb", [128, 4], F32)
    rc = P.sb("rc_sb", [128, 4], F32)
    sq = P.sb("sq_sb", [128, KC, 512], BF16)
    rs1 = P.sb("rs1_sb", [128, 512], F32)
    rs2 = P.sb("rs2_sb", [128, 512], F32)
    wbuf = P.sb("wbuf_sb", [128, KC, 576], BF16)
    cl = P.sb("cl_sb", [128, 4, TOK], F32)
    cn = P.sb("cn_sb", [128, 4, TOK], BF16)
    wu = P.sb("wu_sb", [128, 4, NHB * 192], BF16)
    posi = P.sb("posi_sb", [128, 512], I32)
    posf = P.sb("posf_sb", [128, 512], F32)
    kint = P.sb("kint_sb", [128, 512], I32)
    ang = P.sb("ang_sb", [128, 512], F32)
    r1 = P.sb("r1_sb", [128, 512], F32)
    r2 = P.sb("r2_sb", [128, 512], F32)
    cos2 = [P.sb("cos2_sb%d" % i, [128, 512], F32) for i in range(2)]
    sinS = [P.sb("sinS_sb%d" % i, [128, 512], F32) for i in range(2)]
    qf = P.sb("qf_sb", [128, 512], F32)
    qsw = P.sb("qsw_sb", [128, 512], F32)
    ta = P.sb("ta_sb", [128, 512], F32)
    tb = P.sb("tb_sb", [128, 512], F32)
    stg = [P.sb("stg_sb%d" % i, [128, 512], BF16) for i in range(3)]
    pj = [P.ps("pj%d" % i, [128, 512]) for i in range(3)]
    ph = P.ps("ph", [128, 512])
    pjk = [("pj", i) for i in range(3)]
    cnt = [0]
    scnt = [0]

    xT_v = xT_d.rearrange("(k p) t -> p k t", p=128)
    for kc in range(KC):
        P.dma(SP, xT[:, kc, :], xT_v[:, kc, :], writes=[("x", kc, 0), ("x", kc, 1)])
    P.dma(SP, gx[:, :], gx_d[:, :], writes=["gx"])
    P.dma(SP, gl[:, :], gl_d[:, :], writes=["gl"])
    P.dma(SP, rc[:, :], rc_d[:, :], writes=["ropec"])
    NW1 = 576 if kind == "kv" else 512
    P.dma(POOL, wbuf[:, :, 0:NW1], w1_d[:, :, 0:NW1], writes=["wbuf"])
    if kind == "kv":
        P.dma(POOL, wu[:, :, 0:1536], wuk_d[:, :, :], writes=["wu"])
    else:
        P.dma(POOL, wu[:, :, :], wuq_d[:, :, :], writes=["wu"])
    for tt in range(2):
        emit_rmsnorm(P, "gx", lambda kc, tt=tt: xT[:, kc, tt * 512:(tt + 1) * 512], [("x", kc, tt) for kc in range(KC)],
                     KC, 512, DM, gx, lambda kc, tt=tt: hT[:, kc, tt * 512:(tt + 1) * 512],
                     [("h", kc, tt) for kc in range(KC)], ones, ph, "ph", sq, rs1, rs2)
    for tt in range(2):
        P.dma(SP, posi[:, :], posb_d[:, tt * 512:(tt + 1) * 512], writes=["posi"])
        emit_rope_tables(P, 64, 512, posi[0:64, :], "posi", rc[:, 0:1], rc[:, 1:2], rc[:, 2:3], posf, ang, r1, r2,
                         cos2[tt], sinS[tt], ("rt", tt), kint)

    def hk(tt):
        return [("h", kc, tt) for kc in range(KC)]

    def mm_in(col, M, tt, s):
        def mm(e):
            ins = None
            for kc in range(KC):
                ins = e.matmul(pj[s][0:M, :], lhsT=wbuf[:, kc, col:col + M], rhs=hT[:, kc, tt * 512:(tt + 1) * 512],
                               start=(kc == 0), stop=(kc == KC - 1))
            return ins
        return mm

    def out_bf16(src_ap, srckey, dst_ap, NP=128):
        i = scnt[0] % 3
        scnt[0] += 1
        P.op(ACT, lambda e: e.activation(out=stg[i][0:NP, :], in_=src_ap, func=AF.Copy), reads=[srckey],
             writes=[("stg", i)])
        P.dma(SP, dst_ap, stg[i][0:NP, :], reads=[("stg", i)])

    def rope_out(s, tt, dst_ap):
        i = scnt[0] % 3
        scnt[0] += 1
        emit_rope_apply(P, 64, 512, lambda lo, hi: pj[s][lo:hi, :], pjk[s], cos2[tt], sinS[tt], ("rt", tt), qf, qsw, ta, tb,
                        "rtmp", stg[i][0:64, :], [("stg", i)])
        P.dma(SP, dst_ap, stg[i][0:64, :], reads=[("stg", i)])

    for j in range(4):
        for tt in range(2):
            s = cnt[0] % 3
            cnt[0] += 1
            P.op(PE, mm_in(j * 128, 128, tt, s), reads=["wbuf"] + hk(tt), writes=[pjk[s]])
            P.op(ACT, lambda e, j=j, tt=tt, s=s: e.activation(out=cl[:, j, tt * 512:(tt + 1) * 512], in_=pj[s][:, :],
                                                              func=AF.Copy), reads=[pjk[s]], writes=[("cl", j, tt)])
    if kind == "kv":
        for tt in range(2):
            s = cnt[0] % 3
            cnt[0] += 1
            P.op(PE, mm_in(512, 64, tt, s), reads=["wbuf"] + hk(tt), writes=[pjk[s]])
            rope_out(s, tt, kr_d[:, tt * 512:(tt + 1) * 512])
    else:
        P.dma(POOL, wbuf[:, :, 0:512], w1_d[:, :, 512:1024], writes=["wbuf"])
        for h in range(MH):
            for tt in range(2):
                s = cnt[0] % 3
                cnt[0] += 1
                P.op(PE, mm_in(h * 128, 128, tt, s), reads=["wbuf"] + hk(tt), writes=[pjk[s]])
                out_bf16(pj[s][:, :], pjk[s], qm_d[h][:, tt * 512:(tt + 1) * 512])
    for tt in range(2):
        emit_rmsnorm(P, "gl", lambda kc, tt=tt: cl[:, kc, tt * 512:(tt + 1) * 512], [("cl", kc, tt) for kc in range(4)],
                     4, 512, 512, gl, lambda kc, tt=tt: cn[:, kc, tt * 512:(tt + 1) * 512],
                     [("cn", kc, tt) for kc in range(4)], ones, ph, "ph", sq, rs1, rs2)

    def mm_up(col, M, tt, s):
        def mm(e):
            ins = None
            for kc in range(4):
                ins = e.matmul(pj[s][0:M, :], lhsT=wu[:, kc, col:col + M], rhs=cn[:, kc, tt * 512:(tt + 1) * 512],
                               start=(kc == 0), stop=(kc == 3))
            return ins
        return mm

    cnk = lambda tt: [("cn", kc, tt) for kc in range(4)]
    if kind == "kv":
        for h in range(NHB):
            for tt in range(2):
                s = cnt[0] % 3
                cnt[0] += 1
                P.op(PE, mm_up(h * 128, 128, tt, s), reads=["wu"] + cnk(tt), writes=[pjk[s]])
                out_bf16(pj[s][:, :], pjk[s], kn_d[h][:, tt * 512:(tt + 1) * 512])
        P.dma(POOL, wu[:, :, 0:1536], wuv_d[:, :, :], writes=["wu"])
        for blk in range(TOK // 128):
            tt = blk // 4
            for nt in range(3):
                s = cnt[0] % 3
                cnt[0] += 1

                def mm(e, blk=blk, nt=nt, s=s):
                    ins = None
                    for kc in range(4):
                        ins = e.matmul(pj[s][:, :], lhsT=cn[:, kc, blk * 128:(blk + 1) * 128],
                                       rhs=wu[:, kc, nt * 512:(nt + 1) * 512], start=(kc == 0), stop=(kc == 3))
                    return ins
                P.op(PE, mm, reads=["wu"] + cnk(tt), writes=[pjk[s]])
                out_bf16(pj[s][:, :], pjk[s], v_d[blk][:, nt * 512:(nt + 1) * 512])
    else:
        for h in range(NHB):
            for tt in range(2):
                s = cnt[0] % 3
                cnt[0] += 1
                P.op(PE, mm_up(h * 192, 128, tt, s), reads=["wu"] + cnk(tt), writes=[pjk[s]])
                out_bf16(pj[s][:, :], pjk[s], qn_d[h][:, tt * 512:(tt + 1) * 512])
                s = cnt[0] % 3
                cnt[0] += 1
                P.op(PE, mm_up(h * 192 + 128, 64, tt, s), reads=["wu"] + cnk(tt), writes=[pjk[s]])
                rope_out(s, tt, qr_d[h][:, tt * 512:(tt + 1) * 512])
    return P.build()


def posb_of(pos_tok):
    return np.ascontiguousarray(np.broadcast_to(pos_tok.astype(np.int32)[None, :], (128, pos_tok.shape[0])))


def zz_block(c, m):
    return 8 * m + (c if m % 2 == 0 else 7 - c)


def build_b1b():
    P = Prog()
    NHC = 16
    NKB = SEQ // 128
    qn_d = P.din("qnT", [NHB, 128, TOK], BF16)
    qr_d = P.din("qrT", [NHB, 64, TOK], BF16)
    qm_d = P.din("qmT", [MH, 128, TOK], BF16)
    kn_d = P.din("knT", [NHB, 128, SEQ], BF16)
    kr_d = P.din("krT", [64, SEQ], BF16)
    vv_d = P.din("vv", [NHB, 128, NKB, 128], BF16)
    mask_d = P.din("mask", [128, 8, 8, 128], BF16)
    memT_d = P.din("memT", [128, KC, MEMT])
    wmk_d = P.din("wmk", [128, KC, 1024])
    wo_d = P.din("wo", [128, NHC, DM])
    xT_d = P.din("xT", [DM, TOK])
    xo_d = P.dout("xo", [DM, TOK])

    ones = emit_consts(P)
    ones32 = P.sb("ones32_sb", [128, 128], F32)
    P.op(DVE, lambda e: e.memset(ones32[:], 1.0), writes=["ones32"])
    krT = P.sb("krT_sb", [64, SEQ], BF16)
    Kb = [P.sb("Kb_sb%d" % i, [128, 32 * 128], BF16) for i in range(2)]
    Vb = [P.sb("Vb_sb%d" % i, [128, 32, 128], BF16) for i in range(2)]
    qn = [P.sb("qn_sb%d" % i, [128, TOK], BF16) for i in range(2)]
    qr = [P.sb("qr_sb%d" % i, [64, TOK], BF16) for i in range(2)]
    qmT = P.sb("qmT_sb", [128, MH, TOK], BF16)
    mask = P.sb("mask_sb", [128, 8, 8, 128], BF16)
    headsT = P.sb("headsT_sb", [128, NHC, TOK], BF16)
    PT = [P.sb("PT_sb%d" % i, [128, 512], BF16) for i in range(3)]
    PTm = P.sb("PTm_sb", [128, 2, 512], BF16)
    lacc = [[P.sb("lacc_sb%d_%d" % (i, j), [128, 512], F32) for j in range(2)] for i in range(2)]
    rinv = P.sb("rinv_sb", [128, 512], F32)
    wbuf = P.sb("wbuf_sb", [128, KC, 512], BF16)
    wo_bufs = [P.sb("wo_sb%d" % i, [128, NHC, 512], BF16) for i in range(2)]
    xs = [P.sb("xs_sb%d" % i, [128, 512], F32) for i in range(2)]
    pS = [P.ps("pS%d" % i, [128, 512]) for i in range(3)]
    pO = [P.ps("pO%d" % i, [128, 512]) for i in range(2)]
    pL = P.ps("pL", [128, 512])
    pj = [P.ps("pj%d" % i, [128, 512]) for i in range(2)]
    pjkeys = [("pj", 0), ("pj", 1)]

    P.dma(SP, krT[:, :], kr_d[:, :], writes=["krT"])
    P.dma(SP, mask[:, :, :, :], mask_d[:, :, :, :], writes=["mask"])
    for h in range(MH):
        P.dma(SP, qmT[:, h, :], qm_d[h], writes=[("qm", h, 0), ("qm", h, 1)])

    scale = 192.0 ** -0.5
    sc = [0]

    def load_kv(h, half):
        seq = 2 * h + half
        b = seq % 2
        P.dma(SP, Kb[b][:, :], kn_d[h][:, half * 4096:(half + 1) * 4096], writes=[("K", b)])
        P.dma(SP, Vb[b][:, :, :], vv_d[h][:, half * 32:(half + 1) * 32, :], writes=[("V", b)])

    def load_q(h):
        b = h % 2
        P.dma(SP, qn[b][:, :], qn_d[h], writes=[("qn", b)])
        P.dma(SP, qr[b][:, :], qr_d[h], writes=[("qr", b)])

    NSL = 3
    items = []

    def seg(h, X, kbs, first, last, pre=None):
        for kb in kbs:
            items.append(dict(h=h, X=X, kb=kb, isf=(first and kb == kbs[0]), isl=(last and kb == kbs[-1]),
                              pre=(pre if kb == kbs[0] else None)))

    def stage1(it, idx):
        h, X, kb = it["h"], it["X"], it["kb"]
        qb = h % 2
        half = kb // 32
        b = (2 * h + half) % 2
        m0 = kb // 8
        masked = (m0 >= 4 * X) and (m0 < 4 * X + 4)
        c0 = 128 * (m0 - 4 * X) if masked else 0
        N = 512 - c0
        s = idx % NSL
        q0 = X * 512 + c0
        kl = kb % 32
        it.update(b=b, c0=c0, N=N, s=s, kl=kl)

        def mms(e):
            e.matmul(pS[s][:, 0:N], lhsT=Kb[b][:, kl * 128:(kl + 1) * 128], rhs=qn[qb][:, q0:q0 + N],
                     start=True, stop=False)
            return e.matmul(pS[s][:, 0:N], lhsT=krT[0:64, kb * 128:(kb + 1) * 128], rhs=qr[qb][0:64, q0:q0 + N],
                            start=False, stop=True)
        P.op(PE, mms, reads=[("K", b), "krT", ("qn", qb), ("qr", qb)], writes=[("pS", s)])
        P.op(ACT, lambda e: e.activation(out=PT[s][:, 0:N], in_=pS[s][:, 0:N], func=AF.Exp, scale=scale),
             reads=[("pS", s)], writes=[("PT", s)])
        if masked:
            P.op(POOL, lambda e: e.tensor_tensor(out=PT[s][:, 0:128], in0=PT[s][:, 0:128],
                                                 in1=mask[:, m0, kb - 8 * m0, :], op=ALU.mult),
                 reads=[("PT", s), "mask"], writes=[("PT", s)])

    def stage2(it):
        h, X, kb = it["h"], it["X"], it["kb"]
        b, c0, N, s, kl = it["b"], it["c0"], it["N"], it["s"], it["kl"]
        o = (2 * h + X) % 2
        par = kb % 2
        isf, isl = it["isf"], it["isl"]
        P.op(PE, lambda e: e.matmul(pO[o][:, c0:512], lhsT=Vb[b][:, kl, :], rhs=PT[s][:, 0:N], start=isf, stop=isl),
             reads=[("V", b), ("PT", s)], writes=[("pO", o)])
        if kb < 2:
            P.op(DVE, lambda e: e.tensor_copy(out=lacc[o][par][:, :], in_=PT[s][:, :]),
                 reads=[("PT", s)], writes=[("lacc", o, par)])
        else:
            P.op(DVE, lambda e: e.tensor_tensor(out=lacc[o][par][:, c0:512], in0=lacc[o][par][:, c0:512],
                                                in1=PT[s][:, 0:N], op=ALU.add),
                 reads=[("PT", s), ("lacc", o, par)], writes=[("lacc", o, par)])
        if isl:
            finalize(h, X)

    def finalize(h, X):
        o = (2 * h + X) % 2

        def mm(e):
            e.matmul(pL[:, :], lhsT=ones32[:], rhs=lacc[o][0][:, :], start=True, stop=False)
            return e.matmul(pL[:, :], lhsT=ones32[:], rhs=lacc[o][1][:, :], start=False, stop=True)
        P.op(PE, mm, reads=["ones32", ("lacc", o, 0), ("lacc", o, 1)], writes=["pL"])
        P.op(ACT, lambda e: e.activation(out=rinv[:, :], in_=pL[:, :], func=AF.Ln), reads=["pL"], writes=["rinv"])
        P.op(ACT, lambda e: e.activation(out=rinv[:, :], in_=rinv[:, :], func=AF.Exp, scale=-1.0),
             reads=["rinv"], writes=["rinv"])
        P.op(DVE, lambda e: e.tensor_tensor(out=headsT[:, h, X * 512:(X + 1) * 512], in0=pO[o][:, :],
                                            in1=rinv[:, :], op=ALU.mult),
             reads=[("pO", o), "rinv"], writes=[("heads", h, X)])

    load_q(0)
    load_kv(0, 0)
    for h in range(NHB):
        seg(h, 0, list(range(0, 32)), True, True, pre=(lambda h=h: load_kv(h, 1)))
        seg(h, 1, list(range(0, 32)), True, False)
        if h + 1 < NHB:
            seg(h, 1, list(range(32, 64)), False, True, pre=(lambda h=h: (load_q(h + 1), load_kv(h + 1, 0))))
        else:
            seg(h, 1, list(range(32, 64)), False, True)
    for i in range(len(items) + 1):
        if i < len(items) and items[i]["pre"] is not None:
            if i >= 1:
                stage2(items[i - 1])
            items[i]["pre"]()
            stage1(items[i], i)
            continue
        if i < len(items):
            stage1(items[i], i)
        if i >= 1:
            stage2(items[i - 1])

    mkT, mv = emit_mem_kv(P, memT_d, wmk_d, ones, wbuf, lambda: "wbuf", pj, pjkeys)
    emit_mem_attn(P, qmT, lambda h, tt: ("qm", h, tt), mkT, mv, ones, headsT, NHB, pS, pO[0], pL, PTm, rinv,
                  pOkey=("pO", 0))

    xT_v = xT_d.rearrange("(k p) t -> p k t", p=128)
    xo_v = xo_d.rearrange("(k p) t -> p k t", p=128)
    cnt = 0
    for fq in range(DM // 512):
        b = fq % 2
        P.dma(POOL, wo_bufs[b][:, :, :], wo_d[:, :, fq * 512:(fq + 1) * 512], writes=[("wo", b)])
        for f4 in range(4):
            fo = fq * 4 + f4
            for tt in range(2):
                i = cnt % 2
                cnt += 1
                sl = slice(tt * 512, (tt + 1) * 512)
                P.dma(SP, xs[i][:, :], xT_v[:, fo, sl], writes=[("xs", i)])

                def mm(e, f4=f4, sl=sl, i=i, b=b):
                    ins = None
                    for kc in range(NHC):
                        ins = e.matmul(pj[i][:, :], lhsT=wo_bufs[b][:, kc, f4 * 128:(f4 + 1) * 128], rhs=headsT[:, kc, sl],
                                       start=(kc == 0), stop=(kc == NHC - 1))
                    return ins
                P.op(PE, mm, reads=[("wo", b)] + [("heads", kc, tt) for kc in range(NHC)], writes=[pjkeys[i]])
                P.op(DVE, lambda e, i=i: e.tensor_tensor(out=xs[i][:, :], in0=pj[i][:, :], in1=xs[i][:, :], op=ALU.add),
                     reads=[pjkeys[i], ("xs", i)], writes=[("xs", i)])
                P.dma(SP, xo_v[:, fo, sl], xs[i][:, :], reads=[("xs", i)])
    return P.build()


import ml_dtypes

BF16_NP = ml_dtypes.bfloat16


def _run(nc, maps):
    res = run_bass_kernel_spmd(nc, maps, core_ids=list(range(NCORES)))
    return res.results


def zz_index(c):
    return np.concatenate([np.arange(zz_block(c, m) * 128, zz_block(c, m) * 128 + 128) for m in range(8)])


def zz_mask(c):
    k = np.arange(128)[:, None]
    q = np.arange(128)[None, :]
    tri = (k <= q).astype(np.float32)
    mk = np.zeros((128, 8, 8, 128), np.float32)
    for m in range(8):
        bm = zz_block(c, m)
        for j in range(8):
            kb = 8 * m + j
            if kb < bm:
                mk[:, m, j, :] = 1.0
            elif kb == bm:
                mk[:, m, j, :] = tri
    return mk.astype(BF16_NP)


def run_a_layer(x, pos, mem, w_in, w_mem_kv, w_out, g_attn):
    res = _run(build_a1(), a1_host_inputs(x, pos, w_in, g_attn))
    att_tok = np.concatenate([r["attT"].T for r in res], axis=1)
    res = _run(build_a2(), a2_host_inputs(x, att_tok, w_in, w_mem_kv, mem, w_out, g_attn))
    return np.concatenate([r["xo"].T for r in res], axis=0)


def run_ffn(x, w_gate, w_val, conv_w, conv_b, w_down, g_ffn, g_final=None):
    res = _run(build_ffn(final_norm=g_final is not None),
               ffn_host_inputs(x, w_gate, w_val, conv_w, conv_b, w_down, g_ffn, g_final))
    xo = np.concatenate([r["xo"].T for r in res], axis=0)
    fin = None
    if g_final is not None:
        fin = np.concatenate([r["fin"].T for r in res], axis=0)
    return xo, fin


def run_kv(x, pos, g_norm, w_dkv, g_latent, w_uk, w_uv):
    rc = rope_consts_host(64, 32)
    maps = []
    for c in range(NCORES):
        sl = slice(c * TOK, (c + 1) * TOK)
        maps.append({"xT": np.ascontiguousarray(x[sl].T), "gx": gvec(g_norm), "gl": gvec(g_latent),
                     "posb": posb_of(pos[sl]), "rc": rc, "w1": pkn(w_dkv), "wuk": pkn(w_uk), "wuv": pkn(w_uv)})
    res = _run(build_proj("kv"), maps)
    knT = np.ascontiguousarray(np.concatenate([r["knT"] for r in res], axis=2))
    krT = np.ascontiguousarray(np.concatenate([r["krT"] for r in res], axis=1))
    v = np.concatenate([r["v"].reshape(TOK, NHB, 128) for r in res], axis=0)
    vv = np.ascontiguousarray(v.reshape(SEQ // 128, 128, NHB, 128).transpose(2, 1, 0, 3))
    return knT, krT, vv


def run_b_layer(x, pos, mem, kvs, w_in, g_qnorm, w_uq, w_mem_kv, w_out, g_attn):
    knT, krT, vv = kvs
    rc = rope_consts_host(64, 32)
    idx = [zz_index(c) for c in range(NCORES)]
    xTs = [np.ascontiguousarray(x[idx[c]].T) for c in range(NCORES)]
    maps = []
    for c in range(NCORES):
        maps.append({"xT": xTs[c], "gx": gvec(g_attn), "gl": gvec(g_qnorm), "posb": posb_of(pos[idx[c]]), "rc": rc,
                     "w1": pkn(w_in), "wuq": pkn(w_uq)})
    rq = _run(build_proj("q"), maps)
    memT = pkn(np.ascontiguousarray(mem.T))
    wmk = pkn(w_mem_kv)
    wo = pkn(w_out)
    maps = []
    for c in range(NCORES):
        maps.append({"qnT": rq[c]["qnT"], "qrT": rq[c]["qrT"], "qmT": rq[c]["qmT"], "knT": knT, "krT": krT, "vv": vv,
                     "mask": zz_mask(c), "memT": memT, "wmk": wmk, "wo": wo, "xT": xTs[c]})
    res = _run(build_b1b(), maps)
    xo = np.empty_like(x)
    for c in range(NCORES):
        xo[idx[c]] = res[c]["xo"].T
    return xo


def kernel(x, mem, positions, a_w_in, a_w_mem_kv, a_w_out, b_w_in, b_g_qnorm, b_w_uq, b_w_mem_kv, b_w_out,
           kv_g_norm, kv_w_dkv, kv_g_latent, kv_w_uk, kv_w_uv, g_attn, g_ffn, ffn_w_gate, ffn_w_val, ffn_conv_w,
           ffn_conv_b, ffn_w_down, g_final):
    f = lambda a: np.asarray(a, dtype=np.float32)
    xs = f(x)[0]
    memh = f(mem)[0]
    pos = np.asarray(positions)[0].astype(np.int32)
    fin = None
    kvs = None
    for layer in range(4):
        if layer < 2:
            xs = run_a_layer(xs, pos, memh, f(a_w_in[layer]), f(a_w_mem_kv[layer]), f(a_w_out[layer]), f(g_attn[layer]))
        else:
            if kvs is None:
                kvs = run_kv(xs, pos, f(kv_g_norm), f(kv_w_dkv), f(kv_g_latent), f(kv_w_uk), f(kv_w_uv))
            i = layer - 2
            xs = run_b_layer(xs, pos, memh, kvs, f(b_w_in[i]), f(b_g_qnorm[i]), f(b_w_uq[i]), f(b_w_mem_kv[i]),
                             f(b_w_out[i]), f(g_attn[layer]))
        xs, fin = run_ffn(xs, f(ffn_w_gate[layer]), f(ffn_w_val[layer]), f(ffn_conv_w[layer]), f(ffn_conv_b[layer]),
                          f(ffn_w_down[layer]), f(g_ffn[layer]), f(g_final) if layer == 3 else None)
    return np.ascontiguousarray(fin[None]).astype(np.float32)
```

```python
import numpy as np
from contextlib import ExitStack
import concourse.bass as bass
import concourse.mybir as mybir
from concourse.bass_utils import run_bass_kernel_spmd

F32 = mybir.dt.float32
BF16 = mybir.dt.bfloat16
I32 = mybir.dt.int32
AF = mybir.ActivationFunctionType
ALU = mybir.AluOpType

PE, ACT, DVE, POOL, SP = "pe", "act", "dve", "pool", "sp"
NCORES = 8
SAME_ENGINE_SYNC = True
N_DMA_SEMS = 8


class Op:
    __slots__ = ("eng", "fn", "deps", "signal", "ticket", "is_dma", "dsem", "dcount", "presem", "idx")

    def __init__(self, eng, fn, is_dma):
        self.eng = eng
        self.fn = fn
        self.deps = ()
        self.signal = False
        self.ticket = 0
        self.is_dma = is_dma
        self.dsem = None
        self.dcount = 0
        self.presem = None


class Prog:
    def __init__(self):
        self.nc = bass.Bass("TRN2", target_bir_lowering=False)
        self.ops = []
        self.last_w = {}
        self.rd_eng = {}
        self.rd_dma = {}
        self.stack = ExitStack()
        self.uid = 0

    def din(self, name, shape, dt=F32):
        return self.nc.dram_tensor(name, list(shape), dt, kind="ExternalInput").ap()

    def dout(self, name, shape, dt=F32):
        return self.nc.dram_tensor(name, list(shape), dt, kind="ExternalOutput").ap()

    def sb(self, name, shape, dt):
        return self.stack.enter_context(self.nc.sbuf_tensor(name, list(shape), dt))

    def ps(self, name, shape, dt=F32):
        return self.stack.enter_context(self.nc.psum_tensor(name, list(shape), dt))

    def op(self, eng, fn, reads=(), writes=(), is_dma=False):
        o = Op(eng, fn, is_dma)
        deps = set()
        for k in reads:
            w = self.last_w.get(k)
            if w is not None:
                deps.add(w)
        for k in writes:
            w = self.last_w.get(k)
            if w is not None:
                deps.add(w)
            for r in self.rd_eng.get(k, {}).values():
                deps.add(r)
            for r in self.rd_dma.get(k, ()):
                deps.add(r)
        for k in reads:
            if is_dma:
                self.rd_dma.setdefault(k, []).append(o)
            else:
                self.rd_eng.setdefault(k, {})[eng] = o
        for k in writes:
            self.last_w[k] = o
            self.rd_eng[k] = {}
            self.rd_dma[k] = []
        deps.discard(o)
        o.deps = tuple(deps)
        self.ops.append(o)
        return o

    def dma(self, queue, out, in_, reads=(), writes=()):
        return self.op(queue, lambda e: e.dma_start(out=out, in_=in_), reads, writes, is_dma=True)

    def build(self):
        nc = self.nc
        for i, o in enumerate(self.ops):
            o.idx = i
        for o in self.ops:
            latest = {}
            keep = []
            for d in o.deps:
                if d.is_dma:
                    keep.append(d)
                    continue
                if d.eng == o.eng and (d.eng == PE or not SAME_ENGINE_SYNC):
                    continue
                if d.eng not in latest or latest[d.eng].idx < d.idx:
                    latest[d.eng] = d
            for d in latest.values():
                d.signal = True
                keep.append(d)
            o.deps = tuple(keep)
        cnt = {}
        dma_i = {}
        for o in self.ops:
            if o.is_dma:
                i = dma_i.get(o.eng, 0)
                dma_i[o.eng] = i + 1
                o.dsem = (o.eng, i % N_DMA_SEMS)
                o.dcount = 16 * (i // N_DMA_SEMS + 1)
                if i >= N_DMA_SEMS:
                    o.presem = (o.dsem, o.dcount - 16)
            elif o.signal:
                cnt[o.eng] = cnt.get(o.eng, 0) + 1
                o.ticket = cnt[o.eng]
        sems = {}
        for eng in (PE, ACT, DVE, POOL):
            sems[eng] = self.stack.enter_context(nc.semaphore("s_" + eng))
        for q in (SP, POOL, ACT):
            if q in dma_i:
                for i in range(min(N_DMA_SEMS, dma_i[q])):
                    sems[(q, i)] = self.stack.enter_context(nc.semaphore("d_%s%d" % (q, i)))
        final_dma = {}
        for o in self.ops:
            if o.is_dma:
                final_dma[o.dsem] = o.dcount
        block = self.stack.enter_context(nc.Block())

        def make_body(eng_name):
            ops_e = [o for o in self.ops if o.eng == eng_name]

            def body(e):
                waited = {}
                for o in ops_e:
                    need = {}
                    for d in o.deps:
                        if d.is_dma:
                            k, v = d.dsem, d.dcount
                        else:
                            if d.eng == eng_name and (eng_name == PE or not SAME_ENGINE_SYNC):
                                continue
                            k, v = d.eng, d.ticket
                        if waited.get(k, 0) >= v:
                            continue
                        if need.get(k, 0) < v:
                            need[k] = v
                    if o.presem is not None:
                        k, v = o.presem
                        if waited.get(k, 0) < v and need.get(k, 0) < v:
                            need[k] = v
                    for k, v in need.items():
                        e.wait_ge(sems[k], v)
                        waited[k] = v
                    ins = o.fn(e)
                    if o.is_dma:
                        ins.then_inc(sems[o.dsem], 16)
                    elif o.signal:
                        ins.then_inc(sems[eng_name], 1)
                for k, v in final_dma.items():
                    if k[0] == eng_name and waited.get(k, 0) < v:
                        e.wait_ge(sems[k], v)
            return body

        for eng_name, deco in ((PE, block.tensor), (ACT, block.scalar), (DVE, block.vector),
                               (POOL, block.gpsimd), (SP, block.sync)):
            if any(o.eng == eng_name for o in self.ops):
                deco(make_body(eng_name))
        self.stack.close()
        return nc


def emit_consts(P):
    ones = P.sb("ones_bf", [128, 128], BF16)
    P.op(DVE, lambda e: e.memset(ones[:], 1.0), writes=["ones"])
    return ones


def emit_rmsnorm(P, gkey, xin, xkeys, KC, T, D, g_sb, out, okeys, ones, ps_ss, pskey, sq, rs1, rs2, eps=1e-6,
                 post=None, lnexp_eps=None):
    for kc in range(KC):
        P.op(ACT, lambda e, kc=kc: e.activation(out=sq[:, kc, 0:T], in_=xin(kc), func=AF.Square),
             reads=[xkeys[kc]], writes=[("nsq", kc)])

    def mm(e):
        ins = None
        for kc in range(KC):
            ins = e.matmul(ps_ss[:, 0:T], lhsT=ones[:], rhs=sq[:, kc, 0:T], start=(kc == 0), stop=(kc == KC - 1))
        return ins
    P.op(PE, mm, reads=[("nsq", kc) for kc in range(KC)] + ["ones"], writes=[pskey])
    if lnexp_eps is not None:
        P.op(ACT, lambda e: e.activation(out=rs1[:, 0:T], in_=ps_ss[:, 0:T], func=AF.Ln, scale=1.0 / D,
                                         bias=lnexp_eps[:, 0:1]), reads=[pskey, "epsc"], writes=["rs1"])
        P.op(ACT, lambda e: e.activation(out=rs2[:, 0:T], in_=rs1[:, 0:T], func=AF.Exp, scale=-0.5),
             reads=["rs1"], writes=["rs2"])
    else:
        P.op(ACT, lambda e: e.activation(out=rs1[:, 0:T], in_=ps_ss[:, 0:T], func=AF.Sqrt, scale=1.0 / D, bias=eps),
             reads=[pskey], writes=["rs1"])
        P.op(DVE, lambda e: e.reciprocal(out=rs2[:, 0:T], in_=rs1[:, 0:T]), reads=["rs1"], writes=["rs2"])
    for kc in range(KC):
        P.op(DVE, lambda e, kc=kc: e.scalar_tensor_tensor(out=out(kc), in0=xin(kc), scalar=g_sb[:, kc:kc + 1],
                                                          in1=rs2[:, 0:T], op0=ALU.mult, op1=ALU.mult),
             reads=[xkeys[kc], "rs2", gkey], writes=[okeys[kc]])
        if post is not None:
            post(kc)


TOK = 1024
DM = 2048
KC = 16
DFF = 5632
FG = 256
NG = DFF // FG
NFT = FG // 128


def build_ffn(final_norm=False):
    P = Prog()
    xT_d = P.din("xT", [DM, TOK])
    xh_d = P.din("xh", [DM, 2])
    wg_d = P.din("wg", [NG, 128, KC, FG])
    wv_d = P.din("wv", [NG, 128, KC, FG])
    wd_d = P.din("wd", [DFF, DM])
    cw_d = P.din("cw", [128, DFF // 128, 4])
    gf_d = P.din("gf", [128, KC])
    xo_d = P.dout("xo", [DM, TOK])
    if final_norm:
        gfin_d = P.din("gfin", [128, KC])
        fo_d = P.dout("fin", [DM, TOK])

    ones = emit_consts(P)
    xT = P.sb("xT_sb", [128, KC, TOK], F32)
    xh = P.sb("xh_sb", [128, KC, 2], F32)
    hT = P.sb("hT_sb", [128, KC, TOK], BF16)
    hh = P.sb("hh_sb", [128, KC, 2], BF16)
    gf = P.sb("gf_sb", [128, KC], F32)
    cw = P.sb("cw_sb", [128, DFF // 128, 4], F32)
    sq = P.sb("sq_sb", [128, KC, 512], BF16)
    rs1 = P.sb("rs1_sb", [128, 512], F32)
    rs2 = P.sb("rs2_sb", [128, 512], F32)
    wg = [P.sb("wg_sb%d" % i, [128, KC, FG], BF16) for i in range(2)]
    wv = [P.sb("wv_sb%d" % i, [128, KC, FG], BF16) for i in range(2)]
    wd = [P.sb("wd_sb%d" % i, [128, NFT, DM], BF16) for i in range(2)]
    gsb = [P.sb("g_sb%d" % i, [128, TOK + 2], F32) for i in range(2)]
    t1 = [P.sb("t1_sb%d" % i, [128, TOK], F32) for i in range(2)]
    t2 = [P.sb("t2_sb%d" % i, [128, TOK], F32) for i in range(2)]
    uT = [P.sb("uT_sb%d" % i, [128, NFT, TOK], BF16) for i in range(2)]
    pg = [P.ps("pg%d" % i, [128, 512]) for i in range(2)]
    pv = [P.ps("pv%d" % i, [128, 512]) for i in range(2)]
    pd = [P.ps("pd%d" % i, [128, 512]) for i in range(2)]
    ph = P.ps("ph", [128, 512])

    xT_v = xT_d.rearrange("(k p) t -> p k t", p=128)
    for kc in range(KC):
        P.dma(SP, xT[:, kc, :], xT_v[:, kc, :], writes=[("x", kc, 0), ("x", kc, 1)])
    P.dma(SP, xh[:, :, :], xh_d.rearrange("(k p) t -> p k t", p=128), writes=["xh"])
    P.dma(SP, gf[:, :], gf_d[:, :], writes=["gf"])
    P.dma(SP, cw[:, :, :], cw_d[:, :, :], writes=["cw"])

    def load_w(G):
        b = G % 2
        P.dma(POOL, wg[b][:, :, :], wg_d[G], writes=[("wg", b)])
        P.dma(POOL, wv[b][:, :, :], wv_d[G], writes=[("wv", b)])

    def load_wd(G):
        b = G % 2
        P.dma(POOL, wd[b][:, :, :], wd_d[G * FG:(G + 1) * FG, :].rearrange("(f p) n -> p f n", p=128),
              writes=[("wd", b)])

    load_w(0)
    load_wd(0)
    for tt in range(2):
        emit_rmsnorm(P, "gf", lambda kc, tt=tt: xT[:, kc, tt * 512:(tt + 1) * 512], [("x", kc, tt) for kc in range(KC)],
                     KC, 512, DM, gf, lambda kc, tt=tt: hT[:, kc, tt * 512:(tt + 1) * 512],
                     [("h", kc, tt) for kc in range(KC)], ones, ph, "ph", sq, rs1, rs2)
    emit_rmsnorm(P, "gf", lambda kc: xh[:, kc, :], ["xh"] * KC, KC, 2, DM, gf, lambda kc: hh[:, kc, :],
                 [("hh", kc) for kc in range(KC)], ones, ph, "ph", sq, rs1, rs2)

    hkeys = lambda tt: [("h", kc, tt) for kc in range(KC)]
    hhkeys = [("hh", kc) for kc in range(KC)]

    def down(G):
        b = G % 2
        for fo in range(KC):
            for tt in range(2):
                i = (fo * 2 + tt) % 2

                def mm(e, fo=fo, tt=tt, i=i, b=b):
                    ins = None
                    for ft in range(NFT):
                        ins = e.matmul(pd[i][:, :], lhsT=wd[b][:, ft, fo * 128:(fo + 1) * 128],
                                       rhs=uT[b][:, ft, tt * 512:(tt + 1) * 512], start=(ft == 0), stop=(ft == NFT - 1))
                    return ins
                P.op(PE, mm, reads=[("wd", b)] + [("u", b, ft, tt) for ft in range(NFT)], writes=[("pd", i)])
                P.op(DVE, lambda e, fo=fo, tt=tt, i=i: e.tensor_tensor(
                    out=xT[:, fo, tt * 512:(tt + 1) * 512], in0=pd[i][:, :], in1=xT[:, fo, tt * 512:(tt + 1) * 512],
                    op=ALU.add), reads=[("pd", i), ("x", fo, tt)], writes=[("x", fo, tt)])

    for G in range(NG):
        b = G % 2
        if G + 1 < NG:
            load_w(G + 1)
        for ft in range(NFT):
            fi = G * NFT + ft
            gb = fi % 2
            for tt in range(2):
                def mmg(e, tt=tt, ft=ft, b=b):
                    ins = None
                    for kc in range(KC):
                        ins = e.matmul(pg[tt][:, :], lhsT=wg[b][:, kc, ft * 128:(ft + 1) * 128],
                                       rhs=hT[:, kc, tt * 512:(tt + 1) * 512], start=(kc == 0), stop=(kc == KC - 1))
                    return ins
                P.op(PE, mmg, reads=[("wg", b)] + hkeys(tt), writes=[("pg", tt)])
                P.op(ACT, lambda e, tt=tt, gb=gb: e.activation(out=gsb[gb][:, 2 + tt * 512:2 + (tt + 1) * 512],
                                                               in_=pg[tt][:, :], func=AF.Copy),
                     reads=[("pg", tt)], writes=[("gsb", gb, tt)])

            def mmh(e, ft=ft, b=b):
                ins = None
                for kc in range(KC):
                    ins = e.matmul(ph[:, 0:2], lhsT=wg[b][:, kc, ft * 128:(ft + 1) * 128], rhs=hh[:, kc, :],
                                   start=(kc == 0), stop=(kc == KC - 1))
                return ins
            P.op(PE, mmh, reads=[("wg", b)] + hhkeys, writes=["ph"])
            P.op(ACT, lambda e, gb=gb: e.activation(out=gsb[gb][:, 0:2], in_=ph[:, 0:2], func=AF.Copy),
                 reads=["ph"], writes=[("gsb", gb, "h")])
            for tt in range(2):
                def mmv(e, tt=tt, ft=ft, b=b):
                    ins = None
                    for kc in range(KC):
                        ins = e.matmul(pv[tt][:, :], lhsT=wv[b][:, kc, ft * 128:(ft + 1) * 128],
                                       rhs=hT[:, kc, tt * 512:(tt + 1) * 512], start=(kc == 0), stop=(kc == KC - 1))
                    return ins
                P.op(PE, mmv, reads=[("wv", b)] + hkeys(tt), writes=[("pv", tt)])
            if ft == 0:
                if G > 0:
                    down(G - 1)
                if G + 1 < NG:
                    load_wd(G + 1)
            gk = [("gsb", gb, 0), ("gsb", gb, 1), ("gsb", gb, "h")]
            P.op(DVE, lambda e, gb=gb, fi=fi: e.tensor_scalar(out=t1[gb][:, :], in0=gsb[gb][:, 2:TOK + 2],
                                                              scalar1=cw[:, fi, 2:3], scalar2=cw[:, fi, 3:4],
                                                              op0=ALU.mult, op1=ALU.add),
                 reads=gk + ["cw"], writes=[("t1", gb)])
            P.op(DVE, lambda e, gb=gb, fi=fi: e.scalar_tensor_tensor(out=t2[gb][:, :], in0=gsb[gb][:, 1:TOK + 1],
                                                                     scalar=cw[:, fi, 1:2], in1=t1[gb][:, :],
                                                                     op0=ALU.mult, op1=ALU.add),
                 reads=gk + ["cw", ("t1", gb)], writes=[("t2", gb)])
            P.op(DVE, lambda e, gb=gb, fi=fi: e.scalar_tensor_tensor(out=t1[gb][:, :], in0=gsb[gb][:, 0:TOK],
                                                                     scalar=cw[:, fi, 0:1], in1=t2[gb][:, :],
                                                                     op0=ALU.mult, op1=ALU.add),
                 reads=gk + ["cw", ("t2", gb)], writes=[("t1", gb)])
            P.op(ACT, lambda e, gb=gb: e.activation(out=t2[gb][:, :], in_=t1[gb][:, :], func=AF.Silu),
                 reads=[("t1", gb)], writes=[("t2", gb)])
            for tt in range(2):
                P.op(DVE, lambda e, gb=gb, tt=tt, ft=ft, b=b: e.tensor_tensor(
                    out=uT[b][:, ft, tt * 512:(tt + 1) * 512], in0=pv[tt][:, :], in1=t2[gb][:, tt * 512:(tt + 1) * 512],
                    op=ALU.mult), reads=[("pv", tt), ("t2", gb)], writes=[("u", b, ft, tt)])
    down(NG - 1)

    xo_v = xo_d.rearrange("(k p) t -> p k t", p=128)
    for kc in range(KC):
        P.dma(SP, xo_v[:, kc, :], xT[:, kc, :], reads=[("x", kc, 0), ("x", kc, 1)])
    if final_norm:
        gfin = P.sb("gfin_sb", [128, KC], F32)
        P.dma(SP, gfin[:, :], gfin_d[:, :], writes=["gfin"])
        stg = [(t1[0], ("t1", 0)), (t1[1], ("t1", 1)), (t2[0], ("t2", 0)), (t2[1], ("t2", 1))]
        fo_v = fo_d.rearrange("(k p) t -> p k t", p=128)
        for tt in range(2):
            emit_rmsnorm(P, "gfin", lambda kc, tt=tt: xT[:, kc, tt * 512:(tt + 1) * 512],
                         [("x", kc, tt) for kc in range(KC)], KC, 512, DM, gfin, lambda kc: stg[kc % 4][0][:, 0:512],
                         [stg[kc % 4][1] for kc in range(KC)], ones, ph, "ph", sq, rs1, rs2, post=lambda kc, tt=tt: P.dma(
                             SP, fo_v[:, kc, tt * 512:(tt + 1) * 512], stg[kc % 4][0][:, 0:512], reads=[stg[kc % 4][1]]))
    return P.build()


def ffn_host_inputs(x_tok, w_gate, w_val, conv_w, conv_b, w_down, g_ffn, g_final=None):
    S = x_tok.shape[0]
    wg = np.ascontiguousarray(w_gate.reshape(KC, 128, NG, FG).transpose(2, 1, 0, 3))
    wv = np.ascontiguousarray(w_val.reshape(KC, 128, NG, FG).transpose(2, 1, 0, 3))
    cw = np.ascontiguousarray(np.concatenate([conv_w, conv_b[None, :]], axis=0).reshape(4, DFF // 128, 128).transpose(2, 1, 0))
    gf = np.ascontiguousarray(g_ffn.reshape(KC, 128).T)
    maps = []
    for c in range(NCORES):
        xs = x_tok[c * TOK:(c + 1) * TOK]
        halo = np.zeros((2, DM), np.float32)
        if c > 0:
            halo[:] = x_tok[c * TOK - 2:c * TOK]
        m = {"xT": np.ascontiguousarray(xs.T), "xh": np.ascontiguousarray(halo.T), "wg": wg, "wv": wv,
             "wd": w_down, "cw": cw, "gf": gf}
        if g_final is not None:
            m["gfin"] = np.ascontiguousarray(g_final.reshape(KC, 128).T)
        maps.append(m)
    return maps


TWO_PI = float(2.0 * np.pi)


def emit_rope_tables(P, NP, T, pos_ap, poskey, inv, nsgn, negpi, posf, ang, r1, r2, cos2, sinS, kpre, ki):
    P.op(DVE, lambda e: e.tensor_copy(out=posf[0:NP, 0:T], in_=pos_ap), reads=[poskey], writes=["rt_posf"])
    P.op(DVE, lambda e: e.tensor_scalar(out=ang[0:NP, 0:T], in0=posf[0:NP, 0:T], scalar1=inv[0:NP, 0:1], scalar2=None,
                                        op0=ALU.mult), reads=["rt_posf", "ropec"], writes=["rt_ang"])
    for which, (rr, dst) in enumerate(((r1, sinS), (r2, cos2))):
        rk = ("rt_r", which)
        if which == 1:
            P.op(DVE, lambda e: e.tensor_scalar(out=ang[0:NP, 0:T], in0=ang[0:NP, 0:T], scalar1=0.25, scalar2=None,
                                                op0=ALU.add), reads=["rt_ang"], writes=["rt_ang"])
        P.op(DVE, lambda e: e.tensor_copy(out=ki[0:NP, 0:T], in_=ang[0:NP, 0:T]), reads=["rt_ang"], writes=["rt_ki"])
        P.op(DVE, lambda e, rr=rr: e.tensor_copy(out=rr[0:NP, 0:T], in_=ki[0:NP, 0:T]), reads=["rt_ki"], writes=[rk])
        P.op(DVE, lambda e, rr=rr: e.tensor_tensor(out=rr[0:NP, 0:T], in0=ang[0:NP, 0:T], in1=rr[0:NP, 0:T],
                                                   op=ALU.subtract), reads=["rt_ang", rk], writes=[rk])
        P.op(DVE, lambda e, rr=rr: e.scalar_tensor_tensor(out=rr[0:NP, 0:T], in0=rr[0:NP, 0:T], scalar=0.5,
                                                          in1=rr[0:NP, 0:T], op0=ALU.is_ge, op1=ALU.subtract),
             reads=[rk], writes=[rk])
        if which == 0:
            P.op(ACT, lambda e, rr=rr, dst=dst: e.activation(out=dst[0:NP, 0:T], in_=rr[0:NP, 0:T], func=AF.Sin,
                                                             scale=nsgn[0:NP, 0:1]),
                 reads=[rk, "ropec"], writes=[(kpre, "sinS")])
        else:
            P.op(ACT, lambda e, rr=rr, dst=dst: e.activation(out=dst[0:NP, 0:T], in_=rr[0:NP, 0:T], func=AF.Sin,
                                                             scale=-TWO_PI),
                 reads=[rk], writes=[(kpre, "cos2")])


def emit_rope_apply(P, NP, T, src, srckey, cos2, sinS, kpre, qf, qsw, ta, tb, tkey, out_ap, outkeys, in_view=None,
                    eng2=POOL):
    H = NP // 2
    P.op(ACT, lambda e: e.activation(out=qsw[0:H, 0:T], in_=src(H, NP), func=AF.Copy),
         reads=[srckey], writes=[(tkey, "qsw0")])
    P.op(ACT, lambda e: e.activation(out=qsw[H:NP, 0:T], in_=src(0, H), func=AF.Copy),
         reads=[srckey], writes=[(tkey, "qsw1")])
    P.op(DVE, lambda e: e.tensor_tensor(out=ta[0:NP, 0:T], in0=src(0, NP), in1=cos2[0:NP, 0:T], op=ALU.mult),
         reads=[(kpre, "cos2")], writes=[(tkey, "a"), srckey])
    P.op(eng2, lambda e: e.tensor_tensor(out=tb[0:NP, 0:T], in0=qsw[0:NP, 0:T], in1=sinS[0:NP, 0:T], op=ALU.mult),
         reads=[(tkey, "qsw0"), (tkey, "qsw1"), (kpre, "sinS")], writes=[(tkey, "b")])
    v = in_view if in_view is not None else (lambda a: a)
    P.op(DVE, lambda e: e.tensor_tensor(out=out_ap, in0=v(ta[0:NP, 0:T]), in1=v(tb[0:NP, 0:T]), op=ALU.add),
         reads=[(tkey, "a"), (tkey, "b")], writes=outkeys)


def rope_consts_host(NP, half):
    j = np.arange(NP) % half
    inv = (np.float32(10000.0) ** (-(j.astype(np.float32)) / np.float32(half))).astype(np.float32)
    sgn = np.where(np.arange(NP) % (2 * half) < half, -1.0, 1.0).astype(np.float32)
    c = np.zeros((128, 4), np.float32)
    c[:NP, 0] = (inv.astype(np.float64) / (2.0 * np.pi)).astype(np.float32)
    c[:NP, 1] = (-2.0 * np.pi * sgn).astype(np.float32)
    return c


MEMT = 256
MH = 4


def emit_mem_kv(P, memT_d, wmk_d, ones, wbuf, wkeyfn, pj, pjkeys):
    memT = P.sb("memT_sb", [128, KC, MEMT], BF16)
    mkT = P.sb("mkT_sb", [128, MH, MEMT], BF16)
    mv = P.sb("mv_sb", [128, 2, MH * 128], BF16)
    P.dma(POOL, memT[:, :, :], memT_d[:, :, :], writes=["memT"])
    for part in range(2):
        P.dma(POOL, wbuf[:, :, :], wmk_d[:, :, part * 512:(part + 1) * 512], writes=[wkeyfn()])
        if part == 0:
            for h in range(MH):
                i = h % 2

                def mm(e, h=h, i=i):
                    ins = None
                    for kc in range(KC):
                        ins = e.matmul(pj[i][:, 0:MEMT], lhsT=wbuf[:, kc, h * 128:(h + 1) * 128], rhs=memT[:, kc, :],
                                       start=(kc == 0), stop=(kc == KC - 1))
                    return ins
                P.op(PE, mm, reads=[wkeyfn(), "memT"], writes=[pjkeys[i]])
                P.op(ACT, lambda e, h=h, i=i: e.activation(out=mkT[:, h, :], in_=pj[i][:, 0:MEMT], func=AF.Copy),
                     reads=[pjkeys[i]], writes=[("mkT", h)])
        else:
            for mt in range(2):
                i = mt % 2

                def mm(e, mt=mt, i=i):
                    ins = None
                    for kc in range(KC):
                        ins = e.matmul(pj[i][:, :], lhsT=memT[:, kc, mt * 128:(mt + 1) * 128], rhs=wbuf[:, kc, :],
                                       start=(kc == 0), stop=(kc == KC - 1))
                    return ins
                P.op(PE, mm, reads=[wkeyfn(), "memT"], writes=[pjkeys[i]])
                P.op(ACT, lambda e, mt=mt, i=i: e.activation(out=mv[:, mt, :], in_=pj[i][:, :], func=AF.Copy),
                     reads=[pjkeys[i]], writes=[("mv", mt)])
    return mkT, mv


def emit_mem_attn(P, qmT, qmkeyfn, mkT, mv, ones, headsT, hbase, pS, pO, pL, PT, rinv, pOkey="pO", PTkey="PTm"):
    scale = 128.0 ** -0.5
    for h in range(MH):
        for tt in range(TOK // 512):
            sl = slice(tt * 512, (tt + 1) * 512)
            for mt in range(2):
                P.op(PE, lambda e, h=h, mt=mt, sl=sl: e.matmul(pS[mt][:, :], lhsT=mkT[:, h, mt * 128:(mt + 1) * 128],
                                                              rhs=qmT[:, h, sl], start=True, stop=True),
                     reads=[("mkT", h), qmkeyfn(h, tt)], writes=[("pS", mt)])
                P.op(ACT, lambda e, mt=mt: e.activation(out=PT[:, mt, :], in_=pS[mt][:, :], func=AF.Exp, scale=scale),
                     reads=[("pS", mt)], writes=[(PTkey, mt)])

            def mmo(e, h=h):
                ins = None
                for mt in range(2):
                    ins = e.matmul(pO[:, :], lhsT=mv[:, mt, h * 128:(h + 1) * 128], rhs=PT[:, mt, :],
                                   start=(mt == 0), stop=(mt == 1))
                return ins
            P.op(PE, mmo, reads=[("mv", 0), ("mv", 1), (PTkey, 0), (PTkey, 1)], writes=[pOkey])

            def mml(e):
                ins = None
                for mt in range(2):
                    ins = e.matmul(pL[:, :], lhsT=ones[:], rhs=PT[:, mt, :], start=(mt == 0), stop=(mt == 1))
                return ins
            P.op(PE, mml, reads=["ones", (PTkey, 0), (PTkey, 1)], writes=["pL"])
            P.op(DVE, lambda e: e.reciprocal(out=rinv[:, :], in_=pL[:, :]), reads=["pL"], writes=["rinv"])
            P.op(DVE, lambda e, h=h, sl=sl: e.tensor_tensor(out=headsT[:, hbase + h, sl], in0=pO[:, :], in1=rinv[:, :],
                                                            op=ALU.mult),
                 reads=[pOkey, "rinv"], writes=[("heads", hbase + h, tt)])


def emit_out_proj(P, headsT, NHC, wo_d, wo_bufs, xres, xkeyfn, pj, pjkeys):
    cnt = 0
    for fq in range(DM // 512):
        b = fq % 2
        P.dma(POOL, wo_bufs[b][:, :, :], wo_d[:, :, fq * 512:(fq + 1) * 512], writes=[("wo", b)])
        for f4 in range(4):
            fo = fq * 4 + f4
            for tt in range(TOK // 512):
                i = cnt % 2
                cnt += 1
                sl = slice(tt * 512, (tt + 1) * 512)

                def mm(e, f4=f4, sl=sl, i=i, b=b):
                    ins = None
                    for kc in range(NHC):
                        ins = e.matmul(pj[i][:, :], lhsT=wo_bufs[b][:, kc, f4 * 128:(f4 + 1) * 128], rhs=headsT[:, kc, sl],
                                       start=(kc == 0), stop=(kc == NHC - 1))
                    return ins
                P.op(PE, mm, reads=[("wo", b)] + [("heads", kc, tt) for kc in range(NHC)], writes=[pjkeys[i]])
                P.op(DVE, lambda e, fo=fo, sl=sl, i=i: e.tensor_tensor(out=xres[:, fo, sl], in0=pj[i][:, :],
                                                                       in1=xres[:, fo, sl], op=ALU.add),
                     reads=[pjkeys[i], xkeyfn(fo, tt)], writes=[xkeyfn(fo, tt)])


def build_a2():
    P = Prog()
    NHC = 12
    xT_d = P.din("xT", [DM, TOK])
    att_d = P.din("attT", [8 * 128, TOK], BF16)
    wq_d = P.din("wq", [128, KC, 512])
    wmk_d = P.din("wmk", [128, KC, 1024])
    memT_d = P.din("memT", [128, KC, MEMT])
    wo_d = P.din("wo", [128, NHC, DM])
    ga_d = P.din("ga", [128, KC])
    xo_d = P.dout("xo", [DM, TOK])

    ones = emit_consts(P)
    xT = P.sb("xT_sb", [128, KC, TOK], F32)
    hT = P.sb("hT_sb", [128, KC, TOK], BF16)
    ga = P.sb("ga_sb", [128, KC], F32)
    sq = P.sb("sq_sb", [128, KC, 512], BF16)
    rs1 = P.sb("rs1_sb", [128, 512], F32)
    rs2 = P.sb("rs2_sb", [128, 512], F32)
    wbuf = P.sb("wbuf_sb", [128, KC, 512], BF16)
    qmT = P.sb("qmT_sb", [128, MH, TOK], BF16)
    headsT = P.sb("headsT_sb", [128, NHC, TOK], BF16)
    wo_bufs = [P.sb("wo_sb%d" % i, [128, NHC, 512], BF16) for i in range(2)]
    PT = P.sb("PT_sb", [128, 2, 512], BF16)
    rinv = P.sb("rinv_sb", [128, 512], F32)
    pj = [P.ps("pj%d" % i, [128, 512]) for i in range(2)]
    pS = [P.ps("pS%d" % i, [128, 512]) for i in range(2)]
    pO = P.ps("pO", [128, 512])
    pL = P.ps("pL", [128, 512])
    ph = P.ps("ph", [128, 512])
    pjkeys = [("pj", 0), ("pj", 1)]
    wver = [0]

    xT_v = xT_d.rearrange("(k p) t -> p k t", p=128)
    for kc in range(KC):
        P.dma(SP, xT[:, kc, :], xT_v[:, kc, :], writes=[("x", kc, 0), ("x", kc, 1)])
    P.dma(SP, ga[:, :], ga_d[:, :], writes=["ga"])
    att_v = att_d.rearrange("(k p) t -> p k t", p=128)
    for kc in range(8):
        P.dma(SP, headsT[:, kc, :], att_v[:, kc, :], writes=[("heads", kc, 0), ("heads", kc, 1)])
    P.dma(POOL, wbuf[:, :, :], wq_d[:, :, :], writes=["wbuf"])
    for tt in range(2):
        emit_rmsnorm(P, "ga", lambda kc, tt=tt: xT[:, kc, tt * 512:(tt + 1) * 512], [("x", kc, tt) for kc in range(KC)],
                     KC, 512, DM, ga, lambda kc, tt=tt: hT[:, kc, tt * 512:(tt + 1) * 512],
                     [("h", kc, tt) for kc in range(KC)], ones, ph, "ph", sq, rs1, rs2)
    cnt = 0
    for h in range(MH):
        for tt in range(2):
            i = cnt % 2
            cnt += 1
            sl = slice(tt * 512, (tt + 1) * 512)

            def mm(e, h=h, sl=sl, i=i):
                ins = None
                for kc in range(KC):
                    ins = e.matmul(pj[i][:, :], lhsT=wbuf[:, kc, h * 128:(h + 1) * 128], rhs=hT[:, kc, sl],
                                   start=(kc == 0), stop=(kc == KC - 1))
                return ins
            P.op(PE, mm, reads=["wbuf"] + [("h", kc, tt) for kc in range(KC)], writes=[pjkeys[i]])
            P.op(ACT, lambda e, h=h, sl=sl, i=i: e.activation(out=qmT[:, h, sl], in_=pj[i][:, :], func=AF.Copy),
                 reads=[pjkeys[i]], writes=[("qm", h, tt)])
    mkT, mv = emit_mem_kv(P, memT_d, wmk_d, ones, wbuf, lambda: "wbuf", pj, pjkeys)
    emit_mem_attn(P, qmT, lambda h, tt: ("qm", h, tt), mkT, mv, ones, headsT, 8, pS, pO, pL, PT, rinv)
    emit_out_proj(P, headsT, NHC, wo_d, wo_bufs, xT, lambda fo, tt: ("x", fo, tt), pj, pjkeys)
    xo_v = xo_d.rearrange("(k p) t -> p k t", p=128)
    for kc in range(KC):
        P.dma(SP, xo_v[:, kc, :], xT[:, kc, :], reads=[("x", kc, 0), ("x", kc, 1)])
    return P.build()


def pkn(w):
    K_ = w.shape[0] // 128
    return np.ascontiguousarray(w.reshape(K_, 128, w.shape[1]).transpose(1, 0, 2))


def gvec(g):
    return np.ascontiguousarray(g.reshape(-1, 128).T)


def a2_host_inputs(x_tok, att_tok_bf16, w_in, w_mem_kv, mem, w_out, g_attn):
    wq = pkn(w_in[:, 9216:9728])
    wmk = pkn(w_mem_kv)
    memT = pkn(np.ascontiguousarray(mem.T))
    wo = pkn(w_out)
    ga = gvec(g_attn)
    maps = []
    for c in range(NCORES):
        sl = slice(c * TOK, (c + 1) * TOK)
        maps.append({"xT": np.ascontiguousarray(x_tok[sl].T), "attT": np.ascontiguousarray(att_tok_bf16[sl].T),
                     "wq": wq, "wmk": wmk, "memT": memT, "wo": wo, "ga": ga})
    return maps


SEQ = 8192
DIL = (1, 4, 16)
A1_TT = 256
A1_SPAN = 2048


def build_a1():
    P = Prog()
    TT = A1_TT
    TPS = A1_SPAN // TT
    NSP = SEQ // A1_SPAN
    xT_d = P.din("xT", [SEQ // A1_TT, 128, KC, A1_TT])
    w_d = P.din("w", [128, KC, 1152])
    ga_d = P.din("ga", [128, KC])
    posb_d = P.din("posb", [128, SEQ], I32)
    rc_d = P.din("rc", [128, 4])
    mk_d = P.din("mk", [128, 2, 2, 128])
    id_d = P.din("ident", [128, 128])
    att_d = P.dout("attT", [128, SEQ], BF16)

    ones = emit_consts(P)
    w_sb = P.sb("w_sb", [128, KC, 1152], BF16)
    ga = P.sb("ga_sb", [128, KC], F32)
    rc = P.sb("rc_sb", [128, 4], F32)
    mk = P.sb("mk_sb", [128, 2, 2, 128], BF16)
    ident = P.sb("ident_sb", [128, 128], BF16)
    x_sb = [P.sb("x_sb0", [128, KC, TT], F32)]
    sq = P.sb("sq_sb", [128, KC, TT], BF16)
    hT = [P.sb("hT_sb%d" % i, [128, KC, 2 * TT], BF16) for i in range(2)]
    rs1 = P.sb("rs1_sb", [128, TT], F32)
    rs2 = P.sb("rs2_sb", [128, TT], F32)
    posi = P.sb("posi_sb", [128, TT], I32)
    posf = P.sb("posf_sb", [128, TT], F32)
    kint = P.sb("kint_sb", [128, TT], I32)
    ang = P.sb("ang_sb", [128, TT], F32)
    r1 = P.sb("r1_sb", [128, TT], F32)
    r2 = P.sb("r2_sb", [128, TT], F32)
    cos2 = [P.sb("cos2_sb%d" % i, [128, TT], F32) for i in range(4)]
    sinS = [P.sb("sinS_sb%d" % i, [128, TT], F32) for i in range(4)]
    ta = [P.sb("ta_sb%d" % i, [128, TT], F32) for i in range(2)]
    tb = [P.sb("tb_sb%d" % i, [128, TT], F32) for i in range(2)]
    qkf = [None, None]
    qsw = [P.sb("qsw_sb%d" % i, [128, TT], F32) for i in range(2)]
    QT = [P.sb("QT_sb%d" % g, [128, A1_SPAN], BF16) for g in range(3)]
    KT = [[P.sb("KT_sb%d_%d" % (p, g), [128, A1_SPAN], BF16) for g in range(3)] for p in range(2)]
    VT = [P.sb("VT_sb%d" % g, [128, A1_SPAN], BF16) for g in range(3)]
    V = [P.sb("V_sb%d" % p, [128, 3, 16, 128], BF16) for p in range(2)]
    acc = P.sb("acc_sb", [128, A1_SPAN], F32)
    lacc = P.sb("lacc_sb", [128, A1_SPAN], F32)
    ob = P.sb("ob_sb", [128, A1_SPAN // 2], BF16)
    PT = [P.sb("PT_sb%d" % i, [128, 2, 128], BF16) for i in range(2)]
    ph = P.ps("ph", [128, 512])
    pjb = [P.ps("pjb%d" % i, [128, 512]) for i in range(2)]
    pSb = [P.ps("pSb%d" % i, [128, 4, 128]) for i in range(2)]
    pOL = [P.ps("pOL%d" % i, [128, 4, 128]) for i in range(2)]
    pTb = P.ps("pTb", [128, 8, 128], BF16)
    pj = [pjb[0][:, :], pjb[1][:, :], pSb[0][:, :, :].rearrange("p a b -> p (a b)"),
          pSb[1][:, :, :].rearrange("p a b -> p (a b)")]
    pjkeys = [("pj", 0), ("pj", 1), ("pS", 0), ("pS", 1)]
    pT = [pTb[:, 0, :], ph[:, :].bitcast(BF16)[:, 0:128]]
    pTk = ["pT", "ph"]

    epsc = P.sb("epsc_sb", [128, 1], F32)
    P.op(DVE, lambda e: e.memset(epsc[:, :], 1e-6), writes=["epsc"])
    P.dma(POOL, w_sb[:, :, :], w_d[:, :, :], writes=["w"])
    P.dma(SP, ga[:, :], ga_d[:, :], writes=["ga"])
    P.dma(SP, rc[:, :], rc_d[:, :], writes=["ropec"])
    P.dma(POOL, mk[:, :, :, :], mk_d[:, :, :, :], writes=["mk"])
    P.dma(POOL, ident[:, :], id_d[:, :], writes=["ident"])
    for g in range(3):
        P.op(DVE, lambda e, g=g: e.memset(KT[1][g][:, :], 0.0), writes=[("KT", 1, g)])
    P.op(DVE, lambda e: e.memset(V[1][:, :, :, :], 0.0), writes=[("V", 1, g, b) for g in range(3) for b in range(16)])

    NT = SEQ // TT

    def load(t):
        P.dma(SP, x_sb[0][:, :, :], xT_d[t], writes=[("x", 0)])

    def norm(t):
        hb = (t // 2) % 2
        hf = t % 2
        ts = t % 4
        emit_rmsnorm(P, "ga", lambda kc: x_sb[0][:, kc, :], [("x", 0)] * KC, KC, TT, DM, ga,
                     lambda kc: hT[hb][:, kc, hf * TT:(hf + 1) * TT], [("h", hb, hf, kc) for kc in range(KC)], ones, ph,
                     "ph", sq, rs1, rs2, lnexp_eps=epsc)
        P.dma(SP, posi[:, :], posb_d[:, t * TT:(t + 1) * TT], writes=["posi"])
        emit_rope_tables(P, 128, TT, posi[:, :], "posi", rc[:, 0:1], rc[:, 1:2], rc[:, 2:3], posf, ang, r1, r2,
                         cos2[ts], sinS[ts], ("rt", ts), kint)

    pjc = [0]

    def proj(p, units):
        hb = p % 2
        n = (2 * p) // TPS
        par = n % 2
        hk = [("h", hb, hf, kc) for hf in range(2) for kc in range(KC)]
        for g in range(3):
            d = DIL[g]
            for tq in range(3):
                if (g * 3 + tq) not in units:
                    continue
                col = (g * 3 + tq) * 128
                s = pjc[0] % 4
                pjc[0] += 1

                def mm(e, col=col, s=s, hb=hb):
                    ins = None
                    for kc in range(KC):
                        ins = e.matmul(pj[s], lhsT=w_sb[:, kc, col:col + 128], rhs=hT[hb][:, kc, :],
                                       start=(kc == 0), stop=(kc == KC - 1))
                    return ins
                P.op(PE, mm, reads=["w"] + hk, writes=[pjkeys[s]])
                i0 = (2 * p) % TPS
                if tq == 2:
                    P.op(ACT, lambda e, g=g, i0=i0, s=s: e.activation(out=VT[g][:, i0 * TT:(i0 + 2) * TT], in_=pj[s],
                                                                      func=AF.Copy),
                         reads=[pjkeys[s]], writes=[("VT", g, i0), ("VT", g, i0 + 1)])
                    continue
                for hf in range(2):
                    t = 2 * p + hf
                    i = t % TPS
                    ts = t % 4
                    s2 = rtc[0] % 2
                    rtc[0] += 1
                    dst = QT[g] if tq == 0 else KT[par][g]
                    dkey = ("QT", g, i) if tq == 0 else ("KT", par, g, i)
                    if d == 1:
                        out_ap = dst[:, i * TT:(i + 1) * TT]
                        view = None
                    else:
                        a = TT // d
                        out_ap = dst[:, :].rearrange("p (r a) -> p r a", r=d)[:, :, a * i:a * (i + 1)]
                        view = (lambda ap, d=d: ap.rearrange("p (a r) -> p r a", r=d))
                    emit_rope_apply(P, 128, TT, lambda lo, hi, s=s, hf=hf: pj[s][lo:hi, hf * TT:(hf + 1) * TT], pjkeys[s],
                                    cos2[ts], sinS[ts], ("rt", ts), qkf[s2], qsw[s2], ta[s2], tb[s2], ("rtmp", s2),
                                    out_ap, [dkey], in_view=view, eng2=POOL)

    rtc = [0]
    sc = [0]
    scale = 128.0 ** -0.5

    def attention(n):
        par = n % 2
        tcount = 0
        for g in range(3):
            d = DIL[g]
            nb = 16 // d
            for r in range(d):
                for m in range(nb):
                    blk = r * nb + m
                    s = tcount % 2
                    tcount += 1
                    st = r + d * 128 * m
                    P.op(PE, lambda e, g=g, s=s, st=st, d=d: e.transpose(out=pT[s],
                                                                         in_=VT[g][:, st:st + d * 127 + 1:d],
                                                                         identity=ident[:, :]),
                         reads=[("VT", g, i) for i in range(TPS)] + ["ident"], writes=[pTk[s]])
                    P.op(ACT, lambda e, g=g, s=s, blk=blk, par=par: e.activation(out=V[par][:, g, blk, :],
                                                                                 in_=pT[s], func=AF.Copy),
                         reads=[pTk[s]], writes=[("V", par, g, blk)])
        items = []
        for g in range(3):
            d = DIL[g]
            nb = 16 // d
            for r in range(d):
                for m in range(nb):
                    items.append((g, r, m))

        def stage1(it):
            g, r, m = it
            d = DIL[g]
            nb = 16 // d
            L = A1_SPAN // d
            kview = lambda p_: KT[p_][g][:, :].rearrange("p (r a) -> p r a", r=d)
            qview = QT[g][:, :].rearrange("p (r a) -> p r a", r=d)
            s = sc[0] % 2
            sc[0] += 1
            kcur = kview(par)[:, r, 128 * m:128 * (m + 1)]
            if m > 0:
                kprev = kview(par)[:, r, 128 * (m - 1):128 * m]
                kpk = []
            else:
                kprev = kview(1 - par)[:, r, L - 128:L]
                kpk = [("KT", 1 - par, g, i) for i in range(TPS)] if n > 0 else [("KT", 1, g)]
            q = qview[:, r, 128 * m:128 * (m + 1)]
            variant = 1 if (n == 0 and m == 0) else 0

            def mms(e):
                e.matmul(pSb[s][:, 0, :], lhsT=kprev, rhs=q, start=True, stop=True)
                return e.matmul(pSb[s][:, 1, :], lhsT=kcur, rhs=q, start=True, stop=True)
            P.op(PE, mms, reads=kpk + [("KT", par, g, i) for i in range(TPS)] + [("QT", g, i) for i in range(TPS)],
                 writes=[("pS", s)])
            P.op(ACT, lambda e: e.activation(out=PT[s][:, :, :], in_=pSb[s][:, 0:2, :], func=AF.Exp, scale=scale),
                 reads=[("pS", s)], writes=[("PT", s)])
            P.op(DVE, lambda e: e.tensor_tensor(out=PT[s][:, :, :], in0=PT[s][:, :, :], in1=mk[:, variant, :, :],
                                                op=ALU.mult), reads=[("PT", s), "mk"], writes=[("PT", s)])
            return s

        def stage2(it, s):
            g, r, m = it
            d = DIL[g]
            nb = 16 // d
            blk = r * nb + m
            if m > 0:
                vprev = V[par][:, g, blk - 1, :]
                vpk = ("V", par, g, blk - 1)
            else:
                vprev = V[1 - par][:, g, r * nb + nb - 1, :]
                vpk = ("V", 1 - par, g, r * nb + nb - 1)

            def mmo(e):
                e.matmul(pOL[s][:, 0, :], lhsT=vprev, rhs=PT[s][:, 0, :], start=True, stop=False)
                e.matmul(pOL[s][:, 0, :], lhsT=V[par][:, g, blk, :], rhs=PT[s][:, 1, :], start=False, stop=True)
                e.matmul(pOL[s][:, 1, :], lhsT=ones[:], rhs=PT[s][:, 0, :], start=True, stop=False)
                return e.matmul(pOL[s][:, 1, :], lhsT=ones[:], rhs=PT[s][:, 1, :], start=False, stop=True)
            P.op(PE, mmo, reads=[("PT", s), vpk, ("V", par, g, blk), "ones"], writes=[("pOL", s)])
            st = r + d * 128 * m
            cols = slice(st, st + d * 127 + 1, d)
            if g == 0:
                pbs = [m]
            elif g == 1:
                pbs = [4 * m + j for j in range(4)]
            else:
                pbs = list(range(16))
            akeys = [("acc", pb) for pb in pbs]
            lkeys = [("lacc", pb) for pb in pbs]
            if g == 0:
                P.op(ACT, lambda e: e.activation(out=acc[:, cols], in_=pOL[s][:, 0, :], func=AF.Copy),
                     reads=[("pOL", s)], writes=akeys)
                P.op(ACT, lambda e: e.activation(out=lacc[:, cols], in_=pOL[s][:, 1, :], func=AF.Copy),
                     reads=[("pOL", s)], writes=lkeys)
            else:
                P.op(DVE, lambda e: e.tensor_tensor(out=acc[:, cols], in0=pOL[s][:, 0, :], in1=acc[:, cols], op=ALU.add),
                     reads=[("pOL", s)] + akeys, writes=akeys)
                P.op(DVE, lambda e: e.tensor_tensor(out=lacc[:, cols], in0=pOL[s][:, 1, :], in1=lacc[:, cols],
                                                    op=ALU.add), reads=[("pOL", s)] + lkeys, writes=lkeys)

        prev = None
        for i in range(len(items) + 1):
            cur = None
            if i < len(items):
                cur = (items[i], stage1(items[i]))
            if prev is not None:
                stage2(*prev)
            prev = cur
        allk = [("acc", pb) for pb in range(16)]
        alll = [("lacc", pb) for pb in range(16)]
        P.op(ACT, lambda e: e.activation(out=lacc[:, :], in_=lacc[:, :], func=AF.Ln), reads=alll, writes=alll)
        P.op(ACT, lambda e: e.activation(out=lacc[:, :], in_=lacc[:, :], func=AF.Exp, scale=-1.0), reads=alll, writes=alll)
        HS = A1_SPAN // 2
        for hh_ in range(2):
            P.op(DVE, lambda e, hh_=hh_: e.tensor_tensor(out=ob[:, :], in0=acc[:, hh_ * HS:(hh_ + 1) * HS],
                                                         in1=lacc[:, hh_ * HS:(hh_ + 1) * HS], op=ALU.mult),
                 reads=allk + alll, writes=["ob"])
            P.dma(SP, att_d[:, n * A1_SPAN + hh_ * HS:n * A1_SPAN + (hh_ + 1) * HS], ob[:, :], reads=["ob"])

    load(0)
    norm(0)
    load(1)
    norm(1)
    load(2)
    for p in range(NT // 2):
        proj(p, range(0, 3))
        if 2 * p + 2 < NT:
            norm(2 * p + 2)
            if 2 * p + 3 < NT:
                load(2 * p + 3)
        proj(p, range(3, 6))
        if 2 * p + 3 < NT:
            norm(2 * p + 3)
            if 2 * p + 4 < NT:
                load(2 * p + 4)
        proj(p, range(6, 9))
        if (2 * p + 1) % TPS == TPS - 1:
            attention((2 * p + 1) // TPS)
    return P.build()


def a1_host_inputs(x_tok, positions, w_in, g_attn):
    xT = np.ascontiguousarray(x_tok.reshape(SEQ // A1_TT, A1_TT, KC, 128).transpose(0, 3, 2, 1))
    posb = np.ascontiguousarray(np.broadcast_to(positions.astype(np.int32)[None, :], (128, SEQ)))
    rc = rope_consts_host(128, 64)
    k = np.arange(128)[:, None]
    q = np.arange(128)[None, :]
    mk = np.zeros((128, 2, 2, 128), np.float32)
    mk[:, 0, 0, :] = (k >= q)
    mk[:, 0, 1, :] = (k <= q)
    mk[:, 1, 1, :] = (k <= q)
    ident = np.eye(128, dtype=np.float32)
    ga = gvec(g_attn)
    w4 = w_in[:, :9216].reshape(DM, 3, 3, 8, 128)
    maps = []
    for c in range(NCORES):
        w = pkn(np.ascontiguousarray(w4[:, :, :, c, :]).reshape(DM, 1152))
        maps.append({"xT": xT, "w": w, "ga": ga, "posb": posb, "rc": rc, "mk": mk, "ident": ident})
    return maps


NHB = 12


def build_proj(kind):
    P = Prog()
    xT_d = P.din("xT", [DM, TOK])
    gx_d = P.din("gx", [128, KC])
    gl_d = P.din("gl", [128, 4])
    posb_d = P.din("posb", [128, TOK], I32)
    rc_d = P.din("rc", [128, 4])
    if kind == "kv":
        w1_d = P.din("w1", [128, KC, 576])
        wuk_d = P.din("wuk", [128, 4, 1536])
        wuv_d = P.din("wuv", [128, 4, 1536])
        kn_d = P.dout("knT", [NHB, 128, TOK], BF16)
        kr_d = P.dout("krT", [64, TOK], BF16)
        v_d = P.dout("v", [TOK // 128, 128, 1536], BF16)
    else:
        w1_d = P.din("w1", [128, KC, 1024])
        wuq_d = P.din("wuq", [128, 4, NHB * 192])
        qn_d = P.dout("qnT", [NHB, 128, TOK], BF16)
        qr_d = P.dout("qrT", [NHB, 64, TOK], BF16)
        qm_d = P.dout("qmT", [MH, 128, TOK], BF16)

    ones = emit_consts(P)
    xT = P.sb("xT_sb", [128, KC, TOK], F32)
    hT = P.sb("hT_sb", [128, KC, TOK], BF16)
    gx = P.sb("gx_sb", [128, KC], F32)
    gl = P.sb("gl_sb", [128, 4], F32)
    rc = P.sb("rc_sb", [128, 4], F32)
    sq = P.sb("sq_sb", [128, KC, 512], BF16)
    rs1 = P.sb("rs1_sb", [128, 512], F32)
    rs2 = P.sb("rs2_sb", [128, 512], F32)
    wbuf = P.sb("wbuf_sb", [128, KC, 576], BF16)
    cl = P.sb("cl_sb", [128, 4, TOK], F32)
    cn = P.sb("cn_sb", [128, 4, TOK], BF16)
    wu = P.sb("wu_sb", [128, 4, NHB * 192], BF16)
    posi = P.sb("posi_sb", [128, 512], I32)
    posf = P.sb("posf_sb", [128, 512], F32)
    kint = P.sb("kint_sb", [128, 512], I32)
    ang = P.sb("ang_sb", [128, 512], F32)
    r1 = P.sb("r1_sb", [128, 512], F32)
    r2 = P.sb("r2_sb", [128, 512], F32)
    cos2 = [P.sb("cos2_sb%d" % i, [128, 512], F32) for i in range(2)]
    sinS = [P.sb("sinS_sb%d" % i, [128, 512], F32) for i in range(2)]
    qf = P.sb("qf_sb", [128, 512], F32)
    qsw = P.sb("qsw_sb", [128, 512], F32)
    ta = P.sb("ta_sb", [128, 512], F32)
    tb = P.sb("tb_sb", [128, 512], F32)
    stg = [P.sb("stg_sb%d" % i, [128, 512], BF16) for i in range(3)]
    pj = [P.ps("pj%d" % i, [128, 512]) for i in range(3)]
    ph = P.ps("ph", [128, 512])
    pjk = [("pj", i) for i in range(3)]
    cnt = [0]
    scnt = [0]

    xT_v = xT_d.rearrange("(k p) t -> p k t", p=128)
    for kc in range(KC):
        P.dma(SP, xT[:, kc, :], xT_v[:, kc, :], writes=[("x", kc, 0), ("x", kc, 1)])
    P.dma(SP, gx[:, :], gx_d[:, :], writes=["gx"])
    P.dma(SP, gl[:, :], gl_d[:, :], writes=["gl"])
    P.dma(SP, rc[:, :], rc_d[:, :], writes=["ropec"])
    NW1 = 576 if kind == "kv" else 512
    P.dma(POOL, wbuf[:, :, 0:NW1], w1_d[:, :, 0:NW1], writes=["wbuf"])
    if kind == "kv":
        P.dma(POOL, wu[:, :, 0:1536], wuk_d[:, :, :], writes=["wu"])
    else:
        P.dma(POOL, wu[:, :, :], wuq_d[:, :, :], writes=["wu"])
    for tt in range(2):
        emit_rmsnorm(P, "gx", lambda kc, tt=tt: xT[:, kc, tt * 512:(tt + 1) * 512], [("x", kc, tt) for kc in range(KC)],
                     KC, 512, DM, gx, lambda kc, tt=tt: hT[:, kc, tt * 512:(tt + 1) * 512],
                     [("h", kc, tt) for kc in range(KC)], ones, ph, "ph", sq, rs1, rs2)
    for tt in range(2):
        P.dma(SP, posi[:, :], posb_d[:, tt * 512:(tt + 1) * 512], writes=["posi"])
        emit_rope_tables(P, 64, 512, posi[0:64, :], "posi", rc[:, 0:1], rc[:, 1:2], rc[:, 2:3], posf, ang, r1, r2,
                         cos2[tt], sinS[tt], ("rt", tt), kint)

    def hk(tt):
        return [("h", kc, tt) for kc in range(KC)]

    def mm_in(col, M, tt, s):
        def mm(e):
            ins = None
            for kc in range(KC):
                ins = e.matmul(pj[s][0:M, :], lhsT=wbuf[:, kc, col:col + M], rhs=hT[:, kc, tt * 512:(tt + 1) * 512],
                               start=(kc == 0), stop=(kc == KC - 1))
            return ins
        return mm

    def out_bf16(src_ap, srckey, dst_ap, NP=128):
        i = scnt[0] % 3
        scnt[0] += 1
        P.op(ACT, lambda e: e.activation(out=stg[i][0:NP, :], in_=src_ap, func=AF.Copy), reads=[srckey],
             writes=[("stg", i)])
        P.dma(SP, dst_ap, stg[i][0:NP, :], reads=[("stg", i)])

    def rope_out(s, tt, dst_ap):
        i = scnt[0] % 3
        scnt[0] += 1
        emit_rope_apply(P, 64, 512, lambda lo, hi: pj[s][lo:hi, :], pjk[s], cos2[tt], sinS[tt], ("rt", tt), qf, qsw, ta, tb,
                        "rtmp", stg[i][0:64, :], [("stg", i)])
        P.dma(SP, dst_ap, stg[i][0:64, :], reads=[("stg", i)])

    for j in range(4):
        for tt in range(2):
            s = cnt[0] % 3
            cnt[0] += 1
            P.op(PE, mm_in(j * 128, 128, tt, s), reads=["wbuf"] + hk(tt), writes=[pjk[s]])
            P.op(ACT, lambda e, j=j, tt=tt, s=s: e.activation(out=cl[:, j, tt * 512:(tt + 1) * 512], in_=pj[s][:, :],
                                                              func=AF.Copy), reads=[pjk[s]], writes=[("cl", j, tt)])
    if kind == "kv":
        for tt in range(2):
            s = cnt[0] % 3
            cnt[0] += 1
            P.op(PE, mm_in(512, 64, tt, s), reads=["wbuf"] + hk(tt), writes=[pjk[s]])
            rope_out(s, tt, kr_d[:, tt * 512:(tt + 1) * 512])
    else:
        P.dma(POOL, wbuf[:, :, 0:512], w1_d[:, :, 512:1024], writes=["wbuf"])
        for h in range(MH):
            for tt in range(2):
                s = cnt[0] % 3
                cnt[0] += 1
                P.op(PE, mm_in(h * 128, 128, tt, s), reads=["wbuf"] + hk(tt), writes=[pjk[s]])
                out_bf16(pj[s][:, :], pjk[s], qm_d[h][:, tt * 512:(tt + 1) * 512])
    for tt in range(2):
        emit_rmsnorm(P, "gl", lambda kc, tt=tt: cl[:, kc, tt * 512:(tt + 1) * 512], [("cl", kc, tt) for kc in range(4)],
                     4, 512, 512, gl, lambda kc, tt=tt: cn[:, kc, tt * 512:(tt + 1) * 512],
                     [("cn", kc, tt) for kc in range(4)], ones, ph, "ph", sq, rs1, rs2)

    def mm_up(col, M, tt, s):
        def mm(e):
            ins = None
            for kc in range(4):
                ins = e.matmul(pj[s][0:M, :], lhsT=wu[:, kc, col:col + M], rhs=cn[:, kc, tt * 512:(tt + 1) * 512],
                               start=(kc == 0), stop=(kc == 3))
            return ins
        return mm

    cnk = lambda tt: [("cn", kc, tt) for kc in range(4)]
    if kind == "kv":
        for h in range(NHB):
            for tt in range(2):
                s = cnt[0] % 3
                cnt[0] += 1
                P.op(PE, mm_up(h * 128, 128, tt, s), reads=["wu"] + cnk(tt), writes=[pjk[s]])
                out_bf16(pj[s][:, :], pjk[s], kn_d[h][:, tt * 512:(tt + 1) * 512])
        P.dma(POOL, wu[:, :, 0:1536], wuv_d[:, :, :], writes=["wu"])
        for blk in range(TOK // 128):
            tt = blk // 4
            for nt in range(3):
                s = cnt[0] % 3
                cnt[0] += 1

                def mm(e, blk=blk, nt=nt, s=s):
                    ins = None
                    for kc in range(4):
                        ins = e.matmul(pj[s][:, :], lhsT=cn[:, kc, blk * 128:(blk + 1) * 128],
                                       rhs=wu[:, kc, nt * 512:(nt + 1) * 512], start=(kc == 0), stop=(kc == 3))
                    return ins
                P.op(PE, mm, reads=["wu"] + cnk(tt), writes=[pjk[s]])
                out_bf16(pj[s][:, :], pjk[s], v_d[blk][:, nt * 512:(nt + 1) * 512])
    else:
        for h in range(NHB):
            for tt in range(2):
                s = cnt[0] % 3
                cnt[0] += 1
                P.op(PE, mm_up(h * 192, 128, tt, s), reads=["wu"] + cnk(tt), writes=[pjk[s]])
                out_bf16(pj[s][:, :], pjk[s], qn_d[h][:, tt * 512:(tt + 1) * 512])
                s = cnt[0] % 3
                cnt[0] += 1
                P.op(PE, mm_up(h * 192 + 128, 64, tt, s), reads=["wu"] + cnk(tt), writes=[pjk[s]])
                rope_out(s, tt, qr_d[h][:, tt * 512:(tt + 1) * 512])
    return P.build()


def posb_of(pos_tok):
    return np.ascontiguousarray(np.broadcast_to(pos_tok.astype(np.int32)[None, :], (128, pos_tok.shape[0])))


def zz_block(c, m):
    return 8 * m + (c if m % 2 == 0 else 7 - c)


def build_b1b():
    P = Prog()
    NHC = 16
    NKB = SEQ // 128
    qn_d = P.din("qnT", [NHB, 128, TOK], BF16)
    qr_d = P.din("qrT", [NHB, 64, TOK], BF16)
    qm_d = P.din("qmT", [MH, 128, TOK], BF16)
    kn_d = P.din("knT", [NHB, 128, SEQ], BF16)
    kr_d = P.din("krT", [64, SEQ], BF16)
    vv_d = P.din("vv", [NHB, 128, NKB, 128], BF16)
    mask_d = P.din("mask", [128, 8, 8, 128], BF16)
    memT_d = P.din("memT", [128, KC, MEMT])
    wmk_d = P.din("wmk", [128, KC, 1024])
    wo_d = P.din("wo", [128, NHC, DM])
    xT_d = P.din("xT", [DM, TOK])
    xo_d = P.dout("xo", [DM, TOK])

    ones = emit_consts(P)
    ones32 = P.sb("ones32_sb", [128, 128], F32)
    P.op(DVE, lambda e: e.memset(ones32[:], 1.0), writes=["ones32"])
    krT = P.sb("krT_sb", [64, SEQ], BF16)
    Kb = [P.sb("Kb_sb%d" % i, [128, 32 * 128], BF16) for i in range(2)]
    Vb = [P.sb("Vb_sb%d" % i, [128, 32, 128], BF16) for i in range(2)]
    qn = [P.sb("qn_sb%d" % i, [128, TOK], BF16) for i in range(2)]
    qr = [P.sb("qr_sb%d" % i, [64, TOK], BF16) for i in range(2)]
    qmT = P.sb("qmT_sb", [128, MH, TOK], BF16)
    mask = P.sb("mask_sb", [128, 8, 8, 128], BF16)
    headsT = P.sb("headsT_sb", [128, NHC, TOK], BF16)
    PT = [P.sb("PT_sb%d" % i, [128, 512], BF16) for i in range(3)]
    PTm = P.sb("PTm_sb", [128, 2, 512], BF16)
    lacc = [[P.sb("lacc_sb%d_%d" % (i, j), [128, 512], F32) for j in range(2)] for i in range(2)]
    rinv = P.sb("rinv_sb", [128, 512], F32)
    wbuf = P.sb("wbuf_sb", [128, KC, 512], BF16)
    wo_bufs = [P.sb("wo_sb%d" % i, [128, NHC, 512], BF16) for i in range(2)]
    xs = [P.sb("xs_sb%d" % i, [128, 512], F32) for i in range(2)]
    pS = [P.ps("pS%d" % i, [128, 512]) for i in range(3)]
    pO = [P.ps("pO%d" % i, [128, 512]) for i in range(2)]
    pL = P.ps("pL", [128, 512])
    pj = [P.ps("pj%d" % i, [128, 512]) for i in range(2)]
    pjkeys = [("pj", 0), ("pj", 1)]

    P.dma(SP, krT[:, :], kr_d[:, :], writes=["krT"])
    P.dma(SP, mask[:, :, :, :], mask_d[:, :, :, :], writes=["mask"])
    for h in range(MH):
        P.dma(SP, qmT[:, h, :], qm_d[h], writes=[("qm", h, 0), ("qm", h, 1)])

    scale = 192.0 ** -0.5
    sc = [0]

    def load_kv(h, half):
        seq = 2 * h + half
        b = seq % 2
        P.dma(SP, Kb[b][:, :], kn_d[h][:, half * 4096:(half + 1) * 4096], writes=[("K", b)])
        P.dma(SP, Vb[b][:, :, :], vv_d[h][:, half * 32:(half + 1) * 32, :], writes=[("V", b)])

    def load_q(h):
        b = h % 2
        P.dma(SP, qn[b][:, :], qn_d[h], writes=[("qn", b)])
        P.dma(SP, qr[b][:, :], qr_d[h], writes=[("qr", b)])

    NSL = 3
    items = []

    def seg(h, X, kbs, first, last, pre=None):
        for kb in kbs:
            items.append(dict(h=h, X=X, kb=kb, isf=(first and kb == kbs[0]), isl=(last and kb == kbs[-1]),
                              pre=(pre if kb == kbs[0] else None)))

    def stage1(it, idx):
        h, X, kb = it["h"], it["X"], it["kb"]
        qb = h % 2
        half = kb // 32
        b = (2 * h + half) % 2
        m0 = kb // 8
        masked = (m0 >= 4 * X) and (m0 < 4 * X + 4)
        c0 = 128 * (m0 - 4 * X) if masked else 0
        N = 512 - c0
        s = idx % NSL
        q0 = X * 512 + c0
        kl = kb % 32
        it.update(b=b, c0=c0, N=N, s=s, kl=kl)

        def mms(e):
            e.matmul(pS[s][:, 0:N], lhsT=Kb[b][:, kl * 128:(kl + 1) * 128], rhs=qn[qb][:, q0:q0 + N],
                     start=True, stop=False)
            return e.matmul(pS[s][:, 0:N], lhsT=krT[0:64, kb * 128:(kb + 1) * 128], rhs=qr[qb][0:64, q0:q0 + N],
                            start=False, stop=True)
        P.op(PE, mms, reads=[("K", b), "krT", ("qn", qb), ("qr", qb)], writes=[("pS", s)])
        P.op(ACT, lambda e: e.activation(out=PT[s][:, 0:N], in_=pS[s][:, 0:N], func=AF.Exp, scale=scale),
             reads=[("pS", s)], writes=[("PT", s)])
        if masked:
            P.op(POOL, lambda e: e.tensor_tensor(out=PT[s][:, 0:128], in0=PT[s][:, 0:128],
                                                 in1=mask[:, m0, kb - 8 * m0, :], op=ALU.mult),
                 reads=[("PT", s), "mask"], writes=[("PT", s)])

    def stage2(it):
        h, X, kb = it["h"], it["X"], it["kb"]
        b, c0, N, s, kl = it["b"], it["c0"], it["N"], it["s"], it["kl"]
        o = (2 * h + X) % 2
        par = kb % 2
        isf, isl = it["isf"], it["isl"]
        P.op(PE, lambda e: e.matmul(pO[o][:, c0:512], lhsT=Vb[b][:, kl, :], rhs=PT[s][:, 0:N], start=isf, stop=isl),
             reads=[("V", b), ("PT", s)], writes=[("pO", o)])
        if kb < 2:
            P.op(DVE, lambda e: e.tensor_copy(out=lacc[o][par][:, :], in_=PT[s][:, :]),
                 reads=[("PT", s)], writes=[("lacc", o, par)])
        else:
            P.op(DVE, lambda e: e.tensor_tensor(out=lacc[o][par][:, c0:512], in0=lacc[o][par][:, c0:512],
                                                in1=PT[s][:, 0:N], op=ALU.add),
                 reads=[("PT", s), ("lacc", o, par)], writes=[("lacc", o, par)])
        if isl:
            finalize(h, X)

    def finalize(h, X):
        o = (2 * h + X) % 2

        def mm(e):
            e.matmul(pL[:, :], lhsT=ones32[:], rhs=lacc[o][0][:, :], start=True, stop=False)
            return e.matmul(pL[:, :], lhsT=ones32[:], rhs=lacc[o][1][:, :], start=False, stop=True)
        P.op(PE, mm, reads=["ones32", ("lacc", o, 0), ("lacc", o, 1)], writes=["pL"])
        P.op(ACT, lambda e: e.activation(out=rinv[:, :], in_=pL[:, :], func=AF.Ln), reads=["pL"], writes=["rinv"])
        P.op(ACT, lambda e: e.activation(out=rinv[:, :], in_=rinv[:, :], func=AF.Exp, scale=-1.0),
             reads=["rinv"], writes=["rinv"])
        P.op(DVE, lambda e: e.tensor_tensor(out=headsT[:, h, X * 512:(X + 1) * 512], in0=pO[o][:, :],
                                            in1=rinv[:, :], op=ALU.mult),
             reads=[("pO", o), "rinv"], writes=[("heads", h, X)])

    load_q(0)
    load_kv(0, 0)
    for h in range(NHB):
        seg(h, 0, list(range(0, 32)), True, True, pre=(lambda h=h: load_kv(h, 1)))
        seg(h, 1, list(range(0, 32)), True, False)
        if h + 1 < NHB:
            seg(h, 1, list(range(32, 64)), False, True, pre=(lambda h=h: (load_q(h + 1), load_kv(h + 1, 0))))
        else:
            seg(h, 1, list(range(32, 64)), False, True)
    for i in range(len(items) + 1):
        if i < len(items) and items[i]["pre"] is not None:
            if i >= 1:
                stage2(items[i - 1])
            items[i]["pre"]()
            stage1(items[i], i)
            continue
        if i < len(items):
            stage1(items[i], i)
        if i >= 1:
            stage2(items[i - 1])

    mkT, mv = emit_mem_kv(P, memT_d, wmk_d, ones, wbuf, lambda: "wbuf", pj, pjkeys)
    emit_mem_attn(P, qmT, lambda h, tt: ("qm", h, tt), mkT, mv, ones, headsT, NHB, pS, pO[0], pL, PTm, rinv,
                  pOkey=("pO", 0))

    xT_v = xT_d.rearrange("(k p) t -> p k t", p=128)
    xo_v = xo_d.rearrange("(k p) t -> p k t", p=128)
    cnt = 0
    for fq in range(DM // 512):
        b = fq % 2
        P.dma(POOL, wo_bufs[b][:, :, :], wo_d[:, :, fq * 512:(fq + 1) * 512], writes=[("wo", b)])
        for f4 in range(4):
            fo = fq * 4 + f4
            for tt in range(2):
                i = cnt % 2
                cnt += 1
                sl = slice(tt * 512, (tt + 1) * 512)
                P.dma(SP, xs[i][:, :], xT_v[:, fo, sl], writes=[("xs", i)])

                def mm(e, f4=f4, sl=sl, i=i, b=b):
                    ins = None
                    for kc in range(NHC):
                        ins = e.matmul(pj[i][:, :], lhsT=wo_bufs[b][:, kc, f4 * 128:(f4 + 1) * 128], rhs=headsT[:, kc, sl],
                                       start=(kc == 0), stop=(kc == NHC - 1))
                    return ins
                P.op(PE, mm, reads=[("wo", b)] + [("heads", kc, tt) for kc in range(NHC)], writes=[pjkeys[i]])
                P.op(DVE, lambda e, i=i: e.tensor_tensor(out=xs[i][:, :], in0=pj[i][:, :], in1=xs[i][:, :], op=ALU.add),
                     reads=[pjkeys[i], ("xs", i)], writes=[("xs", i)])
                P.dma(SP, xo_v[:, fo, sl], xs[i][:, :], reads=[("xs", i)])
    return P.build()


import ml_dtypes

BF16_NP = ml_dtypes.bfloat16


def _run(nc, maps):
    res = run_bass_kernel_spmd(nc, maps, core_ids=list(range(NCORES)))
    return res.results


def zz_index(c):
    return np.concatenate([np.arange(zz_block(c, m) * 128, zz_block(c, m) * 128 + 128) for m in range(8)])


def zz_mask(c):
    k = np.arange(128)[:, None]
    q = np.arange(128)[None, :]
    tri = (k <= q).astype(np.float32)
    mk = np.zeros((128, 8, 8, 128), np.float32)
    for m in range(8):
        bm = zz_block(c, m)
        for j in range(8):
            kb = 8 * m + j
            if kb < bm:
                mk[:, m, j, :] = 1.0
            elif kb == bm:
                mk[:, m, j, :] = tri
    return mk.astype(BF16_NP)


def run_a_layer(x, pos, mem, w_in, w_mem_kv, w_out, g_attn):
    res = _run(build_a1(), a1_host_inputs(x, pos, w_in, g_attn))
    att_tok = np.concatenate([r["attT"].T for r in res], axis=1)
    res = _run(build_a2(), a2_host_inputs(x, att_tok, w_in, w_mem_kv, mem, w_out, g_attn))
    return np.concatenate([r["xo"].T for r in res], axis=0)


def run_ffn(x, w_gate, w_val, conv_w, conv_b, w_down, g_ffn, g_final=None):
    res = _run(build_ffn(final_norm=g_final is not None),
               ffn_host_inputs(x, w_gate, w_val, conv_w, conv_b, w_down, g_ffn, g_final))
    xo = np.concatenate([r["xo"].T for r in res], axis=0)
    fin = None
    if g_final is not None:
        fin = np.concatenate([r["fin"].T for r in res], axis=0)
    return xo, fin


def run_kv(x, pos, g_norm, w_dkv, g_latent, w_uk, w_uv):
    rc = rope_consts_host(64, 32)
    maps = []
    for c in range(NCORES):
        sl = slice(c * TOK, (c + 1) * TOK)
        maps.append({"xT": np.ascontiguousarray(x[sl].T), "gx": gvec(g_norm), "gl": gvec(g_latent),
                     "posb": posb_of(pos[sl]), "rc": rc, "w1": pkn(w_dkv), "wuk": pkn(w_uk), "wuv": pkn(w_uv)})
    res = _run(build_proj("kv"), maps)
    knT = np.ascontiguousarray(np.concatenate([r["knT"] for r in res], axis=2))
    krT = np.ascontiguousarray(np.concatenate([r["krT"] for r in res], axis=1))
    v = np.concatenate([r["v"].reshape(TOK, NHB, 128) for r in res], axis=0)
    vv = np.ascontiguousarray(v.reshape(SEQ // 128, 128, NHB, 128).transpose(2, 1, 0, 3))
    return knT, krT, vv


def run_b_layer(x, pos, mem, kvs, w_in, g_qnorm, w_uq, w_mem_kv, w_out, g_attn):
    knT, krT, vv = kvs
    rc = rope_consts_host(64, 32)
    idx = [zz_index(c) for c in range(NCORES)]
    xTs = [np.ascontiguousarray(x[idx[c]].T) for c in range(NCORES)]
    maps = []
    for c in range(NCORES):
        maps.append({"xT": xTs[c], "gx": gvec(g_attn), "gl": gvec(g_qnorm), "posb": posb_of(pos[idx[c]]), "rc": rc,
                     "w1": pkn(w_in), "wuq": pkn(w_uq)})
    rq = _run(build_proj("q"), maps)
    memT = pkn(np.ascontiguousarray(mem.T))
    wmk = pkn(w_mem_kv)
    wo = pkn(w_out)
    maps = []
    for c in range(NCORES):
        maps.append({"qnT": rq[c]["qnT"], "qrT": rq[c]["qrT"], "qmT": rq[c]["qmT"], "knT": knT, "krT": krT, "vv": vv,
                     "mask": zz_mask(c), "memT": memT, "wmk": wmk, "wo": wo, "xT": xTs[c]})
    res = _run(build_b1b(), maps)
    xo = np.empty_like(x)
    for c in range(NCORES):
        xo[idx[c]] = res[c]["xo"].T
    return xo


def kernel(x, mem, positions, a_w_in, a_w_mem_kv, a_w_out, b_w_in, b_g_qnorm, b_w_uq, b_w_mem_kv, b_w_out,
           kv_g_norm, kv_w_dkv, kv_g_latent, kv_w_uk, kv_w_uv, g_attn, g_ffn, ffn_w_gate, ffn_w_val, ffn_conv_w,
           ffn_conv_b, ffn_w_down, g_final):
    f = lambda a: np.asarray(a, dtype=np.float32)
    xs = f(x)[0]
    memh = f(mem)[0]
    pos = np.asarray(positions)[0].astype(np.int32)
    fin = None
    kvs = None
    for layer in range(4):
        if layer < 2:
            xs = run_a_layer(xs, pos, memh, f(a_w_in[layer]), f(a_w_mem_kv[layer]), f(a_w_out[layer]), f(g_attn[layer]))
        else:
            if kvs is None:
                kvs = run_kv(xs, pos, f(kv_g_norm), f(kv_w_dkv), f(kv_g_latent), f(kv_w_uk), f(kv_w_uv))
            i = layer - 2
            xs = run_b_layer(xs, pos, memh, kvs, f(b_w_in[i]), f(b_g_qnorm[i]), f(b_w_uq[i]), f(b_w_mem_kv[i]),
                             f(b_w_out[i]), f(g_attn[layer]))
        xs, fin = run_ffn(xs, f(ffn_w_gate[layer]), f(ffn_w_val[layer]), f(ffn_conv_w[layer]), f(ffn_conv_b[layer]),
                          f(ffn_w_down[layer]), f(g_ffn[layer]), f(g_final) if layer == 3 else None)
    return np.ascontiguousarray(fin[None]).astype(np.float32)
```
